# Optimizing a Trainium2 kernel written in Bass

```python
import math, functools
import jax, jax.numpy as jnp
from jax import lax
import numpy as np

D_MODEL = 1024
BATCH = 8
SEQ = 4096
DEPTH = 2

GRID_W = 64
CTX_LEN = 256

ATT_HEADS = 8
ATT_KV_HEADS = 2
GQA_GROUP = ATT_HEADS // ATT_KV_HEADS
HEAD_DIM = 64
ATT_Q_DIM = ATT_HEADS * HEAD_DIM
ATT_KV_DIM = ATT_KV_HEADS * HEAD_DIM
WINDOW = 128
ATT_BLOCK = 128
ATT_SCALE = HEAD_DIM ** -0.5
ROPE_BASE = 10000.0
ROPE_FREQS = HEAD_DIM // 4
RWKV_HEADS = 8
RWKV_HEAD = 64
RWKV_DIM = RWKV_HEADS * RWKV_HEAD
DECAY_LORA = 64
ICLR_LORA = 64
VRES_LORA = 32
GATE_LORA = 128
SHIFT_WIDTH = 3
GN_EPS = 64e-5
RWKV_SPLITS = (RWKV_DIM, RWKV_DIM, RWKV_DIM, DECAY_LORA, ICLR_LORA, GATE_LORA)
RWKV_IN = sum(RWKV_SPLITS)
RWKV_CUTS = tuple(np.cumsum(RWKV_SPLITS)[:-1].tolist())
CHUNK = 128
SGU_GROUPS = 4
SGU_DIM = 512
SGU_GROUP_DIM = SGU_DIM // SGU_GROUPS
N_BRANCH = 3
FFN_HIDDEN = -(-(8 * D_MODEL) // (3 * 256)) * 256
EPS = 1e-6
IN_SPLITS = (ATT_Q_DIM, ATT_KV_DIM, ATT_KV_DIM, RWKV_IN, 2 * SGU_DIM, N_BRANCH * D_MODEL)
IN_DIM = sum(IN_SPLITS)
IN_CUTS = tuple(np.cumsum(IN_SPLITS)[:-1].tolist())

kernel_name = 'hybrid_prefix_dit_attn_rwkv7_sgu'


def rmsnorm(t, gain):
    t32 = t.astype(jnp.float32)
    return (t32 * lax.rsqrt(jnp.mean(t32 * t32, -1, keepdims=True) + EPS) * gain).astype(t.dtype)


def layernorm(t, w, b):
    t32 = t.astype(jnp.float32)
    mu = jnp.mean(t32, -1, keepdims=True)
    var = jnp.mean(jnp.square(t32 - mu), -1, keepdims=True)
    return ((t32 - mu) * lax.rsqrt(var + EPS) * w + b).astype(t.dtype)


def modulate(h, shift, scale):
    return h * (1.0 + scale) + shift


def axial_rope(n):
    rows = n // GRID_W
    row = jnp.repeat(jnp.arange(rows), GRID_W).astype(jnp.float32)
    col = jnp.tile(jnp.arange(GRID_W), rows).astype(jnp.float32)
    inv = ROPE_BASE ** (-jnp.arange(ROPE_FREQS, dtype=jnp.float32) / ROPE_FREQS)
    ang = jnp.concatenate([row[:, None] * inv, col[:, None] * inv], -1)
    return jnp.cos(ang), jnp.sin(ang)


def apply_rope(t, cos, sin):
    t32 = t.astype(jnp.float32)
    t1, t2 = jnp.split(t32, 2, -1)
    cos, sin = cos[None, :, None, :], sin[None, :, None, :]
    return jnp.concatenate([t1 * cos - t2 * sin, t1 * sin + t2 * cos], -1).astype(t.dtype)


def head_rmsnorm(p, n_heads, gain):
    return rmsnorm(p.reshape(p.shape[:-1] + (n_heads, HEAD_DIM)), gain)


def sink_attend(qg, keys, vals, masks, sink):
    B, Q = qg.shape[:2]
    logits = []
    for kt, m in zip(keys, masks):
        s = jnp.einsum('bqhgd,bkhd->bhgqk', qg, kt).astype(jnp.float32) * ATT_SCALE
        if m is not None:
            s = jnp.where(m, s, -jnp.inf)
        logits.append(s)
    sink_col = sink.astype(jnp.float32).reshape(ATT_KV_HEADS, GQA_GROUP, 1, 1)
    logits.append(jnp.broadcast_to(sink_col, (B, ATT_KV_HEADS, GQA_GROUP, Q, 1)))
    p = jax.nn.softmax(jnp.concatenate(logits, -1), axis=-1)
    out, start = None, 0
    for vt in vals:
        n = vt.shape[1]
        o = jnp.einsum('bhgqk,bkhd->bqhgd', p[..., start:start + n].astype(vt.dtype), vt)
        out = o if out is None else out + o
        start += n
    return out


def _band(t, nb):
    B, _, H, hd = t.shape
    tb = jnp.pad(t.reshape(B, nb, ATT_BLOCK, H, hd), ((0, 0), (1, 1), (0, 0), (0, 0), (0, 0)))
    band = jnp.concatenate([tb[:, :-2], tb[:, 1:-1], tb[:, 2:]], axis=2)
    return jnp.moveaxis(band, 1, 0)


def window_attention(q, k, v, kc, vc, sink):
    B, N = q.shape[:2]
    nb = N // ATT_BLOCK
    qb = jnp.moveaxis(q.reshape(B, nb, ATT_BLOCK, ATT_KV_HEADS, GQA_GROUP, HEAD_DIM), 1, 0)
    kb, vb = _band(k, nb), _band(v, nb)
    offs = jnp.arange(3 * ATT_BLOCK) - ATT_BLOCK
    rel = offs[None, :] - jnp.arange(ATT_BLOCK)[:, None]
    key_pos = jnp.arange(nb)[:, None] * ATT_BLOCK + offs[None, :]
    mask = (jnp.abs(rel) <= WINDOW)[None] & ((key_pos >= 0) & (key_pos < N))[:, None, :]

    def block(args):
        qi, ki, vi, mi = args
        return sink_attend(qi, (ki, kc), (vi, vc), (mi, None), sink)

    out = lax.map(block, (qb, kb, vb, mask))
    return jnp.moveaxis(out, 0, 1).reshape(B, N, ATT_Q_DIM)


def context_attention(qc, kc, vc, sink):
    B, L = qc.shape[:2]
    qg = qc.reshape(B, L, ATT_KV_HEADS, GQA_GROUP, HEAD_DIM)
    return sink_attend(qg, (kc,), (vc,), (None,), sink).reshape(B, L, ATT_Q_DIM)


def short_conv(t, w):
    T = t.shape[1]
    half = SHIFT_WIDTH // 2
    tp = jnp.pad(t, ((0, 0), (half, half), (0, 0)))
    return sum(tp[:, i:i + T] * w[i] for i in range(SHIFT_WIDTH))


def _time_major(t):
    t = jnp.stack([t[0], jnp.flip(t[1], axis=1)])
    return jnp.moveaxis(t, 2, 0)


def _rwkv_step(S, inp, emit):
    r, w, k, v, a, b = inp
    sa = jnp.einsum('dbhvk,dbhk->dbhv', S, a)
    S = S * w[..., None, :] + sa[..., :, None] * b[..., None, :] + v[..., :, None] * k[..., None, :]
    return S, (jnp.einsum('dbhvk,dbhk->dbhv', S, r) if emit else None)


def rwkv_time_mix(r, k, v, xw, xa, xg, state0, lp, readout):
    f32 = jnp.float32
    B, T, C = r.shape
    heads = lambda t: t.reshape(t.shape[:-1] + (RWKV_HEADS, RWKV_HEAD))
    kk = heads((k * lp['k_k']).astype(f32))
    kk = kk * lax.rsqrt(jnp.sum(kk * kk, -1, keepdims=True) + 1e-12)
    w_pre = lp['w0'][:, None, None, :] + jnp.einsum('btr,drc->dbtc', jnp.tanh(xw), lp['w_up'])
    decay = jnp.exp(-jnp.exp(-jax.nn.softplus(-w_pre.astype(f32)) - 0.5))
    a = jax.nn.sigmoid((lp['a0'][:, None, None, :] + jnp.einsum('btr,drc->dbtc', xa, lp['a_up'])).astype(f32))
    k_dir = k.astype(f32)[None] * (1.0 + (a - 1.0) * lp['k_a'])
    rh, vh, kdh = heads(r.astype(f32)), heads(v.astype(f32)), heads(k_dir)
    both = lambda t: jnp.stack([t, t])
    seqs = (both(rh), heads(decay), kdh, both(vh), both(-kk), kk[None] * heads(a))
    state, ys = lax.scan(functools.partial(_rwkv_step, emit=readout), state0,
                         tuple(_time_major(t) for t in seqs))
    if not readout:
        return None, state
    y = jnp.moveaxis(ys[:, 0] + jnp.flip(ys[:, 1], axis=0), 0, 1)
    mu = jnp.mean(y, -1, keepdims=True)
    var = jnp.mean(jnp.square(y - mu), -1, keepdims=True)
    y = ((y - mu) * lax.rsqrt(var + GN_EPS)).reshape(B, T, C) * lp['ln_w'] + lp['ln_b']
    bonus = jnp.einsum('bthn,dbthn,hn->bth', rh, kdh, lp['r_k'])[..., None] * vh
    g = jax.nn.sigmoid(xg) @ lp['g_up']
    return ((y + bonus.reshape(B, T, C)) * g).astype(r.dtype), state


def value_residual(v, v_first, h, down, up, bias):
    return v + (v_first - v) * jax.nn.sigmoid(bias + (h @ down) @ up)


def spatial_gating(p_sg, lp):
    u, vg = jnp.split(jax.nn.gelu(p_sg), 2, -1)
    B, T, _ = u.shape
    vn = layernorm(vg, lp['sgu_ln_w'], lp['sgu_ln_b'])
    vc = vn.reshape(B, T // CHUNK, CHUNK, SGU_GROUPS, SGU_GROUP_DIM)
    s = jnp.einsum('gpq,bnqgc->bnpgc', lp['sgu_w'], vc) + lp['sgu_b'].T[None, None, :, :, None]
    return u * s.reshape(B, T, SGU_DIM)


def merge_branches(o_attn, o_rwkv, o_sgu, p_gate, lp):
    ga, gb, gc = jnp.split(jax.nn.sigmoid(p_gate), N_BRANCH, -1)
    m = ga * (o_attn @ lp['w_o_attn']) + gb * (o_rwkv @ lp['w_o_rwkv']) + gc * (o_sgu @ lp['w_o_sgu'])
    return m @ lp['w_out']


def token_mixers(h, hc, lp, rope, v_first, vres, last):
    pq, pk, pv, p_rw, p_sg, p_gt = jnp.split(h @ lp['w_in'], IN_CUTS, -1)
    cq, ck, cv, c_rw, c_sg, c_gt = jnp.split(hc @ lp['w_in'], IN_CUTS, -1)
    B, L = hc.shape[:2]
    q = apply_rope(head_rmsnorm(pq, ATT_HEADS, lp['q_gain']), *rope)
    k = apply_rope(head_rmsnorm(pk, ATT_KV_HEADS, lp['k_gain']), *rope)
    v = pv.reshape(pv.shape[:-1] + (ATT_KV_HEADS, HEAD_DIM))
    kc = head_rmsnorm(ck, ATT_KV_HEADS, lp['k_gain'])
    vc = cv.reshape(B, L, ATT_KV_HEADS, HEAD_DIM)
    o_attn = window_attention(q, k, v, kc, vc, lp['sink'])
    r, kr, vr, xw, xa, xg = jnp.split(short_conv(p_rw, lp['conv']), RWKV_CUTS, -1)
    rc, krc, vrc, xwc, xac, xgc = jnp.split(short_conv(c_rw, lp['conv']), RWKV_CUTS, -1)
    if vres is not None:
        vr = value_residual(vr, v_first[0], h, *vres)
        vrc = value_residual(vrc, v_first[1], hc, *vres)
    state0 = jnp.zeros((2, B, RWKV_HEADS, RWKV_HEAD, RWKV_HEAD), jnp.float32)
    oc_rwkv, state_c = rwkv_time_mix(rc, krc, vrc, xwc, xac, xgc, state0, lp, not last)
    o_rwkv, _ = rwkv_time_mix(r, kr, vr, xw, xa, xg, state_c, lp, True)
    o_sgu = spatial_gating(p_sg, lp)
    mix = merge_branches(o_attn, o_rwkv, o_sgu, p_gt, lp)
    mix_c = None
    if not last:
        qc = head_rmsnorm(cq, ATT_HEADS, lp['q_gain'])
        mix_c = merge_branches(context_attention(qc, kc, vc, lp['sink']), oc_rwkv,
                               spatial_gating(c_sg, lp), c_gt, lp)
    return mix, mix_c, (vr, vrc)


def swiglu(h, w1, w3, w2):
    return (jax.nn.silu(h @ w1) * (h @ w3)) @ w2


def setup_inputs(seed: int = 0) -> dict:
    key = jax.random.key(seed)
    ks = list(jax.random.split(key, 48))
    nrm = lambda shape, scale: jax.random.normal(ks.pop(), shape, jnp.float32) * scale
    D, L, C, F = D_MODEL, DEPTH, RWKV_DIM, FFN_HIDDEN
    centre = jnp.eye(SHIFT_WIDTH, dtype=jnp.float32)[SHIFT_WIDTH // 2][None, :, None]
    return {
        'x': nrm((BATCH, SEQ, D), 1.0),
        'c': nrm((BATCH, D), 1.0),
        'ctx': nrm((BATCH, CTX_LEN, D), 1.0),
        'c_ctx': nrm((D,), 1.0),
        'w_mod': nrm((L, D, 6 * D), 0.3 * D ** -0.5),
        'b_mod': nrm((L, 6 * D), 0.02),
        'norm_mix': 1.0 + nrm((L, D), 0.05),
        'norm_ffn': 1.0 + nrm((L, D), 0.05),
        'w_in': nrm((L, D, IN_DIM), D ** -0.5),
        'q_gain': 1.0 + nrm((L, HEAD_DIM), 0.1),
        'k_gain': 1.0 + nrm((L, HEAD_DIM), 0.1),
        'attn_sink': nrm((L, ATT_HEADS), 0.5),
        'rwkv_conv': centre + nrm((L, SHIFT_WIDTH, RWKV_IN), 0.2),
        'rwkv_w0': jax.random.uniform(ks.pop(), (L, 2, C), jnp.float32, -6.0, 0.0),
        'rwkv_w_up': nrm((L, 2, DECAY_LORA, C), 0.5 * DECAY_LORA ** -0.5),
        'rwkv_a0': nrm((L, 2, C), 0.1),
        'rwkv_a_up': nrm((L, 2, ICLR_LORA, C), 0.5 * ICLR_LORA ** -0.5),
        'rwkv_k_k': 0.85 + nrm((L, C), 0.05),
        'rwkv_k_a': 1.0 + nrm((L, C), 0.05),
        'rwkv_r_k': nrm((L, RWKV_HEADS, RWKV_HEAD), 0.1),
        'rwkv_g_up': nrm((L, GATE_LORA, C), GATE_LORA ** -0.5),
        'rwkv_ln_w': 1.0 + nrm((L, C), 0.05),
        'rwkv_ln_b': nrm((L, C), 0.02),
        'rwkv_vres_down': nrm((L - 1, D, VRES_LORA), D ** -0.5),
        'rwkv_vres_up': nrm((L - 1, VRES_LORA, C), VRES_LORA ** -0.5),
        'rwkv_vres_b': nrm((L - 1, C), 0.1),
        'sgu_ln_w': 1.0 + nrm((L, SGU_DIM), 0.05),
        'sgu_ln_b': nrm((L, SGU_DIM), 0.02),
        'sgu_w': nrm((L, SGU_GROUPS, CHUNK, CHUNK), CHUNK ** -0.5),
        'sgu_b': 1.0 + nrm((L, SGU_GROUPS, CHUNK), 0.1),
        'w_o_attn': nrm((L, ATT_Q_DIM, D), ATT_Q_DIM ** -0.5),
        'w_o_rwkv': nrm((L, C, D), C ** -0.5),
        'w_o_sgu': nrm((L, SGU_DIM, D), SGU_DIM ** -0.5),
        'w_out': nrm((L, D, D), D ** -0.5),
        'ffn_w1': nrm((L, D, F), D ** -0.5),
        'ffn_w3': nrm((L, D, F), D ** -0.5),
        'ffn_w2': nrm((L, F, D), F ** -0.5),
    }


def reference(x, c, ctx, c_ctx, w_mod, b_mod, norm_mix, norm_ffn, w_in, q_gain, k_gain, attn_sink,
              rwkv_conv, rwkv_w0, rwkv_w_up, rwkv_a0, rwkv_a_up, rwkv_k_k, rwkv_k_a, rwkv_r_k, rwkv_g_up,
              rwkv_ln_w, rwkv_ln_b, rwkv_vres_down, rwkv_vres_up, rwkv_vres_b,
              sgu_ln_w, sgu_ln_b, sgu_w, sgu_b, w_o_attn, w_o_rwkv, w_o_sgu, w_out,
              ffn_w1, ffn_w3, ffn_w2):
    rope = axial_rope(x.shape[1])
    xc = ctx
    v_first = None
    for l in range(DEPTH):
        last = l == DEPTH - 1
        lp = {
            'w_in': w_in[l], 'q_gain': q_gain[l], 'k_gain': k_gain[l], 'sink': attn_sink[l],
            'conv': rwkv_conv[l], 'w0': rwkv_w0[l], 'w_up': rwkv_w_up[l], 'a0': rwkv_a0[l],
            'a_up': rwkv_a_up[l], 'k_k': rwkv_k_k[l], 'k_a': rwkv_k_a[l], 'r_k': rwkv_r_k[l],
            'g_up': rwkv_g_up[l], 'ln_w': rwkv_ln_w[l], 'ln_b': rwkv_ln_b[l],
            'sgu_ln_w': sgu_ln_w[l], 'sgu_ln_b': sgu_ln_b[l], 'sgu_w': sgu_w[l], 'sgu_b': sgu_b[l],
            'w_o_attn': w_o_attn[l], 'w_o_rwkv': w_o_rwkv[l], 'w_o_sgu': w_o_sgu[l], 'w_out': w_out[l],
        }
        vres = None if l == 0 else (rwkv_vres_down[l - 1], rwkv_vres_up[l - 1], rwkv_vres_b[l - 1])
        mod = jax.nn.silu(c) @ w_mod[l] + b_mod[l]
        mod_c = jax.nn.silu(c_ctx) @ w_mod[l] + b_mod[l]
        sh1, sc1, g1, sh2, sc2, g2 = jnp.split(mod[:, None, :], 6, axis=-1)
        csh1, csc1, cg1, csh2, csc2, cg2 = jnp.split(mod_c[None, None, :], 6, axis=-1)
        h = modulate(rmsnorm(x, norm_mix[l]), sh1, sc1)
        hc = modulate(rmsnorm(xc, norm_mix[l]), csh1, csc1)
        mix, mix_c, v_pair = token_mixers(h, hc, lp, rope, v_first, vres, last)
        if l == 0:
            v_first = v_pair
        x = x + g1 * mix
        h = modulate(rmsnorm(x, norm_ffn[l]), sh2, sc2)
        x = x + g2 * swiglu(h, ffn_w1[l], ffn_w3[l], ffn_w2[l])
        if not last:
            xc = xc + cg1 * mix_c
            hc = modulate(rmsnorm(xc, norm_ffn[l]), csh2, csc2)
            xc = xc + cg2 * swiglu(hc, ffn_w1[l], ffn_w3[l], ffn_w2[l])
    return x
```

```python
import numpy as np
import ml_dtypes
from contextlib import ExitStack
import concourse.bass as bass
import concourse.mybir as mybir
from concourse.bass_utils import run_bass_kernel_spmd

F32 = mybir.dt.float32
BF16 = mybir.dt.bfloat16
AF = mybir.ActivationFunctionType
ALU = mybir.AluOpType
AX = mybir.AxisListType

D = 1024
DEPTH = 2
LCTX = 256
NH = 8
HD = 64
IN_DIM = 6656
FFN = 2816
NFC = FFN // 128
EPS = 1e-6
GN_EPS = 64e-5
CH = 64
NEG = -30000.0


class Tok:
    __slots__ = ("w", "r", "pw", "pr")

    def __init__(self):
        self.w = []
        self.r = {}
        self.pw = []
        self.pr = {}


class TT:
    def __init__(self, t):
        self.t = t
        self.tok = Tok()

    def __getitem__(self, k):
        return self.t[k]


class KB:
    def __init__(self, nc, es):
        self.nc = nc
        self.es = es
        self.E = dict(pe=nc.tensor, dve=nc.vector, act=nc.scalar, pool=nc.gpsimd, sp=nc.sync)
        self.semobj = {}
        self.cnt = {}
        for k in self.E:
            self.semobj[k] = es.enter_context(nc.semaphore("c_" + k))
            self.cnt[k] = 0
        self.known = {k: {} for k in self.E}
        self.NDS = 8
        self.dq = {}
        for q in ("sp", "pool", "act"):
            names = []
            for i in range(self.NDS):
                nm = "d_%s%d" % (q, i)
                self.semobj[nm] = es.enter_context(nc.semaphore(nm))
                self.cnt[nm] = 0
                names.append(nm)
            self.dq[q] = [names, 0]
        self.uid = 0

    def sb(self, st, shape, dt, name=None):
        self.uid += 1
        return TT(st.enter_context(self.nc.sbuf_tensor("%s_%d" % (name or "t", self.uid), list(shape), dt)))

    def ps(self, st, shape, dt=F32, name=None):
        self.uid += 1
        return TT(st.enter_context(self.nc.psum_tensor("%s_%d" % (name or "p", self.uid), list(shape), dt)))

    def _wait(self, eng, ev):
        s, v = ev
        if v <= 0:
            return
        if self.known[eng].get(s, 0) >= v:
            return
        self.E[eng].wait_ge(self.semobj[s], v)
        self.known[eng][s] = v

    def _deps(self, eng, R, W, acc=False):
        for t in R:
            tk = t.tok if hasattr(t, 'tok') else t
            for ev in tk.w:
                if not (eng == "pe" and ev[0] == "pe"):
                    self._wait(eng, ev)
        for t in W:
            tk = t.tok if hasattr(t, 'tok') else t
            for ev in (tk.pw if acc else tk.w):
                if not (eng == "pe" and ev[0] == "pe"):
                    self._wait(eng, ev)
            rr = list(tk.r.items()) + (list(tk.pr.items()) if acc else [])
            for s, v in rr:
                if eng == "pe" and s == "pe":
                    continue
                self._wait(eng, (s, v))

    def _record(self, ev, R, W, acc=False):
        for t in R:
            tk = t.tok if hasattr(t, 'tok') else t
            if tk.r.get(ev[0], 0) < ev[1]:
                tk.r[ev[0]] = ev[1]
        for t in W:
            tk = t.tok if hasattr(t, 'tok') else t
            if acc:
                tk.w = [e for e in tk.w if e[0] != ev[0]] + [ev]
            else:
                tk.pw = tk.w
                tk.pr = tk.r
                tk.w = [ev]
                tk.r = {}

    def op(self, eng, fn, R=(), W=(), inc=True, acc=False):
        self._deps(eng, R, W, acc)
        ins = fn()
        ev = (eng, self.cnt[eng] + 1)
        if inc:
            ins.then_inc(self.semobj[eng], 1)
            self.cnt[eng] += 1
        self._record(ev, R, W, acc)
        return ins

    def dma(self, q, out, in_, R=(), W=(), acc=False, **kw):
        self._deps(q, R, W, acc)
        names, i = self.dq[q]
        nm = names[i % self.NDS]
        self.dq[q][1] = i + 1
        self._wait(q, (nm, self.cnt[nm]))
        self.E[q].dma_start(out=out, in_=in_, **kw).then_inc(self.semobj[nm], 16)
        self.cnt[nm] += 16
        self._record((nm, self.cnt[nm]), R, W, acc)

    def barrier(self):
        for e in self.E:
            for s, v in self.cnt.items():
                if s == e and e != "pe":
                    pass
                self._wait(e, (s, v))

    def mm(self, out, lhsT, rhs, start, stop, R, W, inc=None):
        if inc is None:
            inc = stop
        return self.op("pe", lambda: self.nc.tensor.matmul(out, lhsT=lhsT, rhs=rhs, start=start, stop=stop), R, W, inc)

    def tr(self, out, in_, ident, R, W, inc=True):
        return self.op("pe", lambda: self.nc.tensor.transpose(out, in_, ident), R, W, inc)

    def act(self, out, in_, func, R, W, bias=None, scale=None, accum=None, acc=False):
        kw = {}
        if bias is not None:
            kw["bias"] = bias
        if scale is not None:
            kw["scale"] = scale
        if accum is not None:
            kw["accum_out"] = accum
        return self.op("act", lambda: self.nc.scalar.activation(out=out, in_=in_, func=func, **kw), R, W, acc=acc)

    def tt(self, eng, out, in0, in1, op, R, W, acc=False):
        e = self.E[eng]
        return self.op(eng, lambda: e.tensor_tensor(out=out, in0=in0, in1=in1, op=op), R, W, acc=acc)

    def ts(self, eng, out, in0, s1, op0, R, W, s2=None, op1=None, accum=None, acc=False):
        e = self.E[eng]
        kw = {}
        if op1 is not None:
            kw["op1"] = op1
        if accum is not None:
            kw["accum_out"] = accum
        return self.op(eng, lambda: e.tensor_scalar(out=out, in0=in0, scalar1=s1, scalar2=s2, op0=op0, **kw), R, W, acc=acc)

    def stt(self, out, in0, scalar, in1, op0, op1, R, W, acc=False):
        return self.op("dve", lambda: self.nc.vector.scalar_tensor_tensor(out=out, in0=in0, scalar=scalar, in1=in1, op0=op0, op1=op1), R, W, acc=acc)

    def cp(self, eng, out, in_, R, W, acc=False):
        e = self.E[eng]
        if eng == "act":
            return self.op(eng, lambda: e.copy(out=out, in_=in_), R, W, acc=acc)
        return self.op(eng, lambda: e.tensor_copy(out=out, in_=in_), R, W, acc=acc)

    def recip(self, out, in_, R, W):
        return self.op("dve", lambda: self.nc.vector.reciprocal(out=out, in_=in_), R, W)

    def memset(self, eng, ap, val, W):
        e = self.E[eng]
        return self.op(eng, lambda: e.memset(ap, val), (), W)


def host_consts(T):
    c = {}
    c["ident_f"] = np.eye(128, dtype=np.float32)
    ob = np.zeros((128, 128), np.float32)
    ob[:64, :64] = 1.0
    ob[64:, 64:] = 1.0
    c["onesblk"] = ob
    c["ones_f"] = np.ones((128, 128), np.float32)
    rm = np.zeros((128, 128), np.float32)
    for h in range(2):
        for d in range(64):
            if d < 32:
                rm[h * 64 + d + 32, h * 64 + d] = -1.0
            else:
                rm[h * 64 + d - 32, h * 64 + d] = 1.0
    c["rotm"] = rm
    rows = T // 64
    row = np.repeat(np.arange(rows), 64).astype(np.float32)
    col = np.tile(np.arange(64), rows).astype(np.float32)
    inv = (10000.0 ** (-np.arange(16, dtype=np.float32) / 16)).astype(np.float32)
    ang = np.concatenate([row[:, None] * inv, col[:, None] * inv], -1)
    cos = np.cos(ang).astype(np.float32).T
    sin = np.sin(ang).astype(np.float32).T
    c["ropecos"] = np.ascontiguousarray(np.tile(cos, (4, 1)))
    c["ropesin"] = np.ascontiguousarray(np.tile(sin, (4, 1)))
    j = np.arange(128)[:, None]
    p = np.arange(128)[None, :]
    mprev = np.where(j >= p, 0.0, NEG).astype(np.float32)
    mnext = np.where(j <= p, 0.0, NEG).astype(np.float32)
    c["maskprev"] = np.ascontiguousarray(np.tile(mprev, (1, 4)))
    c["masknext"] = np.ascontiguousarray(np.tile(mnext, (1, 4)))
    s = np.arange(64)[:, None]
    t = np.arange(64)[None, :]
    m = np.zeros((64, 4, 64), np.float32)
    m[:, 0, :] = (s < t)
    m[:, 1, :] = (s <= t)
    m[:, 2, :] = (s > t)
    m[:, 3, :] = (s >= t)
    c["rmask"] = m
    rs = np.ones((128, 8, 128), np.float32)
    rs[:, :, 0::64] = 0.0
    c["scanreset"] = rs
    return c


CONST_DT = {"ident_f": F32, "onesblk": F32, "ones_f": F32, "rotm": F32, "ropecos": F32, "ropesin": F32,
            "maskprev": F32, "masknext": F32, "rmask": F32, "scanreset": F32}

PARAMS = [
    ("w_mod", (DEPTH, D, 6 * D)), ("b_mod", (DEPTH, 6 * D)), ("norm_mix", (DEPTH, D)), ("norm_ffn", (DEPTH, D)),
    ("w_in", (DEPTH, D, IN_DIM)), ("q_gain", (DEPTH, 64)), ("k_gain", (DEPTH, 64)), ("attn_sink", (DEPTH, 8)),
    ("rwkv_conv", (DEPTH, 3, 1792)), ("rwkv_w0", (DEPTH, 2, 512)), ("rwkv_w_up", (DEPTH, 2, 64, 512)),
    ("rwkv_a0", (DEPTH, 2, 512)), ("rwkv_a_up", (DEPTH, 2, 64, 512)), ("rwkv_k_k", (DEPTH, 512)),
    ("rwkv_k_a", (DEPTH, 512)), ("rwkv_r_k", (DEPTH, 8, 64)), ("rwkv_g_up", (DEPTH, 128, 512)),
    ("rwkv_ln_w", (DEPTH, 512)), ("rwkv_ln_b", (DEPTH, 512)), ("rwkv_vres_down", (DEPTH - 1, D, 32)),
    ("rwkv_vres_up", (DEPTH - 1, 32, 512)), ("rwkv_vres_b", (DEPTH - 1, 512)), ("sgu_ln_w", (DEPTH, 512)),
    ("sgu_ln_b", (DEPTH, 512)), ("sgu_w", (DEPTH, 4, 128, 128)), ("sgu_b", (DEPTH, 4, 128)),
    ("w_o_attn", (DEPTH, 512, D)), ("w_o_rwkv", (DEPTH, 512, D)), ("w_o_sgu", (DEPTH, 512, D)),
    ("w_out", (DEPTH, D, D)), ("ffn_w1", (DEPTH, D, FFN)), ("ffn_w3", (DEPTH, D, FFN)), ("ffn_w2", (DEPTH, FFN, D)),
]


SP_LAYOUT = [("bmod", 48), ("nmix", 8), ("nffn", 8), ("gain", 2), ("sink", 8), ("conv_rkv", 72), ("conv_wa", 6),
             ("conv_g", 3), ("w0", 8), ("a0", 8), ("k_k", 8), ("k_a", 8), ("r_k", 8), ("ln_w", 8), ("ln_b", 8),
             ("vres_b", 8), ("sgu_lnw", 512), ("sgu_lnb", 512), ("sgu_b", 512)]
SP_OFF = {}
_o = 0
for _n, _w in SP_LAYOUT:
    SP_OFF[_n] = (_o, _w)
    _o += _w
SP_N = _o


def pack_small(inp, l):
    a = np.zeros((128, SP_N), np.float32)

    def put(name, arr):
        o, w = SP_OFF[name]
        arr = np.asarray(arr, np.float32)
        a[: arr.shape[0], o:o + w] = arr.reshape(arr.shape[0], -1)

    put("bmod", inp["b_mod"][l].reshape(48, 128).T)
    put("nmix", inp["norm_mix"][l].reshape(8, 128).T)
    put("nffn", inp["norm_ffn"][l].reshape(8, 128).T)
    put("gain", np.stack([np.tile(inp["q_gain"][l], 2), np.tile(inp["k_gain"][l], 2)], 1))
    put("sink", np.tile(inp["attn_sink"][l][None, :], (128, 1)))
    cv = inp["rwkv_conv"][l]
    put("conv_rkv", cv[:, :1536].reshape(3, 3, 8, 64).transpose(3, 0, 1, 2))
    put("conv_wa", cv[:, 1536:1664].reshape(3, 2, 64).transpose(2, 0, 1))
    put("conv_g", cv[:, 1664:1792].T)
    dup = lambda a_: np.concatenate([a_, a_], 0)
    put("w0", inp["rwkv_w0"][l].reshape(2, 8, 64).transpose(0, 2, 1).reshape(128, 8))
    put("a0", inp["rwkv_a0"][l].reshape(2, 8, 64).transpose(0, 2, 1).reshape(128, 8))
    put("k_k", dup(inp["rwkv_k_k"][l].reshape(8, 64).T))
    put("k_a", dup(inp["rwkv_k_a"][l].reshape(8, 64).T))
    put("r_k", dup(inp["rwkv_r_k"][l].reshape(8, 64).T))
    put("ln_w", inp["rwkv_ln_w"][l].reshape(8, 64).T)
    put("ln_b", inp["rwkv_ln_b"][l].reshape(8, 64).T)
    if l > 0:
        put("vres_b", dup(inp["rwkv_vres_b"][l - 1].reshape(8, 64).T))
    put("sgu_lnw", np.tile(inp["sgu_ln_w"][l][None, :], (128, 1)))
    put("sgu_lnb", np.tile(inp["sgu_ln_b"][l][None, :], (128, 1)))
    put("sgu_b", inp["sgu_b"][l].reshape(1, 512))
    return a


class Prog:
    def __init__(self, T, nlayers=DEPTH, dbg=(), stop_after=None):
        self.T = T
        self.NTOK = LCTX + T
        self.nlayers = nlayers
        self.dbg = set(dbg)
        self.stop_after = stop_after
        self.blocks = [(0, LCTX, True, 0)] + [(LCTX + i * 512, 512, False, i * 512) for i in range(T // 512)]
        self.nc = bass.Bass("TRN2", target_bir_lowering=False)
        self.es = ExitStack()

    def din(self, name, shape, dt=F32):
        return self.nc.dram_tensor(name, list(shape), dt, kind="ExternalInput").ap()

    def scr(self, name, shape, dt):
        kind = "ExternalOutput" if name in self.dbg else "Internal"
        return self.nc.dram_tensor(name, list(shape), dt, kind=kind).ap()

    def build(self):
        nc, T, NTOK = self.nc, self.T, self.NTOK
        with self.es as es:
            k = self.k = KB(nc, es)
            self.x_in = self.din("x", [T, D])
            self.ctx_in = self.din("ctx", [LCTX, D])
            self.cfm = self.din("cfm", [128, 16])
            self.smallp_d = self.din("smallp", [DEPTH, 128, SP_N])
            self.cd = {}
            hc = host_consts(T)
            for name, arr in hc.items():
                self.cd[name] = self.din("c_" + name, arr.shape)
            self.pd = {}
            for name, shape in PARAMS:
                if name in ("b_mod", "norm_mix", "norm_ffn", "q_gain", "k_gain", "attn_sink", "rwkv_conv", "rwkv_w0",
                            "rwkv_a0", "rwkv_k_k", "rwkv_k_a", "rwkv_r_k", "rwkv_ln_w", "rwkv_ln_b", "rwkv_vres_b",
                            "sgu_ln_w", "sgu_ln_b", "sgu_b"):
                    continue
                self.pd[name] = self.din(name, shape)
            self.out = nc.dram_tensor("out", [T, D], F32, kind="ExternalOutput").ap()
            L = self.nlayers
            self.XT = self.scr("XT", [8, 128, NTOK], F32)
            self.HT = self.scr("HT", [8, 128, NTOK], BF16)
            self.QT = self.scr("QT", [4, 128, NTOK], BF16)
            self.KT = self.scr("KT", [128, NTOK], BF16)
            self.VA = self.scr("VA", [NTOK, 130], BF16)
            self.PRW = self.scr("PRW", [1792, NTOK], F32)
            self.PRB = self.scr("PRB", [1536, NTOK], BF16)
            self.OST = self.scr("OST", [4, 128, NTOK], BF16)
            self.OAT = self.scr("OAT", [4, 128, NTOK], BF16)
            self.ORT = self.scr("ORT", [64, 8, NTOK], BF16)
            NCH = NTOK // CH
            self.RA4 = [self.scr("RA4_%d" % d_, [64, 8, 4, NTOK], BF16) for d_ in range(2)]
            self.TOKB = self.scr("TOKB", [NTOK, 8, 2, 128], BF16)
            self.TOKV = self.scr("TOKV", [NTOK, 8, 64], BF16)
            self.RWC = [self.scr("RWC_%d" % d_, [NCH, 64, 8], F32) for d_ in range(2)]
            self.RBV = self.scr("RBV", [64, 8, NTOK], F32)
            self.RG = self.scr("RG", [64, 8, NTOK], F32)
            self.VF = self.scr("VF", [64, 8, NTOK], F32)
            self.YD = [self.scr("YD_%d" % d_, [64, 8, NTOK], F32) for d_ in range(2)]
            self.WIN = [self.scr("WIN%d" % l, [D, IN_DIM], BF16) for l in range(L)]
            self.WOA = [self.scr("WOA%d" % l, [512, D], BF16) for l in range(L)]
            self.WOR = [self.scr("WOR%d" % l, [512, D], BF16) for l in range(L)]
            self.WOS = [self.scr("WOS%d" % l, [512, D], BF16) for l in range(L)]
            self.WOUT = [self.scr("WOUT%d" % l, [D, D], BF16) for l in range(L)]
            self.W1 = [self.scr("W1_%d" % l, [D, FFN], BF16) for l in range(L)]
            self.W3 = [self.scr("W3_%d" % l, [D, FFN], BF16) for l in range(L)]
            self.W2 = [self.scr("W2_%d" % l, [FFN, D], BF16) for l in range(L)]

            g = es
            self.smallp = k.sb(g, [128, DEPTH, SP_N], F32, "smallp")
            for l in range(DEPTH):
                k.dma("sp", self.smallp[:, l, :], self.smallp_d[l], W=[self.smallp], acc=(l > 0))
            self.ident_f = k.sb(g, [128, 128], F32, "identf")
            self.ones_f = k.sb(g, [128, 128], F32, "onesf")
            self.onesblk_f = k.sb(g, [128, 128], F32, "onesblkf")
            self.rotm_f = k.sb(g, [128, 128], F32, "rotmf")
            k.dma("sp", self.ident_f[:], self.cd["ident_f"], W=[self.ident_f])
            k.dma("sp", self.ones_f[:], self.cd["ones_f"], W=[self.ones_f])
            k.dma("sp", self.onesblk_f[:], self.cd["onesblk"], W=[self.onesblk_f])
            k.dma("sp", self.rotm_f[:], self.cd["rotm"], W=[self.rotm_f])
            self.ident_b = k.sb(g, [128, 128], BF16, "identb")
            self.onesblk_b = k.sb(g, [128, 128], BF16, "onesblkb")
            self.rotm_b = k.sb(g, [128, 128], BF16, "rotmb")
            k.cp("dve", self.ident_b[:], self.ident_f[:], [self.ident_f], [self.ident_b])
            k.cp("dve", self.onesblk_b[:], self.onesblk_f[:], [self.onesblk_f], [self.onesblk_b])
            k.cp("dve", self.rotm_b[:], self.rotm_f[:], [self.rotm_f], [self.rotm_b])
            self.mods = k.sb(g, [128, 2, 6, 8], F32, "mods")
            self.epsD = k.sb(g, [128, 1], F32, "epsD")
            k.memset("dve", self.epsD[:], EPS, [self.epsD])
            self.epsG = k.sb(g, [128, 1], F32, "epsG")
            k.memset("dve", self.epsG[:], GN_EPS, [self.epsG])

            sa = self.stop_after
            if sa is None or sa[0] != "consts":
                self.phase_weights()
            k.barrier()
            if sa is not None and sa[0] in ("weights", "consts"):
                L = 0
            if L > 0:
                self.phase_transpose_in()
                k.barrier()
            if sa is not None and sa[0] == "tin":
                L = 0
            for l in range(L):
                self.phase_mod(l)
                k.barrier()
                if sa == ("mod", l):
                    break
                self.phase_inproj(l)
                k.barrier()
                if self.stop_after == ("inproj", l):
                    break
                last = (l == self.nlayers - 1)
                self.phase_attn(l, last)
                k.barrier()
                if self.stop_after == ("attn", l):
                    break
                self.phase_rwkv_prep(l)
                k.barrier()
                if self.stop_after == ("rprep", l):
                    break
                self.phase_rwkv_scan(l)
                k.barrier()
                self.phase_rwkv_post(l)
                k.barrier()
                if self.stop_after == ("rwkv", l):
                    break
                self.phase_merge(l, last)
                k.barrier()
                self.phase_ffn(l, last)
                k.barrier()
                if self.stop_after == ("layer", l):
                    break
            self.finish()
        return nc

    def spc(self, l, name, rows=128):
        o, w = SP_OFF[name]
        return self.smallp[0:rows, l, o:o + w]

    def finish(self):
        k = self.k
        k.barrier()

    def phase_weights(self):
        k, nc = self.k, self.nc
        CW = 3328
        with ExitStack() as st:
            stg = [k.sb(st, [128, CW], F32, "wstg") for _ in range(3)]
            bfb = [k.sb(st, [128, CW], BF16, "wbf") for _ in range(3)]
            engs = ["dve", "act", "dve"]
            ctr = [0]

            def conv(src, dst, R, C, permq=False):
                for r0 in range(0, R, 128):
                    rr = min(128, R - r0)
                    for c0 in range(0, C, CW):
                        cc = min(CW, C - c0)
                        i = ctr[0] % 3
                        ctr[0] += 1
                        k.dma("sp", stg[i][0:rr, 0:cc], src[r0:r0 + rr, c0:c0 + cc], W=[stg[i]])
                        k.cp(engs[i], bfb[i][0:rr, 0:cc], stg[i][0:rr, 0:cc], [stg[i]], [bfb[i]])
                        if permq and c0 == 0:
                            for g_ in range(2):
                                k.dma("pool", dst[r0:r0 + rr, 0:512].rearrange("p (j g d) -> p j g d", j=4, g=2, d=64)[:, :, g_, :],
                                      bfb[i][0:rr, 0:512].rearrange("p (g j d) -> p g j d", g=2, j=4, d=64)[:, g_, :, :], R=[bfb[i]])
                            k.dma("pool", dst[r0:r0 + rr, 512:cc], bfb[i][0:rr, 512:cc], R=[bfb[i]])
                        else:
                            k.dma("pool", dst[r0:r0 + rr, c0:c0 + cc], bfb[i][0:rr, 0:cc], R=[bfb[i]])

            for l in range(self.nlayers):
                conv(self.pd["w_in"][l], self.WIN[l], D, IN_DIM, permq=True)
                conv(self.pd["w_o_attn"][l], self.WOA[l], 512, D)
                conv(self.pd["w_o_rwkv"][l], self.WOR[l], 512, D)
                conv(self.pd["w_o_sgu"][l], self.WOS[l], 512, D)
                conv(self.pd["w_out"][l], self.WOUT[l], D, D)
                conv(self.pd["ffn_w1"][l], self.W1[l], D, FFN)
                conv(self.pd["ffn_w3"][l], self.W3[l], D, FFN)
                conv(self.pd["ffn_w2"][l], self.W2[l], FFN, D)

    def phase_transpose_in(self):
        k, nc = self.k, self.nc
        XTv = self.XT.rearrange("k p n -> p k n")
        with ExitStack() as st:
            xin = [k.sb(st, [128, D], F32, "xin") for _ in range(2)]
            xo = [k.sb(st, [128, 8, 128], F32, "xo") for _ in range(2)]
            pbs = [k.ps(st, [128, 4, 128], F32, "ptr") for _ in range(4)]
            ntile = self.NTOK // 128
            for i in range(ntile):
                src = self.ctx_in[i * 128:(i + 1) * 128, :] if i < 2 else self.x_in[(i - 2) * 128:(i - 1) * 128, :]
                xi, xoo = xin[i % 2], xo[i % 2]
                k.dma("sp", xi[:], src, W=[xi])
                for hlf in range(2):
                    pb = pbs[(2 * i + hlf) % 4]
                    for j in range(4):
                        kc = hlf * 4 + j
                        k.tr(pb[:, j, :], xi[:, kc * 128:(kc + 1) * 128], self.ident_f[:], [xi, self.ident_f], [pb], inc=(j == 3))
                    eng = "act" if hlf == 0 else "dve"
                    k.cp(eng, xoo[:, hlf * 4:(hlf + 1) * 4, :], pb[:], [pb], [xoo], acc=(hlf == 1))
                k.dma("pool", XTv[:, :, i * 128:(i + 1) * 128], xoo[:], R=[xoo])

    def phase_mod(self, l):
        k, nc = self.k, self.nc
        with ExitStack() as st:
            cs = k.sb(st, [128, 16], F32, "cs")
            sc = k.sb(st, [128, 8, 2], F32, "sc")
            k.dma("sp", cs[:], self.cfm, W=[cs])
            k.act(sc[:].rearrange("p k m -> p m k"), cs[:].rearrange("p (m k) -> p m k", m=2), AF.Silu, [cs], [sc])
            wm = [k.sb(st, [128, 6 * D], F32, "wm") for _ in range(2)]
            pm = k.ps(st, [128, 48, 2], F32, "pmod")
            for kc in range(8):
                w = wm[kc % 2]
                k.dma("sp", w[:, 0:3072], self.pd["w_mod"][l][kc * 128:(kc + 1) * 128, 0:3072], W=[w])
                k.dma("sp", w[:, 3072:6144], self.pd["w_mod"][l][kc * 128:(kc + 1) * 128, 3072:6144], W=[w], acc=True)
                for fc in range(48):
                    first = (kc == 0 and fc == 0)
                    last = (kc == 7 and fc == 47)
                    k.op("pe", lambda: nc.tensor.matmul(pm[:, fc, :], lhsT=w[:, fc * 128:(fc + 1) * 128], rhs=sc[:, kc, :],
                                                        start=first, stop=last, skip_group_check=True),
                         [w, sc], [pm], inc=(fc == 47))
            raw = k.sb(st, [128, 2, 6, 8], F32, "modraw")
            bm = self.spc(l, "bmod")
            for m in range(2):
                k.tt("dve", raw[:, m, :, :], pm[:, :, m].rearrange("p (j k) -> p j k", j=6),
                     bm.rearrange("p (j k) -> p j k", j=6), ALU.add, [pm, self.smallp], [raw], acc=(m == 1))
            mods = self.mods
            for m in range(2):
                for j in (0, 2, 3, 5):
                    k.cp("dve", mods[:, m, j, :], raw[:, m, j, :], [raw], [mods], acc=not (m == 0 and j == 0))
                k.stt(mods[:, m, 1, :], raw[:, m, 1, :], 1.0, self.spc(l, "nmix"), ALU.add, ALU.mult, [raw, self.smallp], [mods], acc=True)
                k.stt(mods[:, m, 4, :], raw[:, m, 4, :], 1.0, self.spc(l, "nffn"), ALU.add, ALU.mult, [raw, self.smallp], [mods], acc=True)

    def norm_mod(self, st_tiles, blk, l, which):
        k, nc = self.k, self.nc
        xT, work, hT, rstd, pss = st_tiles
        n0, nb, is_ctx, t0 = blk
        m = 1 if is_ctx else 0
        jA, jS = (1, 0) if which == 0 else (4, 3)
        k.act(work[:, :, 0:nb], xT[:, :, 0:nb], AF.Square, [xT], [work])
        for kc in range(8):
            k.mm(pss[:, 0:nb], self.ones_f[:], work[:, kc, 0:nb], kc == 0, kc == 7, [self.ones_f, work], [pss])
        k.act(rstd[:, 0:nb], pss[:, 0:nb], AF.Ln, [pss, self.epsD], [rstd], bias=self.epsD[:], scale=1.0 / D)
        k.act(rstd[:, 0:nb], rstd[:, 0:nb], AF.Exp, [rstd], [rstd], scale=-0.5)
        for kc in range(8):
            k.stt(work[:, kc, 0:nb], xT[:, kc, 0:nb], self.mods[:, m, jA, kc:kc + 1], rstd[:, 0:nb], ALU.mult, ALU.mult,
                  [xT, self.mods, rstd], [work], acc=(kc > 0))
        for kc in range(8):
            k.act(hT[:, kc, 0:nb], work[:, kc, 0:nb], AF.Identity, [work, self.mods], [hT], bias=self.mods[:, m, jS, kc:kc + 1],
                  acc=(kc > 0))

    def phase_inproj(self, l):
        k, nc = self.k, self.nc
        XTv = self.XT.rearrange("k p n -> p k n")
        HTv = self.HT.rearrange("k p n -> p k n")
        QTv = self.QT.rearrange("j p n -> p j n")
        OSTv = self.OST.rearrange("g p n -> p g n")
        with ExitStack() as st:
            wA = k.sb(st, [128, 8, 3584], BF16, "wA")
            for kc in range(8):
                k.dma("sp", wA[:, kc, :], self.WIN[l][kc * 128:(kc + 1) * 128, 0:3584], W=[wA], acc=(kc > 0))
            wsf = k.sb(st, [128, 4, 128], F32, "wsf")
            k.dma("sp", wsf[:], self.pd["sgu_w"][l].rearrange("g p q -> p g q"), W=[wsf])
            WsT = k.sb(st, [128, 4, 128], BF16, "WsT")
            xT = k.sb(st, [128, 8, 512], F32, "xT")
            work = k.sb(st, [128, 8, 512], F32, "work")
            hTs = [k.sb(st, [128, 8, 512], BF16, "hT") for _ in range(2)]
            rstd = k.sb(st, [128, 512], F32, "rstd")
            pss = k.ps(st, [128, 512], F32, "pss")
            pproj = [k.ps(st, [128, 512], F32, "pproj") for _ in range(2)]
            paux = [k.ps(st, [128, 512], F32, "paux") for _ in range(1)]
            ptok = [k.ps(st, [128, 512], F32, "ptok") for _ in range(3)]
            pso = k.ps(st, [128, 4, 128], F32, "pso")
            for g_ in range(4):
                k.tr(ptok[0][:, g_ * 128:(g_ + 1) * 128], wsf[:, g_, :], self.ident_f[:], [wsf, self.ident_f], [ptok[0]], inc=(g_ == 3))
            k.cp("dve", WsT[:].rearrange("p g q -> p (g q)"), ptok[0][:], [ptok[0]], [WsT])
            sqb_l = [k.sb(st, [128, 512], BF16, "sqb") for _ in range(2)]
            r1_l = [k.sb(st, [128, 512], F32, "r1") for _ in range(2)]
            qn_l = [k.sb(st, [128, 512], BF16, "qn") for _ in range(2)]
            t1_l = [k.sb(st, [128, 512], F32, "t1") for _ in range(2)]
            t2_l = [k.sb(st, [128, 512], F32, "t2") for _ in range(2)]
            qr = [k.sb(st, [128, 512], BF16, "qr") for _ in range(2)]
            cosb = k.sb(st, [128, 512], F32, "cosb")
            sinb = k.sb(st, [128, 512], F32, "sinb")
            rstg = [k.sb(st, [128, 512], F32, "rstg") for _ in range(3)]
            rstb = [k.sb(st, [128, 512], BF16, "rstb") for _ in range(3)]
            uT = k.sb(st, [128, 4, 512], BF16, "uT")
            ost = k.sb(st, [128, 4, 512], BF16, "ost")
            va = [k.sb(st, [128, 2, 65], BF16, "va") for _ in range(2)]
            for v_ in va:
                k.memset("dve", v_[:], 1.0, [v_])
            gt_l = [k.sb(st, [128, 512], F32, "gt") for _ in range(2)]
            junk_l = [k.sb(st, [128, 512], BF16, "junk") for _ in range(2)]
            vn_l = [k.sb(st, [128, 512], BF16, "vn") for _ in range(2)]
            vnf_l = [k.sb(st, [128, 512], F32, "vnf") for _ in range(2)]
            stat_l = [k.sb(st, [128, 8], F32, "stat") for _ in range(2)]
            ones_row = self.ones_f[0:1, :]
            sgub = self.spc(l, "sgu_b", rows=1)
            lnw = self.spc(l, "sgu_lnw")
            lnb = self.spc(l, "sgu_lnb")
            gain = self.spc(l, "gain")
            ectr = [0]
            def do_norm(bi_):
                n0_, nb_, _c, _t = self.blocks[bi_]
                hT_ = hTs[bi_ % 2]
                k.dma("sp", xT[:, :, 0:nb_], XTv[:, :, n0_:n0_ + nb_], W=[xT])
                self.norm_mod((xT, work, hT_, rstd, pss), self.blocks[bi_], l, 0)
                k.dma("pool", HTv[:, :, n0_:n0_ + nb_], hT_[:, :, 0:nb_], R=[hT_])

            do_norm(0)
            tctr = 0
            for bi, blk in enumerate(self.blocks):
                n0, nb, is_ctx, t0 = blk
                hT = hTs[bi % 2]
                if not is_ctx:
                    k.dma("sp", cosb[:, 0:nb], self.cd["ropecos"][:, t0:t0 + nb], W=[cosb])
                    k.dma("sp", sinb[:, 0:nb], self.cd["ropesin"][:, t0:t0 + nb], W=[sinb])
                if bi + 1 < len(self.blocks):
                    do_norm(bi + 1)
                chunks = [("q", j, j * 128) for j in range(4)] + [("k", 0, 512)] + \
                         [("r", j, 768 + j * 128) for j in range(14)] + [("u", j, 2560 + j * 128) for j in range(4)]
                for ci, (kind, j, c0) in enumerate(chunks):
                    pb = pproj[ci % 2]
                    for kc in range(8):
                        k.mm(pb[:, 0:nb], wA[:, kc, c0:c0 + 128], hT[:, kc, 0:nb], kc == 0, kc == 7, [wA, hT], [pb])
                    if kind in ("q", "k"):
                        sqb, r1, qn, t1, t2 = sqb_l[ci % 2], r1_l[ci % 2], qn_l[ci % 2], t1_l[ci % 2], t2_l[ci % 2]
                        pss2 = paux[0]
                        prot = paux[0]
                        gcol = gain[:, 0:1] if kind == "q" else gain[:, 1:2]
                        k.act(sqb[:, 0:nb], pb[:, 0:nb], AF.Square, [pb], [sqb])
                        k.mm(pss2[:, 0:nb], self.onesblk_b[:], sqb[:, 0:nb], True, True, [self.onesblk_b, sqb], [pss2])
                        k.act(r1[:, 0:nb], pss2[:, 0:nb], AF.Ln, [pss2, self.epsD], [r1], bias=self.epsD[:], scale=1.0 / HD)
                        k.act(r1[:, 0:nb], r1[:, 0:nb], AF.Exp, [r1], [r1], scale=-0.5)
                        dst = qr[ci % 2]
                        if is_ctx:
                            k.stt(dst[:, 0:nb], pb[:, 0:nb], gcol, r1[:, 0:nb], ALU.mult, ALU.mult, [pb, r1, self.smallp], [dst])
                        else:
                            k.stt(qn[:, 0:nb], pb[:, 0:nb], gcol, r1[:, 0:nb], ALU.mult, ALU.mult, [pb, r1, self.smallp], [qn])
                            k.mm(prot[:, 0:nb], self.rotm_b[:], qn[:, 0:nb], True, True, [self.rotm_b, qn], [prot])
                            k.tt("dve", t1[:, 0:nb], qn[:, 0:nb], cosb[:, 0:nb], ALU.mult, [qn, cosb], [t1])
                            k.tt("dve", t2[:, 0:nb], prot[:, 0:nb], sinb[:, 0:nb], ALU.mult, [prot, sinb], [t2])
                            k.tt("dve", dst[:, 0:nb], t1[:, 0:nb], t2[:, 0:nb], ALU.add, [t1, t2], [dst])
                        if kind == "q":
                            k.dma("pool", QTv[:, j, n0:n0 + nb], dst[:, 0:nb], R=[dst])
                        else:
                            k.dma("pool", self.KT[:, n0:n0 + nb], dst[:, 0:nb], R=[dst])
                    elif kind == "r":
                        eng = "act" if ectr[0] % 2 == 0 else "dve"
                        if j < 12:
                            sg = rstb[ectr[0] % 3]
                            k.cp(eng, sg[:, 0:nb], pb[:, 0:nb], [pb], [sg])
                            k.dma("pool", self.PRB[j * 128:(j + 1) * 128, n0:n0 + nb], sg[:, 0:nb], R=[sg])
                        else:
                            sg = rstg[ectr[0] % 3]
                            k.cp(eng, sg[:, 0:nb], pb[:, 0:nb], [pb], [sg])
                            k.dma("pool", self.PRW[j * 128:(j + 1) * 128, n0:n0 + nb], sg[:, 0:nb], R=[sg])
                        ectr[0] += 1
                    else:
                        k.act(uT[:, j, 0:nb], pb[:, 0:nb], AF.Gelu_apprx_tanh, [pb], [uT], acc=(j > 0))
                for tt_ in range(nb // 128):
                    ts_ = slice(tt_ * 128, (tt_ + 1) * 128)
                    pv = ptok[0]
                    pg = ptok[1 + tctr % 2]
                    gt, junk, vn, vnf, stat = gt_l[tctr % 2], junk_l[tctr % 2], vn_l[tctr % 2], vnf_l[tctr % 2], stat_l[tctr % 2]
                    tctr += 1
                    for kc in range(8):
                        k.mm(pv[:, 0:128], hT[:, kc, ts_], wA[:, kc, 640:768], kc == 0, kc == 7, [wA, hT], [pv])
                    for kc in range(8):
                        k.mm(pg[:, :], hT[:, kc, ts_], wA[:, kc, 3072:3584], kc == 0, kc == 7, [wA, hT], [pg])
                    vt = va[tt_ % 2]
                    k.cp("dve", vt[:, :, 0:64], pv[:, 0:128].rearrange("p (h d) -> p h d", h=2), [pv], [vt])
                    k.dma("pool", self.VA[n0 + tt_ * 128:n0 + (tt_ + 1) * 128, :], vt[:].rearrange("p h d -> p (h d)"), R=[vt])
                    k.act(gt[:], pg[:], AF.Gelu_apprx_tanh, [pg], [gt, stat], accum=stat[:, 0:1])
                    k.ts("dve", stat[:, 1:2], stat[:, 0:1], -1.0 / 512, ALU.mult, [stat], [stat])
                    k.act(junk[:], gt[:], AF.Square, [gt, stat], [junk, stat], bias=stat[:, 1:2], accum=stat[:, 2:3])
                    k.act(stat[:, 3:4], stat[:, 2:3], AF.Ln, [stat, self.epsD], [stat], bias=self.epsD[:], scale=1.0 / 512)
                    k.act(stat[:, 4:5], stat[:, 3:4], AF.Exp, [stat], [stat], scale=-0.5)
                    k.ts("dve", vnf[:], gt[:], stat[:, 1:2], ALU.add, [gt, stat], [vnf], s2=stat[:, 4:5], op1=ALU.mult)
                    k.tt("dve", vnf[:], vnf[:], lnw, ALU.mult, [vnf, self.smallp], [vnf])
                    k.tt("dve", vn[:], vnf[:], lnb, ALU.add, [vnf, self.smallp], [vn])
                    for g_ in range(4):
                        gs = slice(g_ * 128, (g_ + 1) * 128)
                        k.mm(pso[:, g_, :], vn[:, gs], WsT[:, g_, :], True, False, [vn, WsT], [pso], inc=False)
                        k.mm(pso[:, g_, :], ones_row, sgub[:, gs], False, True, [self.ones_f, self.smallp], [pso], inc=(g_ == 3))
                    k.tt("dve", ost[:, :, ts_], uT[:, :, ts_], pso[:], ALU.mult, [uT, pso], [ost], acc=(tt_ > 0))
                k.dma("pool", OSTv[:, :, n0:n0 + nb], ost[:, :, 0:nb], R=[ost])


    def phase_attn(self, l, last):
        k, nc = self.k, self.nc
        NTOK = self.NTOK
        NT = NTOK // 128
        nlat = self.T // 128
        OATv = self.OAT.rearrange("c p n -> p c n")
        with ExitStack() as st:
            Qs = k.sb(st, [128, 4, NTOK], BF16, "Qs")
            Ks = k.sb(st, [128, NTOK], BF16, "Ks")
            Vs = k.sb(st, [128, NT, 130], BF16, "Vs")
            QTv = self.QT.rearrange("j p n -> p j n")
            for j in range(4):
                k.dma("sp", Qs[:, j, :], QTv[:, j, :], W=[Qs], acc=(j > 0))
            k.dma("sp", Ks[:], self.KT, W=[Ks])
            k.dma("sp", Vs[:], self.VA.rearrange("(t p) f -> p t f", p=128), W=[Vs])
            mf = k.sb(st, [128, 2, 512], F32, "mf")
            k.dma("sp", mf[:, 0, :], self.cd["maskprev"], W=[mf])
            k.dma("sp", mf[:, 1, :], self.cd["masknext"], W=[mf], acc=True)
            mb = k.sb(st, [128, 2, 512], BF16, "mb")
            k.cp("dve", mb[:], mf[:], [mf], [mb])
            esink = k.sb(st, [128, 8], F32, "esink")
            k.act(esink[:], self.spc(l, "sink"), AF.Exp, [self.smallp], [esink])
            pS = [k.ps(st, [128, 512], F32, "pS") for _ in range(3)]
            pO = [[k.ps(st, [128, 4, 128], F32, "pO") for _ in range(2)] for _ in range(2)]
            pT = k.ps(st, [128, 4, 128], BF16, "pT")
            PT = [k.sb(st, [128, 512], BF16, "PT") for _ in range(3)]
            den = k.sb(st, [128, 8], F32, "den")
            oa = [k.sb(st, [128, 8, 64], BF16, "oa") for _ in range(2)]
            oat = [k.sb(st, [128, 4, 128], BF16, "oat") for _ in range(2)]
            sctr = 0
            pending = None
            qtiles = list(range(2, NT)) if last else list(range(NT))
            for qn_, qi in enumerate(qtiles):
                qs = slice(qi * 128, (qi + 1) * 128)
                if qi < 2:
                    kbs = [(0, None), (1, None)]
                else:
                    b = qi - 2
                    kbs = []
                    if b > 0:
                        kbs.append((qi - 1, 0))
                    kbs.append((qi, None))
                    if b < nlat - 1:
                        kbs.append((qi + 1, 1))
                    kbs += [(0, None), (1, None)]
                oat_ = oat[qn_ % 2]
                oa_ = oa[qn_ % 2]
                for g_ in range(2):
                    po = pO[qn_ % 2][g_]
                    gp = slice(g_ * 64, (g_ + 1) * 64)
                    for ki, (kb, msk) in enumerate(kbs):
                        ps_ = pS[sctr % 3]
                        pt_ = PT[sctr % 3]
                        sctr += 1
                        k.mm(ps_[:], Ks[gp, kb * 128:(kb + 1) * 128], Qs[gp, :, qs], True, msk is None, [Ks, Qs], [ps_])
                        if msk is not None:
                            k.mm(ps_[:], self.ident_b[:], mb[:, msk, :], False, True, [self.ident_b, mb], [ps_])
                        k.act(pt_[:], ps_[:], AF.Exp, [ps_], [pt_], scale=0.125)
                        for jh in range(4):
                            first = (ki == 0 and jh == 0)
                            lastmm = (ki == len(kbs) - 1 and jh == 3)
                            k.op("pe", lambda: nc.tensor.matmul(po[:, jh, 0:65], lhsT=pt_[:, jh * 128:(jh + 1) * 128],
                                                                rhs=Vs[:, kb, g_ * 65:(g_ + 1) * 65], start=first, stop=lastmm,
                                                                skip_group_check=True),
                                 [pt_, Vs], [po], inc=(jh == 3))
                    hs = slice(g_ * 4, (g_ + 1) * 4)
                    k.tt("dve", den[:, hs], po[:, :, 64], esink[:, hs], ALU.add, [po, esink], [den], acc=(g_ == 1))
                    k.recip(den[:, hs], den[:, hs], [den], [den])
                    k.tt("dve", oa_[:, hs, :], po[:, :, 0:64], den[:, hs].unsqueeze(2).broadcast_to([128, 4, 64]), ALU.mult,
                         [po, den], [oa_], acc=(g_ == 1))
                def finish_tile(oa__=oa_, oat__=oat_, qs_=qs):
                    for c_ in range(4):
                        k.tr(pT[:, c_, :], oa__[:, 2 * c_:2 * c_ + 2, :].rearrange("p h d -> p (h d)"), self.ident_b[:], [oa__, self.ident_b], [pT],
                             inc=(c_ == 3))
                    k.cp("act", oat__[:], pT[:], [pT], [oat__])
                    k.dma("pool", OATv[:, :, qs_], oat__[:], R=[oat__])
                if pending is not None:
                    pending()
                pending = finish_tile
            if pending is not None:
                pending()

    def phase_rwkv_prep(self, l):
        k, nc = self.k, self.nc
        NTOK = self.NTOK
        PB = 128
        CC = 0.6065306597126334
        PRWv = self.PRB.rearrange("(w c) n -> c w n", c=64)
        PWAv = self.PRW[1536:1664].rearrange("(w c) n -> c w n", c=64)
        PGv = self.PRW[1664:1792]
        HTv = self.HT.rearrange("k p n -> p k n")
        with ExitStack() as st:
            T_ = lambda nm, dt=F32, shp=(128, 8, PB): k.sb(st, list(shp), dt, nm)
            dg = k.sb(st, [64, 72, 2, 64], BF16, "dg")
            cw = self.spc(l, "conv_rkv", rows=64)
            idb = self.ident_f[0:64, 0:64].unsqueeze(1).broadcast_to([64, 72, 64])
            cwb = cw.unsqueeze(2).broadcast_to([64, 72, 64])
            for dd in range(2):
                k.tt("dve", dg[:, :, dd, :], idb, cwb, ALU.mult, [self.ident_f, self.smallp], [dg], acc=(dd == 1))
            dg2 = k.sb(st, [64, 6, 64], F32, "dg2")
            k.tt("dve", dg2[:], self.ident_f[0:64, 0:64].unsqueeze(1).broadcast_to([64, 6, 64]),
                 self.spc(l, "conv_wa", rows=64).unsqueeze(2).broadcast_to([64, 6, 64]), ALU.mult, [self.ident_f, self.smallp], [dg2])
            dg3 = k.sb(st, [128, 3, 128], F32, "dg3")
            k.tt("dve", dg3[:], self.ident_f[:].unsqueeze(1).broadcast_to([128, 3, 128]),
                 self.spc(l, "conv_g").unsqueeze(2).broadcast_to([128, 3, 128]), ALU.mult, [self.ident_f, self.smallp], [dg3])
            stg = k.sb(st, [128, 1024], F32, "lstg")
            wup = k.sb(st, [64, 8, 2, 64], BF16, "wup")
            aup = k.sb(st, [64, 8, 2, 64], BF16, "aup")
            for (dst, nm) in ((wup, "rwkv_w_up"), (aup, "rwkv_a_up")):
                for dd in range(2):
                    k.dma("sp", stg[0:64, dd * 512:(dd + 1) * 512], self.pd[nm][l, dd], W=[stg], acc=(dd == 1))
                k.cp("dve", dst[:].rearrange("r h d c -> r d h c"), stg[0:64, :].rearrange("r (d h c) -> r d h c", d=2, h=8), [stg], [dst])
            gup = k.sb(st, [128, 8, 64], BF16, "gup")
            k.dma("sp", stg[:, 0:512], self.pd["rwkv_g_up"][l], W=[stg])
            k.cp("dve", gup[:].rearrange("r h c -> r (h c)"), stg[:, 0:512], [stg], [gup])
            if l > 0:
                dwn = k.sb(st, [128, 8, 32], BF16, "dwn")
                k.dma("sp", stg[:, 0:256].rearrange("p (k r) -> p k r", k=8), self.pd["rwkv_vres_down"][l - 1].rearrange("(k p) r -> p k r", p=128), W=[stg])
                k.cp("dve", dwn[:].rearrange("p k r -> p (k r)"), stg[:, 0:256], [stg], [dwn])
                vup = k.sb(st, [32, 8, 2, 64], BF16, "vup")
                k.dma("sp", stg[0:32, 0:512], self.pd["rwkv_vres_up"][l - 1], W=[stg])
                for dd in range(2):
                    k.cp("dve", vup[:, :, dd, :], stg[0:32, 0:512].rearrange("r (h c) -> r h c", h=8), [stg], [vup], acc=(dd == 1))
            rsm = k.sb(st, [128, 8, PB], F32, "rsm")
            k.dma("sp", rsm[:], self.cd["scanreset"], W=[rsm])
            eps12 = k.sb(st, [128, 1], F32, "eps12")
            k.memset("dve", eps12[:], 1e-12, [eps12])
            hal = [k.sb(st, [64, 8, PB + 2], BF16, "hal") for _ in range(3)]
            hwa = k.sb(st, [64, 2, PB + 2], F32, "hwa")
            hg = k.sb(st, [128, PB + 2], F32, "hg")
            r_l, kx_l, v_l = [T_("r"), T_("r")], [T_("kx"), T_("kx")], [T_("v"), T_("v")]
            sig_l, asg_l = [T_("sig"), T_("sig")], [T_("asg"), T_("asg")]
            kA, kB = T_("kA"), T_("kB")
            P_, Q_, TP = T_("P"), T_("Q"), T_("TP")
            E = [T_("E") for _ in range(2)]
            kd, bb, tm, tm2 = T_("kd"), T_("bb"), T_("tm"), T_("tm2")
            wc = k.sb(st, [128, 2, 8], F32, "wc")
            xwt = k.sb(st, [64, PB], BF16, "xwt")
            xab = k.sb(st, [64, PB], BF16, "xab")
            sgx = k.sb(st, [128, PB], BF16, "sgx")
            hdb = k.sb(st, [32, PB], BF16, "hdb")
            hTb = k.sb(st, [128, 8, PB], BF16, "hTb")
            ob = [k.sb(st, [128, 8, PB], BF16, "ob") for _ in range(6)]
            vb_l = [k.sb(st, [64, 8, PB], BF16, "vb") for _ in range(2)]
            tokb = k.sb(st, [128, 8, 2, 128], BF16, "tokb")
            tokv = k.sb(st, [128, 8, 64], BF16, "tokv")
            gT = k.sb(st, [64, 8, PB], F32, "gT")
            bv = k.sb(st, [64, 8, PB], F32, "bv")
            pu = [k.ps(st, [128, 4, PB], F32, "pu") for _ in range(4)]
            ptr = [k.ps(st, [128, 8, 128], BF16, "ptr") for _ in range(2)]
            pm = k.ps(st, [128, 512], F32, "pm")
            ptv = k.ps(st, [128, 8, 64], BF16, "ptv")
            puc = [0]

            def nextpu():
                puc[0] += 1
                return pu[puc[0] % 4]

            bcast = lambda name: self.spc(l, name).unsqueeze(2).broadcast_to([128, 8, PB])
            nblk = NTOK // PB
            ectr = 0
            def front(bi):
                nonlocal ectr
                r_, kx, v_, sig, asg, vb = (x_[bi % 2] for x_ in (r_l, kx_l, v_l, sig_l, asg_l, vb_l))
                n0 = bi * PB
                lz = (n0 == 0 or n0 == LCTX)
                rz = (n0 + PB == LCTX or n0 + PB == NTOK)
                lo = 1 if lz else 0
                hi = PB + 1 if rz else PB + 2

                def load_halo(tile, src, a3):
                    first = True
                    if lz:
                        ap = tile[:, :, 0:1] if a3 else tile[:, 0:1]
                        k.op("pool", lambda: nc.gpsimd.memset(ap, 0.0), (), [tile])
                        first = False
                    if rz:
                        ap2 = tile[:, :, PB + 1:PB + 2] if a3 else tile[:, PB + 1:PB + 2]
                        k.op("pool", lambda: nc.gpsimd.memset(ap2, 0.0), (), [tile], acc=not first)
                        first = False
                    dst = tile[:, :, lo:hi] if a3 else tile[:, lo:hi]
                    k.dma("sp", dst, src, W=[tile], acc=not first)

                outs3 = (r_, kx, v_)
                for wch in range(3):
                    h_ = hal[wch]
                    load_halo(h_, PRWv[:, wch * 8:(wch + 1) * 8, n0 - 1 + lo:n0 - 1 + hi], True)
                    for half in range(2):
                        pb = nextpu()
                        for hh in range(4):
                            h = half * 4 + hh
                            w = wch * 8 + h
                            for tap in range(3):
                                k.mm(pb[:, hh, :], dg[:, tap * 24 + w, :, :].rearrange("p d c -> p (d c)"), h_[:, h, tap:tap + PB],
                                     tap == 0, tap == 2, [dg, h_], [pb], inc=(tap == 2 and hh == 3))
                        eng = "act" if ectr % 2 == 0 else "dve"
                        ectr += 1
                        k.cp(eng, outs3[wch][:, half * 4:(half + 1) * 4, :], pb[:], [pb], [outs3[wch]], acc=(half == 1))
                load_halo(hwa, PWAv[:, :, n0 - 1 + lo:n0 - 1 + hi], True)
                load_halo(hg, PGv[:, n0 - 1 + lo:n0 - 1 + hi], False)
                for wch in range(2):
                    for tap in range(3):
                        k.mm(pm[0:64, wch * PB:(wch + 1) * PB], dg2[:, tap * 2 + wch, :], hwa[:, wch, tap:tap + PB], tap == 0, tap == 2,
                             [dg2, hwa], [pm], inc=(tap == 2 and wch == 1))
                k.act(xwt[:], pm[0:64, 0:PB], AF.Tanh, [pm], [xwt])
                k.cp("act", xab[:], pm[0:64, PB:2 * PB], [pm], [xab])
                pbg = nextpu()
                for tap in range(3):
                    k.mm(pbg[:, 0, :], dg3[:, tap, :], hg[:, tap:tap + PB], tap == 0, tap == 2, [dg3, hg], [pbg])
                k.act(sgx[:], pbg[:, 0, :], AF.Sigmoid, [pbg], [sgx])
                for (dst, wt, src, bname) in ((sig, wup, xwt, "w0"), (asg, aup, xab, "a0")):
                    bcol = self.spc(l, bname)
                    for half in range(2):
                        pb = nextpu()
                        for hh in range(4):
                            h = half * 4 + hh
                            k.mm(pb[:, hh, :], wt[:, h, :, :].rearrange("r d c -> r (d c)"), src[:], True, True, [wt, src], [pb], inc=(hh == 3))
                        for hh in range(4):
                            h = half * 4 + hh
                            k.act(dst[:, h, :], pb[:, hh, :], AF.Sigmoid, [pb, self.smallp], [dst], bias=bcol[:, h:h + 1], acc=(h > 0))
                for half in range(2):
                    pb = nextpu()
                    for hh in range(4):
                        h = half * 4 + hh
                        k.mm(pb[0:64, hh, :], gup[:, h, :], sgx[:], True, True, [gup, sgx], [pb], inc=(hh == 3))
                    k.cp("act", gT[:, half * 4:(half + 1) * 4, :], pb[0:64, :, :], [pb], [gT], acc=(half == 1))
                k.dma("pool", self.RG[:, :, n0:n0 + PB], gT[:], R=[gT])
            def back(bi):
                r_, kx, v_, sig, asg, vb = (x_[bi % 2] for x_ in (r_l, kx_l, v_l, sig_l, asg_l, vb_l))
                n0 = bi * PB
                if l > 0:
                    k.dma("sp", hTb[:], HTv[:, :, n0:n0 + PB], W=[hTb])
                    for kc in range(8):
                        k.mm(pm[0:32, 3 * PB:4 * PB], dwn[:, kc, :], hTb[:, kc, :], kc == 0, kc == 7, [dwn, hTb], [pm])
                    k.cp("act", hdb[:], pm[0:32, 3 * PB:4 * PB], [pm], [hdb])
                    vb_col = self.spc(l, "vres_b")
                    for half in range(2):
                        pb = nextpu()
                        for hh in range(4):
                            h = half * 4 + hh
                            k.mm(pb[:, hh, :], vup[:, h, :, :].rearrange("r d c -> r (d c)"), hdb[:], True, True, [vup, hdb], [pb], inc=(hh == 3))
                        for hh in range(4):
                            h = half * 4 + hh
                            k.act(tm[:, h, :], pb[:, hh, :], AF.Sigmoid, [pb, self.smallp], [tm], bias=vb_col[:, h:h + 1], acc=(h > 0))
                    for dd in range(2):
                        k.dma("sp", tm2[dd * 64:(dd + 1) * 64], self.VF[:, :, n0:n0 + PB], W=[tm2], acc=(dd == 1))
                    k.tt("dve", tm2[:], tm2[:], v_[:], ALU.subtract, [tm2, v_], [tm2])
                    k.tt("dve", tm2[:], tm2[:], tm[:], ALU.mult, [tm2, tm], [tm2])
                    k.tt("dve", v_[:], v_[:], tm2[:], ALU.add, [v_, tm2], [v_])
                else:
                    k.dma("pool", self.VF[:, :, n0:n0 + PB], v_[0:64], R=[v_])
                k.cp("act", vb[:], v_[0:64], [v_], [vb])
                k.tt("dve", tm[:], kx[:], bcast("k_k"), ALU.mult, [kx, self.smallp], [tm])
                k.act(tm2[:], tm[:], AF.Square, [tm], [tm2])
                for half in range(2):
                    pb = nextpu()
                    for hh in range(4):
                        h = half * 4 + hh
                        k.mm(pb[:, hh, :], self.onesblk_f[:], tm2[:, h, :], True, True, [self.onesblk_f, tm2], [pb], inc=(hh == 3))
                    k.act(kA[:, half * 4:(half + 1) * 4, :], pb[:], AF.Ln, [pb, eps12], [kA], bias=eps12[:], acc=(half == 1))
                k.act(kA[:], kA[:], AF.Exp, [kA], [kA], scale=-0.5)
                kk = kB
                k.tt("dve", kk[:], tm[:], kA[:], ALU.mult, [tm, kA], [kk])
                k.tt("dve", kA[:], kx[:], bcast("k_a"), ALU.mult, [kx, self.smallp], [kA])
                k.tt("dve", kx[:], kx[:], kA[:], ALU.subtract, [kx, kA], [kx])
                k.tt("dve", kd[:], kA[:], asg[:], ALU.mult, [kA, asg], [kd])
                k.tt("dve", kd[:], kd[:], kx[:], ALU.add, [kd, kx], [kd])
                k.tt("dve", bb[:], kk[:], asg[:], ALU.mult, [kk, asg], [bb])
                k.tt("dve", tm[:], r_[:], kd[:], ALU.mult, [r_, kd], [tm])
                k.tt("dve", tm[:], tm[:], bcast("r_k"), ALU.mult, [tm, self.smallp], [tm])
                lnb_b = self.spc(l, "ln_b", rows=64).unsqueeze(2).broadcast_to([64, 8, PB])
                for half in range(2):
                    pb = nextpu()
                    for hh in range(4):
                        h = half * 4 + hh
                        k.mm(pb[0:64, hh, :], self.ones_f[:, 0:64], tm[:, h, :], True, True, [self.ones_f, tm], [pb], inc=(hh == 3))
                    k.tt("dve", bv[:, half * 4:(half + 1) * 4, :], pb[0:64, :, :], v_[0:64, half * 4:(half + 1) * 4, :], ALU.mult, [pb, v_], [bv], acc=(half == 1))
                k.tt("dve", bv[:], bv[:], lnb_b, ALU.add, [bv, self.smallp], [bv])
                k.dma("pool", self.RBV[:, :, n0:n0 + PB], bv[:], R=[bv])
                k.op("dve", lambda: nc.vector.tensor_tensor_scan(out=P_[:].rearrange("p h t -> p (h t)"), data0=rsm[:].rearrange("p h t -> p (h t)"),
                                                                 data1=sig[:].rearrange("p h t -> p (h t)"), initial=0.0, op0=ALU.mult, op1=ALU.add),
                     [rsm, sig], [P_])
                k.tt("dve", Q_[:], P_[:], sig[:], ALU.subtract, [P_, sig], [Q_])
                Pc = P_[:].rearrange("p h (j t) -> p h j t", t=64)
                tot = Pc[:, :, :, 63:64]
                totb = tot.broadcast_to([128, 8, 2, 64])
                k.act(wc[:].rearrange("p j h -> p h j"), P_[:].rearrange("p h (j t) -> p h j t", t=64)[:, :, :, 63], AF.Exp, [P_], [wc], scale=-CC)
                k.tt("dve", TP[:].rearrange("p h (j t) -> p h j t", t=64), totb, Pc, ALU.subtract, [P_], [TP])
                k.tt("dve", P_[64:128], TP[64:128], sig[64:128], ALU.add, [TP, sig], [P_])
                Ee, Ei = E[0], E[1]
                k.act(Ee[0:64], Q_[0:64], AF.Exp, [Q_], [Ee], scale=-CC)
                k.act(Ee[64:128], TP[64:128], AF.Exp, [TP], [Ee], scale=-CC, acc=True)
                k.act(Ei[:], P_[:], AF.Exp, [P_], [Ei], scale=-CC)
                k.stt(ob[0][:], kk[:], -1.0, Ee[:], ALU.mult, ALU.mult, [kk, Ee], [ob[0]])
                k.tt("dve", ob[1][:], r_[:], Ei[:], ALU.mult, [r_, Ei], [ob[1]])
                k.act(Ee[:], P_[:], AF.Exp, [P_], [Ee], scale=CC)
                k.tt("dve", ob[2][:], bb[:], Ee[:], ALU.mult, [bb, Ee], [ob[2]])
                k.tt("dve", ob[3][:], kd[:], Ee[:], ALU.mult, [kd, Ee], [ob[3]])
                k.act(Ei[0:64], TP[0:64], AF.Exp, [TP], [Ei], scale=-CC)
                k.act(Ei[64:128], Q_[64:128], AF.Exp, [Q_], [Ei], scale=-CC, acc=True)
                k.tt("dve", ob[4][:], bb[:], Ei[:], ALU.mult, [bb, Ei], [ob[4]])
                k.tt("dve", ob[5][:], kd[:], Ei[:], ALU.mult, [kd, Ei], [ob[5]])
                for q_ in range(4):
                    for dd in range(2):
                        k.dma("pool", self.RA4[dd][:, :, q_, n0:n0 + PB], ob[q_][dd * 64:(dd + 1) * 64], R=[ob[q_]])
                for dd in range(2):
                    for j in range(2):
                        k.dma("pool", self.RWC[dd][n0 // 64 + j], wc[dd * 64:(dd + 1) * 64, j, :], R=[wc])
                for q_ in range(2):
                    for h in range(8):
                        k.tr(ptr[q_][:, h, :], ob[4 + q_][:, h, :], self.ident_b[:], [ob[4 + q_], self.ident_b], [ptr[q_]], inc=(h == 7))
                    k.cp("act" if q_ == 0 else "dve", tokb[:, :, q_, :], ptr[q_][:], [ptr[q_]], [tokb], acc=(q_ == 1))
                for h in range(8):
                    k.tr(ptv[:, h, :], vb[:, h, :], self.ident_b[0:64, 0:64], [vb, self.ident_b], [ptv], inc=(h == 7))
                k.cp("act", tokv[:], ptv[:], [ptv], [tokv])
                k.dma("pool", self.TOKB[n0:n0 + PB].rearrange("n h q f -> n (h q f)"), tokb[:].rearrange("p h q f -> p (h q f)"), R=[tokb])
                k.dma("pool", self.TOKV[n0:n0 + PB].rearrange("n h f -> n (h f)"), tokv[:].rearrange("p h f -> p (h f)"), R=[tokv])


            front(0)
            for bi in range(nblk):
                if bi + 1 < nblk:
                    front(bi + 1)
                back(bi)

    def phase_rwkv_scan(self, l):
        k, nc = self.k, self.nc
        NTOK = self.NTOK
        NCH = NTOK // CH
        ncx = LCTX // CH
        order = [list(range(NCH)), list(range(ncx - 1, -1, -1)) + list(range(NCH - 1, ncx - 1, -1))]
        with ExitStack() as st:
            rm = k.sb(st, [64, 4, 64], F32, "rm")
            k.dma("sp", rm[:], self.cd["rmask"], W=[rm])
            class _Half:
                def __init__(self, tt_):
                    self.tt_ = tt_
                    self.tok = tt_.tok

                def __getitem__(self, key):
                    if not isinstance(key, tuple):
                        key = (key,)
                    assert key[0] == slice(None)
                    return self.tt_.t[(slice(0, 64),) + tuple(key[1:])]
            B = [_Half(k.ps(st, [128, 8, 64], F32, "B%d" % i)) for i in range(8)]
            B01 = None
            D_ = []
            for d_ in range(2):
                o = {}
                o["arq"] = [k.sb(st, [64, 8, 4, 64], BF16, "arq") for _ in range(2)]
                o["bh"] = [k.sb(st, [64, 8, 64], BF16, "bh") for _ in range(2)]
                o["kh"] = [k.sb(st, [64, 8, 64], BF16, "kh") for _ in range(2)]
                o["vt"] = [k.sb(st, [64, 8, 64], BF16, "vt") for _ in range(2)]
                o["wc"] = [k.sb(st, [64, 8], F32, "wcl") for _ in range(2)]
                o["NL"] = k.sb(st, [64, 8, 2, 64], BF16, "NL")
                o["KL"] = k.sb(st, [64, 8, 2, 64], BF16, "KL")
                o["Nb"] = [k.sb(st, [64, 8, 64], BF16, "Nb") for _ in range(2)]
                o["Pb"] = [k.sb(st, [64, 8, 64], BF16, "Pb") for _ in range(2)]
                o["TTf"] = k.sb(st, [64, 8, 64], F32, "TTf")
                o["TTb"] = [k.sb(st, [64, 8, 64], BF16, "TTb") for _ in range(2)]
                o["Xb"] = k.sb(st, [64, 8, 64], BF16, "Xb")
                o["Ub"] = k.sb(st, [64, 8, 64], BF16, "Ub")
                o["Ys"] = [k.sb(st, [64, 8, 64], F32, "Ys") for _ in range(2)]
                o["STf"] = k.sb(st, [64, 8, 64], F32, "STf")
                o["STb"] = k.sb(st, [64, 8, 64], BF16, "STb")
                k.memset("dve", o["STf"][:], 0.0, [o["STf"]])
                k.memset("dve", o["STb"][:], 0.0, [o["STb"]])
                D_.append(o)
            idb64 = self.ident_f[0:64, 0:64].unsqueeze(1).broadcast_to([64, 8, 64])

            def load(d_, i):
                o = D_[d_]
                ck = order[d_][i]
                n0 = ck * CH
                s = i % 2
                k.dma("sp", o["arq"][s][:].rearrange("c h q t -> c (h q) t"),
                      self.RA4[d_][:, :, :, n0:n0 + CH].rearrange("c h q t -> c (h q) t"), W=[o["arq"][s]])
                k.dma("sp", o["bh"][s][:], self.TOKB[n0:n0 + CH, :, 0, d_ * 64:(d_ + 1) * 64], W=[o["bh"][s]])
                k.dma("sp", o["kh"][s][:], self.TOKB[n0:n0 + CH, :, 1, d_ * 64:(d_ + 1) * 64], W=[o["kh"][s]])
                k.dma("sp", o["vt"][s][:], self.TOKV[n0:n0 + CH], W=[o["vt"][s]])
                k.dma("sp", o["wc"][s][:], self.RWC[d_][ck], W=[o["wc"][s]])

            for d_ in range(2):
                load(d_, 0)
            for i in range(NCH):
                if i + 1 < NCH:
                    for d_ in range(2):
                        load(d_, i + 1)
                s = i % 2
                for d_ in range(2):
                    o = D_[d_]
                    arq = o["arq"][s]
                    mp = rm[:, 0:2, :] if d_ == 0 else rm[:, 2:4, :]
                    mA = rm[:, 2, :] if d_ == 0 else rm[:, 0, :]
                    mpb = mp.unsqueeze(1).broadcast_to([64, 8, 2, 64])
                    mAb = mA.unsqueeze(1).broadcast_to([64, 8, 64])
                    for h in range(8):
                        bn = B[0] if h < 4 else B[1]
                        k.mm(bn[:, 2 * (h % 4):2 * (h % 4) + 2, :], arq[:, h, 2, :], arq[:, h, 0:2, :], True, True, [arq], [bn], inc=(h % 4 == 3))
                    for h in range(8):
                        kn = B[2] if h < 4 else B[3]
                        k.mm(kn[:, 2 * (h % 4):2 * (h % 4) + 2, :], arq[:, h, 3, :], arq[:, h, 0:2, :], True, True, [arq], [kn], inc=(h % 4 == 3))
                    for h in range(8):
                        k.mm(B[4][:, h, :], arq[:, h, 0, :], arq[:, h, 2, :], True, True, [arq], [B[4]], inc=(h == 7))
                    for hf in range(2):
                        hs = slice(hf * 4, (hf + 1) * 4)
                        k.tt("dve", o["NL"][:, hs, :, :], B[hf][:].rearrange("s (h q) t -> s h q t", q=2), mpb[:, hs], ALU.mult, [B[hf], rm], [o["NL"]], acc=(hf == 1))
                        k.tt("dve", o["KL"][:, hs, :, :], B[2 + hf][:].rearrange("s (h q) t -> s h q t", q=2), mpb[:, hs], ALU.mult, [B[2 + hf], rm], [o["KL"]], acc=(hf == 1))
                    k.tt("dve", o["Pb"][0][:], B[4][:], mAb, ALU.mult, [B[4], rm], [o["Pb"][0]])
                    k.tt("dve", o["TTf"][:], o["NL"][:, :, 0, :], idb64, ALU.add, [o["NL"], self.ident_f], [o["TTf"]])
                    k.tt("dve", o["TTb"][0][:], o["NL"][:, :, 0, :], idb64, ALU.add, [o["NL"], self.ident_f], [o["TTb"][0]])
                for j in range(1, 6):
                    cur, prv = j % 2, (j - 1) % 2
                    for d_ in range(2):
                        o = D_[d_]
                        pN, pP, pT = B[3 * d_], B[3 * d_ + 1], B[3 * d_ + 2]
                        Nprev = (lambda h: o["NL"][:, h, 0, :]) if j == 1 else (lambda h: o["Nb"][prv][:, h, :])
                        Ntok = o["NL"] if j == 1 else o["Nb"][prv]
                        Pprev = o["Pb"][prv]
                        if j < 5:
                            for h in range(8):
                                k.mm(pN[:, h, :], Pprev[:, h, :], Nprev(h), True, True, [Pprev, Ntok], [pN], inc=(h == 7))
                        for h in range(8):
                            k.mm(pP[:, h, :], Nprev(h), Pprev[:, h, :], True, True, [Pprev, Ntok], [pP], inc=(h == 7))
                    for d_ in range(2):
                        o = D_[d_]
                        pN, pP, pT = B[3 * d_], B[3 * d_ + 1], B[3 * d_ + 2]
                        if j < 5:
                            k.cp("act", o["Nb"][cur][:], pN[:], [pN], [o["Nb"][cur]])
                        k.cp("act", o["Pb"][cur][:], pP[:], [pP], [o["Pb"][cur]])
                    for d_ in range(2):
                        o = D_[d_]
                        pT = B[3 * d_ + 2]
                        for h in range(8):
                            k.mm(pT[:, h, :], o["Pb"][cur][:, h, :], o["TTb"][prv][:, h, :], True, True, [o["Pb"][cur], o["TTb"][prv]], [pT], inc=(h == 7))
                    for d_ in range(2):
                        o = D_[d_]
                        pT = B[3 * d_ + 2]
                        k.tt("dve", o["TTf"][:], o["TTf"][:], pT[:], ALU.add, [o["TTf"], pT], [o["TTf"]])
                        k.cp("act", o["TTb"][cur][:], o["TTf"][:], [o["TTf"]], [o["TTb"][cur]])
                TTfin = 5 % 2
                for d_ in range(2):
                    o = D_[d_]
                    pX = B[4 * d_]
                    arq = o["arq"][s]
                    for h in range(8):
                        k.mm(pX[:, h, :], arq[:, h, 0, :], o["STb"][:, h, :], True, False, [arq, o["STb"]], [pX], inc=False)
                        k.mm(pX[:, h, :], o["KL"][:, h, 0, :], o["vt"][s][:, h, :], False, True, [o["KL"], o["vt"][s]], [pX], inc=(h == 7))
                for d_ in range(2):
                    o = D_[d_]
                    k.cp("act", o["Xb"][:], B[4 * d_][:], [B[4 * d_]], [o["Xb"]])
                for d_ in range(2):
                    o = D_[d_]
                    pU = B[4 * d_ + 1]
                    for h in range(8):
                        k.mm(pU[:, h, :], o["TTb"][TTfin][:, h, :], o["Xb"][:, h, :], True, True, [o["TTb"][TTfin], o["Xb"]], [pU], inc=(h == 7))
                for d_ in range(2):
                    o = D_[d_]
                    k.cp("dve", o["Ub"][:], B[4 * d_ + 1][:], [B[4 * d_ + 1]], [o["Ub"]])
                for d_ in range(2):
                    o = D_[d_]
                    pY, pS = B[4 * d_ + 2], B[4 * d_ + 3]
                    arq = o["arq"][s]
                    for h in range(8):
                        k.mm(pS[:, h, :], o["bh"][s][:, h, :], o["Ub"][:, h, :], True, False, [o["bh"][s], o["Ub"]], [pS], inc=False)
                        k.mm(pS[:, h, :], o["kh"][s][:, h, :], o["vt"][s][:, h, :], False, True, [o["kh"][s], o["vt"][s]], [pS], inc=(h == 7))
                    for h in range(8):
                        k.mm(pY[:, h, :], o["STb"][:, h, :], arq[:, h, 1, :], True, False, [o["STb"], arq], [pY], inc=False)
                        k.mm(pY[:, h, :], o["Ub"][:, h, :], o["NL"][:, h, 1, :], False, False, [o["Ub"], o["NL"]], [pY], inc=False)
                        k.mm(pY[:, h, :], o["vt"][s][:, h, :], o["KL"][:, h, 1, :], False, True, [o["vt"][s], o["KL"]], [pY], inc=(h == 7))
                for d_ in range(2):
                    o = D_[d_]
                    pY, pS = B[4 * d_ + 2], B[4 * d_ + 3]
                    ck = order[d_][i]
                    k.tt("dve", o["STf"][:], o["STf"][:], o["wc"][s][:].unsqueeze(2).broadcast_to([64, 8, 64]), ALU.mult, [o["STf"], o["wc"][s]], [o["STf"]])
                    k.tt("dve", o["STf"][:], o["STf"][:], pS[:], ALU.add, [o["STf"], pS], [o["STf"]])
                    k.cp("act", o["STb"][:], o["STf"][:], [o["STf"]], [o["STb"]])
                    ys = o["Ys"][s]
                    k.cp("act", ys[:], pY[:], [pY], [ys])
                    k.dma("pool", self.YD[d_][:, :, ck * CH:(ck + 1) * CH], ys[:], R=[ys])

    def phase_rwkv_post(self, l):
        k, nc = self.k, self.nc
        NTOK = self.NTOK
        PB = 256
        with ExitStack() as st:
            T_ = lambda nm, dt=F32: k.sb(st, [64, 8, PB], dt, nm)
            y0, y1, yc, sq, rs_, tmp, bvt, gt = T_("y0"), T_("y1"), T_("yc"), T_("sq"), T_("rs"), T_("tmp"), T_("bvt"), T_("gt")
            ob = [T_("orw", BF16) for _ in range(2)]
            pu_full = [k.ps(st, [128, 4, PB], F32, "ppu") for _ in range(4)]

            class _H2:
                def __init__(self, tt_):
                    self.tt_ = tt_
                    self.tok = tt_.tok

                def __getitem__(self, key):
                    if not isinstance(key, tuple):
                        key = (key,)
                    return self.tt_.t[(slice(0, 64),) + tuple(key[1:])]
            pu = [_H2(x_) for x_ in pu_full]
            ones64 = self.ones_f[0:64, 0:64]
            lnw = self.spc(l, "ln_w", rows=64)
            for bi in range(NTOK // PB):
                n0 = bi * PB
                k.dma("sp", y0[:], self.YD[0][:, :, n0:n0 + PB], W=[y0])
                k.dma("sp", y1[:], self.YD[1][:, :, n0:n0 + PB], W=[y1])
                k.dma("sp", bvt[:], self.RBV[:, :, n0:n0 + PB], W=[bvt])
                k.dma("sp", gt[:], self.RG[:, :, n0:n0 + PB], W=[gt])
                k.tt("dve", y0[:], y0[:], y1[:], ALU.add, [y0, y1], [y0])
                for hf in range(2):
                    hs = slice(hf * 4, (hf + 1) * 4)
                    pb = pu[hf]
                    for hh in range(4):
                        k.mm(pb[:, hh, :], ones64, y0[:, hf * 4 + hh, :], True, True, [self.ones_f, y0], [pb], inc=(hh == 3))
                    k.stt(yc[:, hs, :], pb[:], -1.0 / 64, y0[:, hs, :], ALU.mult, ALU.add, [pb, y0], [yc], acc=(hf == 1))
                k.act(sq[:], yc[:], AF.Square, [yc], [sq])
                for hf in range(2):
                    hs = slice(hf * 4, (hf + 1) * 4)
                    pb = pu[2 + hf]
                    for hh in range(4):
                        k.mm(pb[:, hh, :], ones64, sq[:, hf * 4 + hh, :], True, True, [self.ones_f, sq], [pb], inc=(hh == 3))
                    k.act(rs_[:, hs, :], pb[:], AF.Ln, [pb, self.epsG], [rs_], bias=self.epsG[0:64], scale=1.0 / 64, acc=(hf == 1))
                k.act(rs_[:], rs_[:], AF.Exp, [rs_], [rs_], scale=-0.5)
                for h in range(8):
                    k.stt(tmp[:, h, :], yc[:, h, :], lnw[:, h:h + 1], rs_[:, h, :], ALU.mult, ALU.mult, [yc, rs_, self.smallp], [tmp], acc=(h > 0))
                k.tt("dve", tmp[:], tmp[:], bvt[:], ALU.add, [tmp, bvt], [tmp])
                o_ = ob[bi % 2]
                k.tt("dve", o_[:], tmp[:], gt[:], ALU.mult, [tmp, gt], [o_])
                k.dma("pool", self.ORT[:, :, n0:n0 + PB], o_[:], R=[o_])

    def phase_merge(self, l, last):
        k, nc = self.k, self.nc
        XTv = self.XT.rearrange("k p n -> p k n")
        HTv = self.HT.rearrange("k p n -> p k n")
        OATv = self.OAT.rearrange("c p n -> p c n")
        OSTv = self.OST.rearrange("g p n -> p g n")
        with ExitStack() as st:
            wG = k.sb(st, [128, 8, 3072], BF16, "wG")
            for kc in range(8):
                k.dma("sp", wG[:, kc, :], self.WIN[l][kc * 128:(kc + 1) * 128, 3584:6656], W=[wG], acc=(kc > 0))
            wOA = k.sb(st, [128, 4, D], BF16, "wOA")
            wOS = k.sb(st, [128, 4, D], BF16, "wOS")
            wOR = k.sb(st, [64, 8, D], BF16, "wOR")
            wOU = k.sb(st, [128, 8, D], BF16, "wOU")
            k.dma("sp", wOA[:], self.WOA[l].rearrange("(k p) f -> p k f", p=128), W=[wOA])
            k.dma("sp", wOS[:], self.WOS[l].rearrange("(k p) f -> p k f", p=128), W=[wOS])
            k.dma("sp", wOR[:], self.WOR[l].rearrange("(h c) f -> c h f", c=64), W=[wOR])
            k.dma("sp", wOU[:], self.WOUT[l].rearrange("(k p) f -> p k f", p=128), W=[wOU])
            NB = 512
            xT = k.sb(st, [128, 8, NB], F32, "mxT")
            hT = [k.sb(st, [128, 8, NB], BF16, "mhT") for _ in range(2)]
            oaT = [k.sb(st, [128, 4, NB], BF16, "moa") for _ in range(2)]
            osT = [k.sb(st, [128, 4, NB], BF16, "mos") for _ in range(2)]
            orT = [k.sb(st, [64, 8, NB], BF16, "mor") for _ in range(2)]
            mT = k.sb(st, [128, 8, NB], BF16, "mmT")
            gs = [k.sb(st, [128, NB], F32, "mgs") for _ in range(3)]
            acc_ = k.sb(st, [128, NB], F32, "macc")
            tmp = k.sb(st, [128, NB], F32, "mtmp")
            pg = [k.ps(st, [128, NB], F32, "mpg") for _ in range(3)]
            pp = [k.ps(st, [128, NB], F32, "mpp") for _ in range(3)]
            px = [k.ps(st, [128, NB], F32, "mpx") for _ in range(2)]
            blocks = [b for b in self.blocks if not (last and b[2])]
            for bi, blk in enumerate(blocks):
                n0, nb, is_ctx, t0 = blk
                m = 1 if is_ctx else 0
                s = bi % 2
                k.dma("sp", hT[s][:, :, 0:nb], HTv[:, :, n0:n0 + nb], W=[hT[s]])
                k.dma("sp", oaT[s][:, :, 0:nb], OATv[:, :, n0:n0 + nb], W=[oaT[s]])
                k.dma("sp", osT[s][:, :, 0:nb], OSTv[:, :, n0:n0 + nb], W=[osT[s]])
                k.dma("sp", orT[s][:, :, 0:nb], self.ORT[:, :, n0:n0 + nb], W=[orT[s]])
                k.dma("sp", xT[:, :, 0:nb], XTv[:, :, n0:n0 + nb], W=[xT])
                for fc in range(8):
                    fs = slice(fc * 128, (fc + 1) * 128)
                    for b in range(3):
                        for kc in range(8):
                            k.mm(pg[b][:, 0:nb], wG[:, kc, b * 1024 + fc * 128:b * 1024 + (fc + 1) * 128], hT[s][:, kc, 0:nb], kc == 0, kc == 7, [wG, hT[s]], [pg[b]])
                        k.act(gs[b][:, 0:nb], pg[b][:, 0:nb], AF.Sigmoid, [pg[b]], [gs[b]])
                    for kc in range(4):
                        k.mm(pp[0][:, 0:nb], wOA[:, kc, fs], oaT[s][:, kc, 0:nb], kc == 0, kc == 3, [wOA, oaT[s]], [pp[0]])
                    for h in range(8):
                        k.mm(pp[1][:, 0:nb], wOR[:, h, fs], orT[s][:, h, 0:nb], h == 0, h == 7, [wOR, orT[s]], [pp[1]])
                    for kc in range(4):
                        k.mm(pp[2][:, 0:nb], wOS[:, kc, fs], osT[s][:, kc, 0:nb], kc == 0, kc == 3, [wOS, osT[s]], [pp[2]])
                    k.tt("dve", acc_[:, 0:nb], gs[0][:, 0:nb], pp[0][:, 0:nb], ALU.mult, [gs[0], pp[0]], [acc_])
                    k.tt("dve", tmp[:, 0:nb], gs[1][:, 0:nb], pp[1][:, 0:nb], ALU.mult, [gs[1], pp[1]], [tmp])
                    k.tt("dve", acc_[:, 0:nb], acc_[:, 0:nb], tmp[:, 0:nb], ALU.add, [acc_, tmp], [acc_])
                    k.tt("dve", tmp[:, 0:nb], gs[2][:, 0:nb], pp[2][:, 0:nb], ALU.mult, [gs[2], pp[2]], [tmp])
                    k.tt("dve", mT[:, fc, 0:nb], acc_[:, 0:nb], tmp[:, 0:nb], ALU.add, [acc_, tmp], [mT], acc=(fc > 0))
                for fc in range(8):
                    fs = slice(fc * 128, (fc + 1) * 128)
                    p_ = px[fc % 2]
                    for kc in range(8):
                        k.mm(p_[:, 0:nb], wOU[:, kc, fs], mT[:, kc, 0:nb], kc == 0, kc == 7, [wOU, mT], [p_])
                    k.stt(xT[:, fc, 0:nb], p_[:, 0:nb], self.mods[:, m, 2, fc:fc + 1], xT[:, fc, 0:nb], ALU.mult, ALU.add, [p_, self.mods, xT], [xT])
                k.dma("pool", XTv[:, :, n0:n0 + nb], xT[:, :, 0:nb], R=[xT])

    def phase_ffn(self, l, last):
        k, nc = self.k, self.nc
        XTv = self.XT.rearrange("k p n -> p k n")
        NB = 256
        with ExitStack() as st:
            w1 = k.sb(st, [128, 8, FFN], BF16, "w1")
            w3 = k.sb(st, [128, 8, FFN], BF16, "w3")
            w2 = k.sb(st, [128, NFC, D], BF16, "w2")
            for kc in range(8):
                k.dma("sp", w1[:, kc, :], self.W1[l][kc * 128:(kc + 1) * 128, :], W=[w1], acc=(kc > 0))
                k.dma("sp", w3[:, kc, :], self.W3[l][kc * 128:(kc + 1) * 128, :], W=[w3], acc=(kc > 0))
            for hc in range(NFC):
                k.dma("sp", w2[:, hc, :], self.W2[l][hc * 128:(hc + 1) * 128, :], W=[w2], acc=(hc > 0))
            xTs = [k.sb(st, [128, 8, NB], F32, "fxT") for _ in range(2)]
            work = k.sb(st, [128, 8, NB], F32, "fwork")
            hTs = [k.sb(st, [128, 8, NB], BF16, "fhT") for _ in range(2)]
            rstd = k.sb(st, [128, NB], F32, "frstd")
            hid = k.sb(st, [128, NFC, NB], BF16, "fhid")
            sl = [k.sb(st, [128, NB], F32, "fsl") for _ in range(2)]
            pss = k.ps(st, [128, 512], F32, "fpss")
            pa = [k.ps(st, [128, 512], F32, "fpa") for _ in range(2)]
            pb_ = [k.ps(st, [128, 512], F32, "fpb") for _ in range(2)]
            po = [k.ps(st, [128, 512], F32, "fpo") for _ in range(2)]
            if last:
                ptr = k.ps(st, [128, 512], F32, "fptr")
                xo = [k.sb(st, [128, D], F32, "fxo") for _ in range(1)]
            nblk = self.NTOK // NB
            blist = [bi for bi in range(nblk) if not (last and bi * NB < LCTX)]

            def do_norm(ii):
                bi_ = blist[ii]
                n0_ = bi_ * NB
                xT_ = xTs[ii % 2]
                k.dma("sp", xT_[:], XTv[:, :, n0_:n0_ + NB], W=[xT_])
                self.norm_mod((xT_, work, hTs[ii % 2], rstd, pss), (n0_, NB, n0_ < LCTX, 0), l, 1)

            do_norm(0)
            for ii, bi in enumerate(blist):
                n0 = bi * NB
                is_ctx = n0 < LCTX
                m = 1 if is_ctx else 0
                xT = xTs[ii % 2]
                hT = hTs[ii % 2]
                if ii + 1 < len(blist):
                    do_norm(ii + 1)
                for hc in range(NFC):
                    hs = slice(hc * 128, (hc + 1) * 128)
                    a_, b_ = pa[hc % 2], pb_[hc % 2]
                    for kc in range(8):
                        k.mm(a_[:, 0:NB], w1[:, kc, hs], hT[:, kc, :], kc == 0, kc == 7, [w1, hT], [a_])
                    for kc in range(8):
                        k.mm(b_[:, 0:NB], w3[:, kc, hs], hT[:, kc, :], kc == 0, kc == 7, [w3, hT], [b_])
                    s_ = sl[hc % 2]
                    k.act(s_[:], a_[:, 0:NB], AF.Silu, [a_], [s_])
                    k.tt("dve", hid[:, hc, :], s_[:], b_[:, 0:NB], ALU.mult, [s_, b_], [hid], acc=(hc > 0))
                for fc in range(8):
                    fs = slice(fc * 128, (fc + 1) * 128)
                    p_ = po[fc % 2]
                    for hc in range(NFC):
                        k.mm(p_[:, 0:NB], w2[:, hc, fs], hid[:, hc, :], hc == 0, hc == NFC - 1, [w2, hid], [p_])
                    k.stt(xT[:, fc, :], p_[:, 0:NB], self.mods[:, m, 5, fc:fc + 1], xT[:, fc, :], ALU.mult, ALU.add, [p_, self.mods, xT], [xT])
                if not last:
                    k.dma("pool", XTv[:, :, n0:n0 + NB], xT[:], R=[xT])
                else:
                    for tt_ in range(NB // 128):
                        xo_ = xo[0]
                        for half in range(2):
                            for j in range(4):
                                kc = half * 4 + j
                                k.tr(ptr[:, j * 128:(j + 1) * 128], xT[:, kc, tt_ * 128:(tt_ + 1) * 128], self.ident_f[:], [xT, self.ident_f], [ptr], inc=(j == 3))
                            k.cp("act" if half == 0 else "dve", xo_[:, half * 512:(half + 1) * 512], ptr[:], [ptr], [xo_], acc=(half == 1))
                        t0 = n0 - LCTX + tt_ * 128
                        k.dma("pool", self.out[t0:t0 + 128, :], xo_[:], R=[xo_])


_SMALL_NAMES = ("b_mod", "norm_mix", "norm_ffn", "q_gain", "k_gain", "attn_sink", "rwkv_conv", "rwkv_w0", "rwkv_a0",
                "rwkv_k_k", "rwkv_k_a", "rwkv_r_k", "rwkv_ln_w", "rwkv_ln_b", "rwkv_vres_b", "sgu_ln_w", "sgu_ln_b", "sgu_b")


def make_in_maps(inputs, T, ncores):
    inp = {k_: np.asarray(v) for k_, v in inputs.items()}
    hc = host_consts(T)
    smallp = np.stack([pack_small(inp, l) for l in range(DEPTH)]).astype(np.float32)
    shared = {"smallp": smallp}
    for name, arr in hc.items():
        shared["c_" + name] = np.ascontiguousarray(arr, dtype=np.float32)
    for name, shape in PARAMS:
        if name in _SMALL_NAMES:
            continue
        shared[name] = np.ascontiguousarray(inp[name], dtype=np.float32)
    maps = []
    for b in range(ncores):
        m = dict(shared)
        m["x"] = np.ascontiguousarray(inp["x"][b, :T], dtype=np.float32)
        m["ctx"] = np.ascontiguousarray(inp["ctx"][b], dtype=np.float32)
        cf = np.concatenate([inp["c"][b].reshape(8, 128).T, inp["c_ctx"].reshape(8, 128).T], axis=1)
        m["cfm"] = np.ascontiguousarray(cf, dtype=np.float32)
        maps.append(m)
    return maps


def kernel(**inputs):
    T = 4096
    B = 8
    prog = Prog(T)
    nc = prog.build()
    maps = make_in_maps(inputs, T, B)
    res = run_bass_kernel_spmd(nc, maps, core_ids=list(range(B)))
    out = np.stack([np.asarray(res.results[b]["out"], dtype=np.float32) for b in range(B)])
    return out
```

```python
import numpy as np
import ml_dtypes
from contextlib import ExitStack
import concourse.bass as bass
import concourse.mybir as mybir
from concourse.bass_utils import run_bass_kernel_spmd

F32 = mybir.dt.float32
BF16 = mybir.dt.bfloat16
AF = mybir.ActivationFunctionType
ALU = mybir.AluOpType
AX = mybir.AxisListType

D = 1024
DEPTH = 2
LCTX = 256
NH = 8
HD = 64
IN_DIM = 6656
FFN = 2816
NFC = FFN // 128
EPS = 1e-6
GN_EPS = 64e-5
CH = 64
NEG = -30000.0


class Tok:
    __slots__ = ("w", "r", "pw", "pr")

    def __init__(self):
        self.w = []
        self.r = {}
        self.pw = []
        self.pr = {}


class TT:
    def __init__(self, t):
        self.t = t
        self.tok = Tok()

    def __getitem__(self, k):
        return self.t[k]


class KB:
    def __init__(self, nc, es):
        self.nc = nc
        self.es = es
        self.E = dict(pe=nc.tensor, dve=nc.vector, act=nc.scalar, pool=nc.gpsimd, sp=nc.sync)
        self.semobj = {}
        self.cnt = {}
        for k in self.E:
            self.semobj[k] = es.enter_context(nc.semaphore("c_" + k))
            self.cnt[k] = 0
        self.known = {k: {} for k in self.E}
        self.NDS = 8
        self.dq = {}
        for q in ("sp", "pool", "act"):
            names = []
            for i in range(self.NDS):
                nm = "d_%s%d" % (q, i)
                self.semobj[nm] = es.enter_context(nc.semaphore(nm))
                self.cnt[nm] = 0
                names.append(nm)
            self.dq[q] = [names, 0]
        self.uid = 0

    def sb(self, st, shape, dt, name=None):
        self.uid += 1
        return TT(st.enter_context(self.nc.sbuf_tensor("%s_%d" % (name or "t", self.uid), list(shape), dt)))

    def ps(self, st, shape, dt=F32, name=None):
        self.uid += 1
        return TT(st.enter_context(self.nc.psum_tensor("%s_%d" % (name or "p", self.uid), list(shape), dt)))

    def _wait(self, eng, ev):
        s, v = ev
        if v <= 0:
            return
        if self.known[eng].get(s, 0) >= v:
            return
        self.E[eng].wait_ge(self.semobj[s], v)
        self.known[eng][s] = v

    def _deps(self, eng, R, W, acc=False):
        for t in R:
            tk = t.tok if hasattr(t, 'tok') else t
            for ev in tk.w:
                if not (eng == "pe" and ev[0] == "pe"):
                    self._wait(eng, ev)
        for t in W:
            tk = t.tok if hasattr(t, 'tok') else t
            for ev in (tk.pw if acc else tk.w):
                if not (eng == "pe" and ev[0] == "pe"):
                    self._wait(eng, ev)
            rr = list(tk.r.items()) + (list(tk.pr.items()) if acc else [])
            for s, v in rr:
                if eng == "pe" and s == "pe":
                    continue
                self._wait(eng, (s, v))

    def _record(self, ev, R, W, acc=False):
        for t in R:
            tk = t.tok if hasattr(t, 'tok') else t
            if tk.r.get(ev[0], 0) < ev[1]:
                tk.r[ev[0]] = ev[1]
        for t in W:
            tk = t.tok if hasattr(t, 'tok') else t
            if acc:
                tk.w = [e for e in tk.w if e[0] != ev[0]] + [ev]
            else:
                tk.pw = tk.w
                tk.pr = tk.r
                tk.w = [ev]
                tk.r = {}

    def op(self, eng, fn, R=(), W=(), inc=True, acc=False):
        self._deps(eng, R, W, acc)
        ins = fn()
        ev = (eng, self.cnt[eng] + 1)
        if inc:
            ins.then_inc(self.semobj[eng], 1)
            self.cnt[eng] += 1
        self._record(ev, R, W, acc)
        return ins

    def dma(self, q, out, in_, R=(), W=(), acc=False, **kw):
        self._deps(q, R, W, acc)
        names, i = self.dq[q]
        nm = names[i % self.NDS]
        self.dq[q][1] = i + 1
        self._wait(q, (nm, self.cnt[nm]))
        self.E[q].dma_start(out=out, in_=in_, **kw).then_inc(self.semobj[nm], 16)
        self.cnt[nm] += 16
        self._record((nm, self.cnt[nm]), R, W, acc)

    def barrier(self):
        for e in self.E:
            for s, v in self.cnt.items():
                if s == e and e != "pe":
                    pass
                self._wait(e, (s, v))

    def mm(self, out, lhsT, rhs, start, stop, R, W, inc=None):
        if inc is None:
            inc = stop
        return self.op("pe", lambda: self.nc.tensor.matmul(out, lhsT=lhsT, rhs=rhs, start=start, stop=stop), R, W, inc)

    def tr(self, out, in_, ident, R, W, inc=True):
        return self.op("pe", lambda: self.nc.tensor.transpose(out, in_, ident), R, W, inc)

    def act(self, out, in_, func, R, W, bias=None, scale=None, accum=None, acc=False):
        kw = {}
        if bias is not None:
            kw["bias"] = bias
        if scale is not None:
            kw["scale"] = scale
        if accum is not None:
            kw["accum_out"] = accum
        return self.op("act", lambda: self.nc.scalar.activation(out=out, in_=in_, func=func, **kw), R, W, acc=acc)

    def tt(self, eng, out, in0, in1, op, R, W, acc=False):
        e = self.E[eng]
        return self.op(eng, lambda: e.tensor_tensor(out=out, in0=in0, in1=in1, op=op), R, W, acc=acc)

    def ts(self, eng, out, in0, s1, op0, R, W, s2=None, op1=None, accum=None, acc=False):
        e = self.E[eng]
        kw = {}
        if op1 is not None:
            kw["op1"] = op1
        if accum is not None:
            kw["accum_out"] = accum
        return self.op(eng, lambda: e.tensor_scalar(out=out, in0=in0, scalar1=s1, scalar2=s2, op0=op0, **kw), R, W, acc=acc)

    def stt(self, out, in0, scalar, in1, op0, op1, R, W, acc=False):
        return self.op("dve", lambda: self.nc.vector.scalar_tensor_tensor(out=out, in0=in0, scalar=scalar, in1=in1, op0=op0, op1=op1), R, W, acc=acc)

    def cp(self, eng, out, in_, R, W, acc=False):
        e = self.E[eng]
        if eng == "act":
            return self.op(eng, lambda: e.copy(out=out, in_=in_), R, W, acc=acc)
        return self.op(eng, lambda: e.tensor_copy(out=out, in_=in_), R, W, acc=acc)

    def recip(self, out, in_, R, W):
        return self.op("dve", lambda: self.nc.vector.reciprocal(out=out, in_=in_), R, W)

    def memset(self, eng, ap, val, W):
        e = self.E[eng]
        return self.op(eng, lambda: e.memset(ap, val), (), W)


def host_consts(T):
    c = {}
    c["ident_f"] = np.eye(128, dtype=np.float32)
    ob = np.zeros((128, 128), np.float32)
    ob[:64, :64] = 1.0
    ob[64:, 64:] = 1.0
    c["onesblk"] = ob
    c["ones_f"] = np.ones((128, 128), np.float32)
    rm = np.zeros((128, 128), np.float32)
    for h in range(2):
        for d in range(64):
            if d < 32:
                rm[h * 64 + d + 32, h * 64 + d] = -1.0
            else:
                rm[h * 64 + d - 32, h * 64 + d] = 1.0
    c["rotm"] = rm
    rows = T // 64
    row = np.repeat(np.arange(rows), 64).astype(np.float32)
    col = np.tile(np.arange(64), rows).astype(np.float32)
    inv = (10000.0 ** (-np.arange(16, dtype=np.float32) / 16)).astype(np.float32)
    ang = np.concatenate([row[:, None] * inv, col[:, None] * inv], -1)
    cos = np.cos(ang).astype(np.float32).T
    sin = np.sin(ang).astype(np.float32).T
    c["ropecos"] = np.ascontiguousarray(np.tile(cos, (4, 1)))
    c["ropesin"] = np.ascontiguousarray(np.tile(sin, (4, 1)))
    j = np.arange(128)[:, None]
    p = np.arange(128)[None, :]
    mprev = np.where(j >= p, 0.0, NEG).astype(np.float32)
    mnext = np.where(j <= p, 0.0, NEG).astype(np.float32)
    c["maskprev"] = np.ascontiguousarray(np.tile(mprev, (1, 4)))
    c["masknext"] = np.ascontiguousarray(np.tile(mnext, (1, 4)))
    s = np.arange(64)[:, None]
    t = np.arange(64)[None, :]
    m = np.zeros((64, 4, 64), np.float32)
    m[:, 0, :] = (s < t)
    m[:, 1, :] = (s <= t)
    m[:, 2, :] = (s > t)
    m[:, 3, :] = (s >= t)
    c["rmask"] = m
    rs = np.ones((128, 8, 128), np.float32)
    rs[:, :, 0::64] = 0.0
    c["scanreset"] = rs
    return c


CONST_DT = {"ident_f": F32, "onesblk": F32, "ones_f": F32, "rotm": F32, "ropecos": F32, "ropesin": F32,
            "maskprev": F32, "masknext": F32, "rmask": F32, "scanreset": F32}

PARAMS = [
    ("w_mod", (DEPTH, D, 6 * D)), ("b_mod", (DEPTH, 6 * D)), ("norm_mix", (DEPTH, D)), ("norm_ffn", (DEPTH, D)),
    ("w_in", (DEPTH, D, IN_DIM)), ("q_gain", (DEPTH, 64)), ("k_gain", (DEPTH, 64)), ("attn_sink", (DEPTH, 8)),
    ("rwkv_conv", (DEPTH, 3, 1792)), ("rwkv_w0", (DEPTH, 2, 512)), ("rwkv_w_up", (DEPTH, 2, 64, 512)),
    ("rwkv_a0", (DEPTH, 2, 512)), ("rwkv_a_up", (DEPTH, 2, 64, 512)), ("rwkv_k_k", (DEPTH, 512)),
    ("rwkv_k_a", (DEPTH, 512)), ("rwkv_r_k", (DEPTH, 8, 64)), ("rwkv_g_up", (DEPTH, 128, 512)),
    ("rwkv_ln_w", (DEPTH, 512)), ("rwkv_ln_b", (DEPTH, 512)), ("rwkv_vres_down", (DEPTH - 1, D, 32)),
    ("rwkv_vres_up", (DEPTH - 1, 32, 512)), ("rwkv_vres_b", (DEPTH - 1, 512)), ("sgu_ln_w", (DEPTH, 512)),
    ("sgu_ln_b", (DEPTH, 512)), ("sgu_w", (DEPTH, 4, 128, 128)), ("sgu_b", (DEPTH, 4, 128)),
    ("w_o_attn", (DEPTH, 512, D)), ("w_o_rwkv", (DEPTH, 512, D)), ("w_o_sgu", (DEPTH, 512, D)),
    ("w_out", (DEPTH, D, D)), ("ffn_w1", (DEPTH, D, FFN)), ("ffn_w3", (DEPTH, D, FFN)), ("ffn_w2", (DEPTH, FFN, D)),
]


SP_LAYOUT = [("bmod", 48), ("nmix", 8), ("nffn", 8), ("gain", 2), ("sink", 8), ("conv_rkv", 72), ("conv_wa", 6),
             ("conv_g", 3), ("w0", 8), ("a0", 8), ("k_k", 8), ("k_a", 8), ("r_k", 8), ("ln_w", 8), ("ln_b", 8),
             ("vres_b", 8), ("sgu_lnw", 512), ("sgu_lnb", 512), ("sgu_b", 512)]
SP_OFF = {}
_o = 0
for _n, _w in SP_LAYOUT:
    SP_OFF[_n] = (_o, _w)
    _o += _w
SP_N = _o


def pack_small(inp, l):
    a = np.zeros((128, SP_N), np.float32)

    def put(name, arr):
        o, w = SP_OFF[name]
        arr = np.asarray(arr, np.float32)
        a[: arr.shape[0], o:o + w] = arr.reshape(arr.shape[0], -1)

    put("bmod", inp["b_mod"][l].reshape(48, 128).T)
    put("nmix", inp["norm_mix"][l].reshape(8, 128).T)
    put("nffn", inp["norm_ffn"][l].reshape(8, 128).T)
    put("gain", np.stack([np.tile(inp["q_gain"][l], 2), np.tile(inp["k_gain"][l], 2)], 1))
    put("sink", np.tile(inp["attn_sink"][l][None, :], (128, 1)))
    cv = inp["rwkv_conv"][l]
    put("conv_rkv", cv[:, :1536].reshape(3, 3, 8, 64).transpose(3, 0, 1, 2))
    put("conv_wa", cv[:, 1536:1664].reshape(3, 2, 64).transpose(2, 0, 1))
    put("conv_g", cv[:, 1664:1792].T)
    dup = lambda a_: np.concatenate([a_, a_], 0)
    put("w0", inp["rwkv_w0"][l].reshape(2, 8, 64).transpose(0, 2, 1).reshape(128, 8))
    put("a0", inp["rwkv_a0"][l].reshape(2, 8, 64).transpose(0, 2, 1).reshape(128, 8))
    put("k_k", dup(inp["rwkv_k_k"][l].reshape(8, 64).T))
    put("k_a", dup(inp["rwkv_k_a"][l].reshape(8, 64).T))
    put("r_k", dup(inp["rwkv_r_k"][l].reshape(8, 64).T))
    put("ln_w", inp["rwkv_ln_w"][l].reshape(8, 64).T)
    put("ln_b", inp["rwkv_ln_b"][l].reshape(8, 64).T)
    if l > 0:
        put("vres_b", dup(inp["rwkv_vres_b"][l - 1].reshape(8, 64).T))
    put("sgu_lnw", np.tile(inp["sgu_ln_w"][l][None, :], (128, 1)))
    put("sgu_lnb", np.tile(inp["sgu_ln_b"][l][None, :], (128, 1)))
    put("sgu_b", inp["sgu_b"][l].reshape(1, 512))
    return a


class Prog:
    def __init__(self, T, nlayers=DEPTH, dbg=(), stop_after=None):
        self.T = T
        self.NTOK = LCTX + T
        self.nlayers = nlayers
        self.dbg = set(dbg)
        self.stop_after = stop_after
        self.blocks = [(0, LCTX, True, 0)] + [(LCTX + i * 512, 512, False, i * 512) for i in range(T // 512)]
        self.nc = bass.Bass("TRN2", target_bir_lowering=False)
        self.es = ExitStack()

    def din(self, name, shape, dt=F32):
        return self.nc.dram_tensor(name, list(shape), dt, kind="ExternalInput").ap()

    def scr(self, name, shape, dt):
        kind = "ExternalOutput" if name in self.dbg else "Internal"
        return self.nc.dram_tensor(name, list(shape), dt, kind=kind).ap()

    def build(self):
        nc, T, NTOK = self.nc, self.T, self.NTOK
        with self.es as es:
            k = self.k = KB(nc, es)
            self.x_in = self.din("x", [T, D])
            self.ctx_in = self.din("ctx", [LCTX, D])
            self.cfm = self.din("cfm", [128, 16])
            self.smallp_d = self.din("smallp", [DEPTH, 128, SP_N])
            self.cd = {}
            hc = host_consts(T)
            for name, arr in hc.items():
                self.cd[name] = self.din("c_" + name, arr.shape)
            self.pd = {}
            for name, shape in PARAMS:
                if name in ("b_mod", "norm_mix", "norm_ffn", "q_gain", "k_gain", "attn_sink", "rwkv_conv", "rwkv_w0",
                            "rwkv_a0", "rwkv_k_k", "rwkv_k_a", "rwkv_r_k", "rwkv_ln_w", "rwkv_ln_b", "rwkv_vres_b",
                            "sgu_ln_w", "sgu_ln_b", "sgu_b"):
                    continue
                self.pd[name] = self.din(name, shape)
            self.out = nc.dram_tensor("out", [T, D], F32, kind="ExternalOutput").ap()
            L = self.nlayers
            self.XT = self.scr("XT", [8, 128, NTOK], F32)
            self.HT = self.scr("HT", [8, 128, NTOK], BF16)
            self.QT = self.scr("QT", [4, 128, NTOK], BF16)
            self.KT = self.scr("KT", [128, NTOK], BF16)
            self.VA = self.scr("VA", [NTOK, 130], BF16)
            self.PRW = self.scr("PRW", [1792, NTOK], F32)
            self.PRB = self.scr("PRB", [1536, NTOK], BF16)
            self.OST = self.scr("OST", [4, 128, NTOK], BF16)
            self.OAT = self.scr("OAT", [4, 128, NTOK], BF16)
            self.ORT = self.scr("ORT", [64, 8, NTOK], BF16)
            NCH = NTOK // CH
            self.RA4 = [self.scr("RA4_%d" % d_, [64, 8, 4, NTOK], BF16) for d_ in range(2)]
            self.TOKB = self.scr("TOKB", [NTOK, 8, 2, 128], BF16)
            self.TOKV = self.scr("TOKV", [NTOK, 8, 64], BF16)
            self.RWC = [self.scr("RWC_%d" % d_, [NCH, 64, 8], F32) for d_ in range(2)]
            self.RBV = self.scr("RBV", [64, 8, NTOK], F32)
            self.RG = self.scr("RG", [64, 8, NTOK], F32)
            self.VF = self.scr("VF", [64, 8, NTOK], F32)
            self.YD = [self.scr("YD_%d" % d_, [64, 8, NTOK], F32) for d_ in range(2)]
            self.WIN = [self.scr("WIN%d" % l, [D, IN_DIM], BF16) for l in range(L)]
            self.WOA = [self.scr("WOA%d" % l, [512, D], BF16) for l in range(L)]
            self.WOR = [self.scr("WOR%d" % l, [512, D], BF16) for l in range(L)]
            self.WOS = [self.scr("WOS%d" % l, [512, D], BF16) for l in range(L)]
            self.WOUT = [self.scr("WOUT%d" % l, [D, D], BF16) for l in range(L)]
            self.W1 = [self.scr("W1_%d" % l, [D, FFN], BF16) for l in range(L)]
            self.W3 = [self.scr("W3_%d" % l, [D, FFN], BF16) for l in range(L)]
            self.W2 = [self.scr("W2_%d" % l, [FFN, D], BF16) for l in range(L)]

            g = es
            self.smallp = k.sb(g, [128, DEPTH, SP_N], F32, "smallp")
            for l in range(DEPTH):
                k.dma("sp", self.smallp[:, l, :], self.smallp_d[l], W=[self.smallp], acc=(l > 0))
            self.ident_f = k.sb(g, [128, 128], F32, "identf")
            self.ones_f = k.sb(g, [128, 128], F32, "onesf")
            self.onesblk_f = k.sb(g, [128, 128], F32, "onesblkf")
            self.rotm_f = k.sb(g, [128, 128], F32, "rotmf")
            k.dma("sp", self.ident_f[:], self.cd["ident_f"], W=[self.ident_f])
            k.dma("sp", self.ones_f[:], self.cd["ones_f"], W=[self.ones_f])
            k.dma("sp", self.onesblk_f[:], self.cd["onesblk"], W=[self.onesblk_f])
            k.dma("sp", self.rotm_f[:], self.cd["rotm"], W=[self.rotm_f])
            self.ident_b = k.sb(g, [128, 128], BF16, "identb")
            self.onesblk_b = k.sb(g, [128, 128], BF16, "onesblkb")
            self.rotm_b = k.sb(g, [128, 128], BF16, "rotmb")
            k.cp("dve", self.ident_b[:], self.ident_f[:], [self.ident_f], [self.ident_b])
            k.cp("dve", self.onesblk_b[:], self.onesblk_f[:], [self.onesblk_f], [self.onesblk_b])
            k.cp("dve", self.rotm_b[:], self.rotm_f[:], [self.rotm_f], [self.rotm_b])
            self.mods = k.sb(g, [128, 2, 6, 8], F32, "mods")
            self.epsD = k.sb(g, [128, 1], F32, "epsD")
            k.memset("dve", self.epsD[:], EPS, [self.epsD])
            self.epsG = k.sb(g, [128, 1], F32, "epsG")
            k.memset("dve", self.epsG[:], GN_EPS, [self.epsG])

            sa = self.stop_after
            if sa is None or sa[0] != "consts":
                self.phase_weights()
            k.barrier()
            if sa is not None and sa[0] in ("weights", "consts"):
                L = 0
            if L > 0:
                self.phase_transpose_in()
                k.barrier()
            if sa is not None and sa[0] == "tin":
                L = 0
            for l in range(L):
                self.phase_mod(l)
                k.barrier()
                if sa == ("mod", l):
                    break
                self.phase_inproj(l)
                k.barrier()
                if self.stop_after == ("inproj", l):
                    break
                last = (l == self.nlayers - 1)
                self.phase_attn(l, last)
                k.barrier()
                if self.stop_after == ("attn", l):
                    break
                self.phase_rwkv_prep(l)
                k.barrier()
                if self.stop_after == ("rprep", l):
                    break
                self.phase_rwkv_scan(l)
                k.barrier()
                self.phase_rwkv_post(l)
                k.barrier()
                if self.stop_after == ("rwkv", l):
                    break
                self.phase_merge(l, last)
                k.barrier()
                self.phase_ffn(l, last)
                k.barrier()
                if self.stop_after == ("layer", l):
                    break
            self.finish()
        return nc

    def spc(self, l, name, rows=128):
        o, w = SP_OFF[name]
        return self.smallp[0:rows, l, o:o + w]

    def finish(self):
        k = self.k
        k.barrier()

    def phase_weights(self):
        k, nc = self.k, self.nc
        CW = 3328
        with ExitStack() as st:
            stg = [k.sb(st, [128, CW], F32, "wstg") for _ in range(3)]
            bfb = [k.sb(st, [128, CW], BF16, "wbf") for _ in range(3)]
            engs = ["dve", "act", "dve"]
            ctr = [0]

            def conv(src, dst, R, C, permq=False):
                for r0 in range(0, R, 128):
                    rr = min(128, R - r0)
                    for c0 in range(0, C, CW):
                        cc = min(CW, C - c0)
                        i = ctr[0] % 3
                        ctr[0] += 1
                        k.dma("sp", stg[i][0:rr, 0:cc], src[r0:r0 + rr, c0:c0 + cc], W=[stg[i]])
                        k.cp(engs[i], bfb[i][0:rr, 0:cc], stg[i][0:rr, 0:cc], [stg[i]], [bfb[i]])
                        if permq and c0 == 0:
                            for g_ in range(2):
                                k.dma("pool", dst[r0:r0 + rr, 0:512].rearrange("p (j g d) -> p j g d", j=4, g=2, d=64)[:, :, g_, :],
                                      bfb[i][0:rr, 0:512].rearrange("p (g j d) -> p g j d", g=2, j=4, d=64)[:, g_, :, :], R=[bfb[i]])
                            k.dma("pool", dst[r0:r0 + rr, 512:cc], bfb[i][0:rr, 512:cc], R=[bfb[i]])
                        else:
                            k.dma("pool", dst[r0:r0 + rr, c0:c0 + cc], bfb[i][0:rr, 0:cc], R=[bfb[i]])

            for l in range(self.nlayers):
                conv(self.pd["w_in"][l], self.WIN[l], D, IN_DIM, permq=True)
                conv(self.pd["w_o_attn"][l], self.WOA[l], 512, D)
                conv(self.pd["w_o_rwkv"][l], self.WOR[l], 512, D)
                conv(self.pd["w_o_sgu"][l], self.WOS[l], 512, D)
                conv(self.pd["w_out"][l], self.WOUT[l], D, D)
                conv(self.pd["ffn_w1"][l], self.W1[l], D, FFN)
                conv(self.pd["ffn_w3"][l], self.W3[l], D, FFN)
                conv(self.pd["ffn_w2"][l], self.W2[l], FFN, D)

    def phase_transpose_in(self):
        k, nc = self.k, self.nc
        XTv = self.XT.rearrange("k p n -> p k n")
        with ExitStack() as st:
            xin = [k.sb(st, [128, D], F32, "xin") for _ in range(2)]
            xo = [k.sb(st, [128, 8, 128], F32, "xo") for _ in range(2)]
            pbs = [k.ps(st, [128, 4, 128], F32, "ptr") for _ in range(4)]
            ntile = self.NTOK // 128
            for i in range(ntile):
                src = self.ctx_in[i * 128:(i + 1) * 128, :] if i < 2 else self.x_in[(i - 2) * 128:(i - 1) * 128, :]
                xi, xoo = xin[i % 2], xo[i % 2]
                k.dma("sp", xi[:], src, W=[xi])
                for hlf in range(2):
                    pb = pbs[(2 * i + hlf) % 4]
                    for j in range(4):
                        kc = hlf * 4 + j
                        k.tr(pb[:, j, :], xi[:, kc * 128:(kc + 1) * 128], self.ident_f[:], [xi, self.ident_f], [pb], inc=(j == 3))
                    eng = "act" if hlf == 0 else "dve"
                    k.cp(eng, xoo[:, hlf * 4:(hlf + 1) * 4, :], pb[:], [pb], [xoo], acc=(hlf == 1))
                k.dma("pool", XTv[:, :, i * 128:(i + 1) * 128], xoo[:], R=[xoo])

    def phase_mod(self, l):
        k, nc = self.k, self.nc
        with ExitStack() as st:
            cs = k.sb(st, [128, 16], F32, "cs")
            sc = k.sb(st, [128, 8, 2], F32, "sc")
            k.dma("sp", cs[:], self.cfm, W=[cs])
            k.act(sc[:].rearrange("p k m -> p m k"), cs[:].rearrange("p (m k) -> p m k", m=2), AF.Silu, [cs], [sc])
            wm = [k.sb(st, [128, 6 * D], F32, "wm") for _ in range(2)]
            pm = k.ps(st, [128, 48, 2], F32, "pmod")
            for kc in range(8):
                w = wm[kc % 2]
                k.dma("sp", w[:, 0:3072], self.pd["w_mod"][l][kc * 128:(kc + 1) * 128, 0:3072], W=[w])
                k.dma("sp", w[:, 3072:6144], self.pd["w_mod"][l][kc * 128:(kc + 1) * 128, 3072:6144], W=[w], acc=True)
                for fc in range(48):
                    first = (kc == 0 and fc == 0)
                    last = (kc == 7 and fc == 47)
                    k.op("pe", lambda: nc.tensor.matmul(pm[:, fc, :], lhsT=w[:, fc * 128:(fc + 1) * 128], rhs=sc[:, kc, :],
                                                        start=first, stop=last, skip_group_check=True),
                         [w, sc], [pm], inc=(fc == 47))
            raw = k.sb(st, [128, 2, 6, 8], F32, "modraw")
            bm = self.spc(l, "bmod")
            for m in range(2):
                k.tt("dve", raw[:, m, :, :], pm[:, :, m].rearrange("p (j k) -> p j k", j=6),
                     bm.rearrange("p (j k) -> p j k", j=6), ALU.add, [pm, self.smallp], [raw], acc=(m == 1))
            mods = self.mods
            for m in range(2):
                for j in (0, 2, 3, 5):
                    k.cp("dve", mods[:, m, j, :], raw[:, m, j, :], [raw], [mods], acc=not (m == 0 and j == 0))
                k.stt(mods[:, m, 1, :], raw[:, m, 1, :], 1.0, self.spc(l, "nmix"), ALU.add, ALU.mult, [raw, self.smallp], [mods], acc=True)
                k.stt(mods[:, m, 4, :], raw[:, m, 4, :], 1.0, self.spc(l, "nffn"), ALU.add, ALU.mult, [raw, self.smallp], [mods], acc=True)

    def norm_mod(self, st_tiles, blk, l, which):
        k, nc = self.k, self.nc
        xT, work, hT, rstd, pss = st_tiles
        n0, nb, is_ctx, t0 = blk
        m = 1 if is_ctx else 0
        jA, jS = (1, 0) if which == 0 else (4, 3)
        k.act(work[:, :, 0:nb], xT[:, :, 0:nb], AF.Square, [xT], [work])
        for kc in range(8):
            k.mm(pss[:, 0:nb], self.ones_f[:], work[:, kc, 0:nb], kc == 0, kc == 7, [self.ones_f, work], [pss])
        k.act(rstd[:, 0:nb], pss[:, 0:nb], AF.Ln, [pss, self.epsD], [rstd], bias=self.epsD[:], scale=1.0 / D)
        k.act(rstd[:, 0:nb], rstd[:, 0:nb], AF.Exp, [rstd], [rstd], scale=-0.5)
        for kc in range(8):
            k.stt(work[:, kc, 0:nb], xT[:, kc, 0:nb], self.mods[:, m, jA, kc:kc + 1], rstd[:, 0:nb], ALU.mult, ALU.mult,
                  [xT, self.mods, rstd], [work], acc=(kc > 0))
        for kc in range(8):
            k.act(hT[:, kc, 0:nb], work[:, kc, 0:nb], AF.Identity, [work, self.mods], [hT], bias=self.mods[:, m, jS, kc:kc + 1],
                  acc=(kc > 0))

    def phase_inproj(self, l):
        k, nc = self.k, self.nc
        XTv = self.XT.rearrange("k p n -> p k n")
        HTv = self.HT.rearrange("k p n -> p k n")
        QTv = self.QT.rearrange("j p n -> p j n")
        OSTv = self.OST.rearrange("g p n -> p g n")
        with ExitStack() as st:
            wA = k.sb(st, [128, 8, 3584], BF16, "wA")
            for kc in range(8):
                k.dma("sp", wA[:, kc, :], self.WIN[l][kc * 128:(kc + 1) * 128, 0:3584], W=[wA], acc=(kc > 0))
            wsf = k.sb(st, [128, 4, 128], F32, "wsf")
            k.dma("sp", wsf[:], self.pd["sgu_w"][l].rearrange("g p q -> p g q"), W=[wsf])
            WsT = k.sb(st, [128, 4, 128], BF16, "WsT")
            xT = k.sb(st, [128, 8, 512], F32, "xT")
            work = k.sb(st, [128, 8, 512], F32, "work")
            hTs = [k.sb(st, [128, 8, 512], BF16, "hT") for _ in range(2)]
            rstd = k.sb(st, [128, 512], F32, "rstd")
            pss = k.ps(st, [128, 512], F32, "pss")
            pproj = [k.ps(st, [128, 512], F32, "pproj") for _ in range(2)]
            paux = [k.ps(st, [128, 512], F32, "paux") for _ in range(1)]
            ptok = [k.ps(st, [128, 512], F32, "ptok") for _ in range(3)]
            pso = k.ps(st, [128, 4, 128], F32, "pso")
            for g_ in range(4):
                k.tr(ptok[0][:, g_ * 128:(g_ + 1) * 128], wsf[:, g_, :], self.ident_f[:], [wsf, self.ident_f], [ptok[0]], inc=(g_ == 3))
            k.cp("dve", WsT[:].rearrange("p g q -> p (g q)"), ptok[0][:], [ptok[0]], [WsT])
            sqb_l = [k.sb(st, [128, 512], BF16, "sqb") for _ in range(2)]
            r1_l = [k.sb(st, [128, 512], F32, "r1") for _ in range(2)]
            qn_l = [k.sb(st, [128, 512], BF16, "qn") for _ in range(2)]
            t1_l = [k.sb(st, [128, 512], F32, "t1") for _ in range(2)]
            t2_l = [k.sb(st, [128, 512], F32, "t2") for _ in range(2)]
            qr = [k.sb(st, [128, 512], BF16, "qr") for _ in range(2)]
            cosb = k.sb(st, [128, 512], F32, "cosb")
            sinb = k.sb(st, [128, 512], F32, "sinb")
            rstg = [k.sb(st, [128, 512], F32, "rstg") for _ in range(3)]
            rstb = [k.sb(st, [128, 512], BF16, "rstb") for _ in range(3)]
            uT = k.sb(st, [128, 4, 512], BF16, "uT")
            ost = k.sb(st, [128, 4, 512], BF16, "ost")
            va = [k.sb(st, [128, 2, 65], BF16, "va") for _ in range(2)]
            for v_ in va:
                k.memset("dve", v_[:], 1.0, [v_])
            gt_l = [k.sb(st, [128, 512], F32, "gt") for _ in range(2)]
            junk_l = [k.sb(st, [128, 512], BF16, "junk") for _ in range(2)]
            vn_l = [k.sb(st, [128, 512], BF16, "vn") for _ in range(2)]
            vnf_l = [k.sb(st, [128, 512], F32, "vnf") for _ in range(2)]
            stat_l = [k.sb(st, [128, 8], F32, "stat") for _ in range(2)]
            ones_row = self.ones_f[0:1, :]
            sgub = self.spc(l, "sgu_b", rows=1)
            lnw = self.spc(l, "sgu_lnw")
            lnb = self.spc(l, "sgu_lnb")
            gain = self.spc(l, "gain")
            ectr = [0]
            def do_norm(bi_):
                n0_, nb_, _c, _t = self.blocks[bi_]
                hT_ = hTs[bi_ % 2]
                k.dma("sp", xT[:, :, 0:nb_], XTv[:, :, n0_:n0_ + nb_], W=[xT])
                self.norm_mod((xT, work, hT_, rstd, pss), self.blocks[bi_], l, 0)
                k.dma("pool", HTv[:, :, n0_:n0_ + nb_], hT_[:, :, 0:nb_], R=[hT_])

            do_norm(0)
            tctr = 0
            for bi, blk in enumerate(self.blocks):
                n0, nb, is_ctx, t0 = blk
                hT = hTs[bi % 2]
                if not is_ctx:
                    k.dma("sp", cosb[:, 0:nb], self.cd["ropecos"][:, t0:t0 + nb], W=[cosb])
                    k.dma("sp", sinb[:, 0:nb], self.cd["ropesin"][:, t0:t0 + nb], W=[sinb])
                if bi + 1 < len(self.blocks):
                    do_norm(bi + 1)
                chunks = [("q", j, j * 128) for j in range(4)] + [("k", 0, 512)] + \
                         [("r", j, 768 + j * 128) for j in range(14)] + [("u", j, 2560 + j * 128) for j in range(4)]
                for ci, (kind, j, c0) in enumerate(chunks):
                    pb = pproj[ci % 2]
                    for kc in range(8):
                        k.mm(pb[:, 0:nb], wA[:, kc, c0:c0 + 128], hT[:, kc, 0:nb], kc == 0, kc == 7, [wA, hT], [pb])
                    if kind in ("q", "k"):
                        sqb, r1, qn, t1, t2 = sqb_l[ci % 2], r1_l[ci % 2], qn_l[ci % 2], t1_l[ci % 2], t2_l[ci % 2]
                        pss2 = paux[0]
                        prot = paux[0]
                        gcol = gain[:, 0:1] if kind == "q" else gain[:, 1:2]
                        k.act(sqb[:, 0:nb], pb[:, 0:nb], AF.Square, [pb], [sqb])
                        k.mm(pss2[:, 0:nb], self.onesblk_b[:], sqb[:, 0:nb], True, True, [self.onesblk_b, sqb], [pss2])
                        k.act(r1[:, 0:nb], pss2[:, 0:nb], AF.Ln, [pss2, self.epsD], [r1], bias=self.epsD[:], scale=1.0 / HD)
                        k.act(r1[:, 0:nb], r1[:, 0:nb], AF.Exp, [r1], [r1], scale=-0.5)
                        dst = qr[ci % 2]
                        if is_ctx:
                            k.stt(dst[:, 0:nb], pb[:, 0:nb], gcol, r1[:, 0:nb], ALU.mult, ALU.mult, [pb, r1, self.smallp], [dst])
                        else:
                            k.stt(qn[:, 0:nb], pb[:, 0:nb], gcol, r1[:, 0:nb], ALU.mult, ALU.mult, [pb, r1, self.smallp], [qn])
                            k.mm(prot[:, 0:nb], self.rotm_b[:], qn[:, 0:nb], True, True, [self.rotm_b, qn], [prot])
                            k.tt("dve", t1[:, 0:nb], qn[:, 0:nb], cosb[:, 0:nb], ALU.mult, [qn, cosb], [t1])
                            k.tt("dve", t2[:, 0:nb], prot[:, 0:nb], sinb[:, 0:nb], ALU.mult, [prot, sinb], [t2])
                            k.tt("dve", dst[:, 0:nb], t1[:, 0:nb], t2[:, 0:nb], ALU.add, [t1, t2], [dst])
                        if kind == "q":
                            k.dma("pool", QTv[:, j, n0:n0 + nb], dst[:, 0:nb], R=[dst])
                        else:
                            k.dma("pool", self.KT[:, n0:n0 + nb], dst[:, 0:nb], R=[dst])
                    elif kind == "r":
                        eng = "act" if ectr[0] % 2 == 0 else "dve"
                        if j < 12:
                            sg = rstb[ectr[0] % 3]
                            k.cp(eng, sg[:, 0:nb], pb[:, 0:nb], [pb], [sg])
                            k.dma("pool", self.PRB[j * 128:(j + 1) * 128, n0:n0 + nb], sg[:, 0:nb], R=[sg])
                        else:
                            sg = rstg[ectr[0] % 3]
                            k.cp(eng, sg[:, 0:nb], pb[:, 0:nb], [pb], [sg])
                            k.dma("pool", self.PRW[j * 128:(j + 1) * 128, n0:n0 + nb], sg[:, 0:nb], R=[sg])
                        ectr[0] += 1
                    else:
                        k.act(uT[:, j, 0:nb], pb[:, 0:nb], AF.Gelu_apprx_tanh, [pb], [uT], acc=(j > 0))
                for tt_ in range(nb // 128):
                    ts_ = slice(tt_ * 128, (tt_ + 1) * 128)
                    pv = ptok[0]
                    pg = ptok[1 + tctr % 2]
                    gt, junk, vn, vnf, stat = gt_l[tctr % 2], junk_l[tctr % 2], vn_l[tctr % 2], vnf_l[tctr % 2], stat_l[tctr % 2]
                    tctr += 1
                    for kc in range(8):
                        k.mm(pv[:, 0:128], hT[:, kc, ts_], wA[:, kc, 640:768], kc == 0, kc == 7, [wA, hT], [pv])
                    for kc in range(8):
                        k.mm(pg[:, :], hT[:, kc, ts_], wA[:, kc, 3072:3584], kc == 0, kc == 7, [wA, hT], [pg])
                    vt = va[tt_ % 2]
                    k.cp("dve", vt[:, :, 0:64], pv[:, 0:128].rearrange("p (h d) -> p h d", h=2), [pv], [vt])
                    k.dma("pool", self.VA[n0 + tt_ * 128:n0 + (tt_ + 1) * 128, :], vt[:].rearrange("p h d -> p (h d)"), R=[vt])
                    k.act(gt[:], pg[:], AF.Gelu_apprx_tanh, [pg], [gt, stat], accum=stat[:, 0:1])
                    k.ts("dve", stat[:, 1:2], stat[:, 0:1], -1.0 / 512, ALU.mult, [stat], [stat])
                    k.act(junk[:], gt[:], AF.Square, [gt, stat], [junk, stat], bias=stat[:, 1:2], accum=stat[:, 2:3])
                    k.act(stat[:, 3:4], stat[:, 2:3], AF.Ln, [stat, self.epsD], [stat], bias=self.epsD[:], scale=1.0 / 512)
                    k.act(stat[:, 4:5], stat[:, 3:4], AF.Exp, [stat], [stat], scale=-0.5)
                    k.ts("dve", vnf[:], gt[:], stat[:, 1:2], ALU.add, [gt, stat], [vnf], s2=stat[:, 4:5], op1=ALU.mult)
                    k.tt("dve", vnf[:], vnf[:], lnw, ALU.mult, [vnf, self.smallp], [vnf])
                    k.tt("dve", vn[:], vnf[:], lnb, ALU.add, [vnf, self.smallp], [vn])
                    for g_ in range(4):
                        gs = slice(g_ * 128, (g_ + 1) * 128)
                        k.mm(pso[:, g_, :], vn[:, gs], WsT[:, g_, :], True, False, [vn, WsT], [pso], inc=False)
                        k.mm(pso[:, g_, :], ones_row, sgub[:, gs], False, True, [self.ones_f, self.smallp], [pso], inc=(g_ == 3))
                    k.tt("dve", ost[:, :, ts_], uT[:, :, ts_], pso[:], ALU.mult, [uT, pso], [ost], acc=(tt_ > 0))
                k.dma("pool", OSTv[:, :, n0:n0 + nb], ost[:, :, 0:nb], R=[ost])


    def phase_attn(self, l, last):
        k, nc = self.k, self.nc
        NTOK = self.NTOK
        NT = NTOK // 128
        nlat = self.T // 128
        OATv = self.OAT.rearrange("c p n -> p c n")
        with ExitStack() as st:
            Qs = k.sb(st, [128, 4, NTOK], BF16, "Qs")
            Ks = k.sb(st, [128, NTOK], BF16, "Ks")
            Vs = k.sb(st, [128, NT, 130], BF16, "Vs")
            QTv = self.QT.rearrange("j p n -> p j n")
            for j in range(4):
                k.dma("sp", Qs[:, j, :], QTv[:, j, :], W=[Qs], acc=(j > 0))
            k.dma("sp", Ks[:], self.KT, W=[Ks])
            k.dma("sp", Vs[:], self.VA.rearrange("(t p) f -> p t f", p=128), W=[Vs])
            mf = k.sb(st, [128, 2, 512], F32, "mf")
            k.dma("sp", mf[:, 0, :], self.cd["maskprev"], W=[mf])
            k.dma("sp", mf[:, 1, :], self.cd["masknext"], W=[mf], acc=True)
            mb = k.sb(st, [128, 2, 512], BF16, "mb")
            k.cp("dve", mb[:], mf[:], [mf], [mb])
            esink = k.sb(st, [128, 8], F32, "esink")
            k.act(esink[:], self.spc(l, "sink"), AF.Exp, [self.smallp], [esink])
            pS = [k.ps(st, [128, 512], F32, "pS") for _ in range(3)]
            pO = [[k.ps(st, [128, 4, 128], F32, "pO") for _ in range(2)] for _ in range(2)]
            pT = k.ps(st, [128, 4, 128], BF16, "pT")
            PT = [k.sb(st, [128, 512], BF16, "PT") for _ in range(3)]
            den = k.sb(st, [128, 8], F32, "den")
            oa = [k.sb(st, [128, 8, 64], BF16, "oa") for _ in range(2)]
            oat = [k.sb(st, [128, 4, 128], BF16, "oat") for _ in range(2)]
            sctr_l = [0]
            pending = None
            qtiles = list(range(2, NT)) if last else list(range(NT))
            for qn_, qi in enumerate(qtiles):
                qs = slice(qi * 128, (qi + 1) * 128)
                if qi < 2:
                    kbs = [(0, None), (1, None)]
                else:
                    b = qi - 2
                    kbs = []
                    if b > 0:
                        kbs.append((qi - 1, 0))
                    kbs.append((qi, None))
                    if b < nlat - 1:
                        kbs.append((qi + 1, 1))
                    kbs += [(0, None), (1, None)]
                oat_ = oat[qn_ % 2]
                oa_ = oa[qn_ % 2]
                items = [(g_, ki, kb, msk) for g_ in range(2) for ki, (kb, msk) in enumerate(kbs)]
                slots = []
                nkb = len(kbs)

                def emit_S(it, qs=qs, slots=slots):
                    g_, ki, kb, msk = it
                    gp = slice(g_ * 64, (g_ + 1) * 64)
                    ps_ = pS[sctr_l[0] % 3]
                    pt_ = PT[sctr_l[0] % 3]
                    sctr_l[0] += 1
                    k.mm(ps_[:], Ks[gp, kb * 128:(kb + 1) * 128], Qs[gp, :, qs], True, msk is None, [Ks, Qs], [ps_])
                    if msk is not None:
                        k.mm(ps_[:], self.ident_b[:], mb[:, msk, :], False, True, [self.ident_b, mb], [ps_])
                    k.act(pt_[:], ps_[:], AF.Exp, [ps_], [pt_], scale=0.125)
                    slots.append(pt_)

                def emit_PV(idx, items=items, slots=slots, nkb=nkb, qn_=qn_, oa_=oa_):
                    g_, ki, kb, msk = items[idx]
                    pt_ = slots[idx]
                    po = pO[qn_ % 2][g_]
                    for jh in range(4):
                        first = (ki == 0 and jh == 0)
                        lastmm = (ki == nkb - 1 and jh == 3)
                        k.op("pe", lambda: nc.tensor.matmul(po[:, jh, 0:65], lhsT=pt_[:, jh * 128:(jh + 1) * 128],
                                                            rhs=Vs[:, kb, g_ * 65:(g_ + 1) * 65], start=first, stop=lastmm,
                                                            skip_group_check=True),
                             [pt_, Vs], [po], inc=(jh == 3))
                    if ki == nkb - 1:
                        hs = slice(g_ * 4, (g_ + 1) * 4)
                        k.tt("dve", den[:, hs], po[:, :, 64], esink[:, hs], ALU.add, [po, esink], [den], acc=(g_ == 1))
                        k.recip(den[:, hs], den[:, hs], [den], [den])
                        k.tt("dve", oa_[:, hs, :], po[:, :, 0:64], den[:, hs].unsqueeze(2).broadcast_to([128, 4, 64]), ALU.mult,
                             [po, den], [oa_], acc=(g_ == 1))

                for idx in range(len(items)):
                    emit_S(items[idx])
                    if idx >= 1:
                        emit_PV(idx - 1)
                emit_PV(len(items) - 1)
                def finish_tile(oa__=oa_, oat__=oat_, qs_=qs):
                    for c_ in range(4):
                        k.tr(pT[:, c_, :], oa__[:, 2 * c_:2 * c_ + 2, :].rearrange("p h d -> p (h d)"), self.ident_b[:], [oa__, self.ident_b], [pT],
                             inc=(c_ == 3))
                    k.cp("act", oat__[:], pT[:], [pT], [oat__])
                    k.dma("pool", OATv[:, :, qs_], oat__[:], R=[oat__])
                if pending is not None:
                    pending()
                pending = finish_tile
            if pending is not None:
                pending()

    def phase_rwkv_prep(self, l):
        k, nc = self.k, self.nc
        NTOK = self.NTOK
        PB = 128
        CC = 0.6065306597126334
        PRWv = self.PRB.rearrange("(w c) n -> c w n", c=64)
        PWAv = self.PRW[1536:1664].rearrange("(w c) n -> c w n", c=64)
        PGv = self.PRW[1664:1792]
        HTv = self.HT.rearrange("k p n -> p k n")
        with ExitStack() as st:
            T_ = lambda nm, dt=F32, shp=(128, 8, PB): k.sb(st, list(shp), dt, nm)
            dg = k.sb(st, [64, 72, 2, 64], BF16, "dg")
            cw = self.spc(l, "conv_rkv", rows=64)
            idb = self.ident_f[0:64, 0:64].unsqueeze(1).broadcast_to([64, 72, 64])
            cwb = cw.unsqueeze(2).broadcast_to([64, 72, 64])
            for dd in range(2):
                k.tt("dve", dg[:, :, dd, :], idb, cwb, ALU.mult, [self.ident_f, self.smallp], [dg], acc=(dd == 1))
            dg2 = k.sb(st, [64, 6, 64], F32, "dg2")
            k.tt("dve", dg2[:], self.ident_f[0:64, 0:64].unsqueeze(1).broadcast_to([64, 6, 64]),
                 self.spc(l, "conv_wa", rows=64).unsqueeze(2).broadcast_to([64, 6, 64]), ALU.mult, [self.ident_f, self.smallp], [dg2])
            dg3 = k.sb(st, [128, 3, 128], F32, "dg3")
            k.tt("dve", dg3[:], self.ident_f[:].unsqueeze(1).broadcast_to([128, 3, 128]),
                 self.spc(l, "conv_g").unsqueeze(2).broadcast_to([128, 3, 128]), ALU.mult, [self.ident_f, self.smallp], [dg3])
            stg = k.sb(st, [128, 1024], F32, "lstg")
            wup = k.sb(st, [64, 8, 2, 64], BF16, "wup")
            aup = k.sb(st, [64, 8, 2, 64], BF16, "aup")
            for (dst, nm) in ((wup, "rwkv_w_up"), (aup, "rwkv_a_up")):
                for dd in range(2):
                    k.dma("sp", stg[0:64, dd * 512:(dd + 1) * 512], self.pd[nm][l, dd], W=[stg], acc=(dd == 1))
                k.cp("dve", dst[:].rearrange("r h d c -> r d h c"), stg[0:64, :].rearrange("r (d h c) -> r d h c", d=2, h=8), [stg], [dst])
            gup = k.sb(st, [128, 8, 64], BF16, "gup")
            k.dma("sp", stg[:, 0:512], self.pd["rwkv_g_up"][l], W=[stg])
            k.cp("dve", gup[:].rearrange("r h c -> r (h c)"), stg[:, 0:512], [stg], [gup])
            if l > 0:
                dwn = k.sb(st, [128, 8, 32], BF16, "dwn")
                k.dma("sp", stg[:, 0:256].rearrange("p (k r) -> p k r", k=8), self.pd["rwkv_vres_down"][l - 1].rearrange("(k p) r -> p k r", p=128), W=[stg])
                k.cp("dve", dwn[:].rearrange("p k r -> p (k r)"), stg[:, 0:256], [stg], [dwn])
                vup = k.sb(st, [32, 8, 2, 64], BF16, "vup")
                k.dma("sp", stg[0:32, 0:512], self.pd["rwkv_vres_up"][l - 1], W=[stg])
                for dd in range(2):
                    k.cp("dve", vup[:, :, dd, :], stg[0:32, 0:512].rearrange("r (h c) -> r h c", h=8), [stg], [vup], acc=(dd == 1))
            rsm = k.sb(st, [128, 8, PB], F32, "rsm")
            k.dma("sp", rsm[:], self.cd["scanreset"], W=[rsm])
            eps12 = k.sb(st, [128, 1], F32, "eps12")
            k.memset("dve", eps12[:], 1e-12, [eps12])
            hal = [k.sb(st, [64, 8, PB + 2], BF16, "hal") for _ in range(3)]
            hwa = k.sb(st, [64, 2, PB + 2], F32, "hwa")
            hg = k.sb(st, [128, PB + 2], F32, "hg")
            r_, kx, v_ = T_("r"), T_("kx"), T_("v")
            kA, kB, sig, asg = T_("kA"), T_("kB"), T_("sig"), T_("asg")
            P_, Q_, TP = T_("P"), T_("Q"), T_("TP")
            E = [T_("E") for _ in range(2)]
            kd, bb, tm, tm2 = T_("kd"), T_("bb"), T_("tm"), T_("tm2")
            wc = k.sb(st, [128, 2, 8], F32, "wc")
            xwt = k.sb(st, [64, PB], BF16, "xwt")
            xab = k.sb(st, [64, PB], BF16, "xab")
            sgx = k.sb(st, [128, PB], BF16, "sgx")
            hdb = k.sb(st, [32, PB], BF16, "hdb")
            hTb = k.sb(st, [128, 8, PB], BF16, "hTb")
            ob = [k.sb(st, [128, 8, PB], BF16, "ob") for _ in range(6)]
            vb = k.sb(st, [64, 8, PB], BF16, "vb")
            tokb = k.sb(st, [128, 8, 2, 128], BF16, "tokb")
            tokv = k.sb(st, [128, 8, 64], BF16, "tokv")
            gT = k.sb(st, [64, 8, PB], F32, "gT")
            bv = k.sb(st, [64, 8, PB], F32, "bv")
            pu = [k.ps(st, [128, 4, PB], F32, "pu") for _ in range(4)]
            ptr = [k.ps(st, [128, 8, 128], BF16, "ptr") for _ in range(2)]
            pm = k.ps(st, [128, 512], F32, "pm")
            ptv = k.ps(st, [128, 8, 64], BF16, "ptv")
            puc = [0]

            def nextpu():
                puc[0] += 1
                return pu[puc[0] % 4]

            bcast = lambda name: self.spc(l, name).unsqueeze(2).broadcast_to([128, 8, PB])
            nblk = NTOK // PB
            ectr = 0
            for bi in range(nblk):
                n0 = bi * PB
                lz = (n0 == 0 or n0 == LCTX)
                rz = (n0 + PB == LCTX or n0 + PB == NTOK)
                lo = 1 if lz else 0
                hi = PB + 1 if rz else PB + 2

                def load_halo(tile, src, a3):
                    first = True
                    if lz:
                        ap = tile[:, :, 0:1] if a3 else tile[:, 0:1]
                        k.op("pool", lambda: nc.gpsimd.memset(ap, 0.0), (), [tile])
                        first = False
                    if rz:
                        ap2 = tile[:, :, PB + 1:PB + 2] if a3 else tile[:, PB + 1:PB + 2]
                        k.op("pool", lambda: nc.gpsimd.memset(ap2, 0.0), (), [tile], acc=not first)
                        first = False
                    dst = tile[:, :, lo:hi] if a3 else tile[:, lo:hi]
                    k.dma("sp", dst, src, W=[tile], acc=not first)

                outs3 = (r_, kx, v_)
                for wch in range(3):
                    h_ = hal[wch]
                    load_halo(h_, PRWv[:, wch * 8:(wch + 1) * 8, n0 - 1 + lo:n0 - 1 + hi], True)
                    for half in range(2):
                        pb = nextpu()
                        for hh in range(4):
                            h = half * 4 + hh
                            w = wch * 8 + h
                            for tap in range(3):
                                k.mm(pb[:, hh, :], dg[:, tap * 24 + w, :, :].rearrange("p d c -> p (d c)"), h_[:, h, tap:tap + PB],
                                     tap == 0, tap == 2, [dg, h_], [pb], inc=(tap == 2 and hh == 3))
                        eng = "act" if ectr % 2 == 0 else "dve"
                        ectr += 1
                        k.cp(eng, outs3[wch][:, half * 4:(half + 1) * 4, :], pb[:], [pb], [outs3[wch]], acc=(half == 1))
                if getattr(self, 'rp_cut', 99) < 1:
                    return
                load_halo(hwa, PWAv[:, :, n0 - 1 + lo:n0 - 1 + hi], True)
                load_halo(hg, PGv[:, n0 - 1 + lo:n0 - 1 + hi], False)
                if getattr(self, 'rp_cut', 99) < 0.3:
                    return
                for wch in range(2):
                    for tap in range(3):
                        k.mm(pm[0:64, wch * PB:(wch + 1) * PB], dg2[:, tap * 2 + wch, :], hwa[:, wch, tap:tap + PB], tap == 0, tap == 2,
                             [dg2, hwa], [pm], inc=(tap == 2 and wch == 1))
                k.act(xwt[:], pm[0:64, 0:PB], AF.Tanh, [pm], [xwt])
                k.cp("act", xab[:], pm[0:64, PB:2 * PB], [pm], [xab])
                if getattr(self, 'rp_cut', 99) < 0.6:
                    return
                pbg = nextpu()
                for tap in range(3):
                    k.mm(pbg[:, 0, :], dg3[:, tap, :], hg[:, tap:tap + PB], tap == 0, tap == 2, [dg3, hg], [pbg])
                k.act(sgx[:], pbg[:, 0, :], AF.Sigmoid, [pbg], [sgx])
                if getattr(self, 'rp_cut', 99) < 2:
                    return
                for (dst, wt, src, bname) in ((sig, wup, xwt, "w0"), (asg, aup, xab, "a0")):
                    bcol = self.spc(l, bname)
                    for half in range(2):
                        pb = nextpu()
                        for hh in range(4):
                            h = half * 4 + hh
                            k.mm(pb[:, hh, :], wt[:, h, :, :].rearrange("r d c -> r (d c)"), src[:], True, True, [wt, src], [pb], inc=(hh == 3))
                        for hh in range(4):
                            h = half * 4 + hh
                            k.act(dst[:, h, :], pb[:, hh, :], AF.Sigmoid, [pb, self.smallp], [dst], bias=bcol[:, h:h + 1], acc=(h > 0))
                if getattr(self, 'rp_cut', 99) < 3:
                    return
                for half in range(2):
                    pb = nextpu()
                    for hh in range(4):
                        h = half * 4 + hh
                        k.mm(pb[0:64, hh, :], gup[:, h, :], sgx[:], True, True, [gup, sgx], [pb], inc=(hh == 3))
                    k.cp("act", gT[:, half * 4:(half + 1) * 4, :], pb[0:64, :, :], [pb], [gT], acc=(half == 1))
                k.dma("pool", self.RG[:, :, n0:n0 + PB], gT[:], R=[gT])
                if getattr(self, 'rp_cut', 99) < 4:
                    return
                if l > 0:
                    k.dma("sp", hTb[:], HTv[:, :, n0:n0 + PB], W=[hTb])
                    for kc in range(8):
                        k.mm(pm[0:32, 3 * PB:4 * PB], dwn[:, kc, :], hTb[:, kc, :], kc == 0, kc == 7, [dwn, hTb], [pm])
                    k.cp("act", hdb[:], pm[0:32, 3 * PB:4 * PB], [pm], [hdb])
                    vb_col = self.spc(l, "vres_b")
                    for half in range(2):
                        pb = nextpu()
                        for hh in range(4):
                            h = half * 4 + hh
                            k.mm(pb[:, hh, :], vup[:, h, :, :].rearrange("r d c -> r (d c)"), hdb[:], True, True, [vup, hdb], [pb], inc=(hh == 3))
                        for hh in range(4):
                            h = half * 4 + hh
                            k.act(tm[:, h, :], pb[:, hh, :], AF.Sigmoid, [pb, self.smallp], [tm], bias=vb_col[:, h:h + 1], acc=(h > 0))
                    for dd in range(2):
                        k.dma("sp", tm2[dd * 64:(dd + 1) * 64], self.VF[:, :, n0:n0 + PB], W=[tm2], acc=(dd == 1))
                    k.tt("dve", tm2[:], tm2[:], v_[:], ALU.subtract, [tm2, v_], [tm2])
                    k.tt("dve", tm2[:], tm2[:], tm[:], ALU.mult, [tm2, tm], [tm2])
                    k.tt("dve", v_[:], v_[:], tm2[:], ALU.add, [v_, tm2], [v_])
                else:
                    k.dma("pool", self.VF[:, :, n0:n0 + PB], v_[0:64], R=[v_])
                k.cp("act", vb[:], v_[0:64], [v_], [vb])
                if getattr(self, 'rp_cut', 99) < 5:
                    return
                k.tt("dve", tm[:], kx[:], bcast("k_k"), ALU.mult, [kx, self.smallp], [tm])
                k.act(tm2[:], tm[:], AF.Square, [tm], [tm2])
                for half in range(2):
                    pb = nextpu()
                    for hh in range(4):
                        h = half * 4 + hh
                        k.mm(pb[:, hh, :], self.onesblk_f[:], tm2[:, h, :], True, True, [self.onesblk_f, tm2], [pb], inc=(hh == 3))
                    k.act(kA[:, half * 4:(half + 1) * 4, :], pb[:], AF.Ln, [pb, eps12], [kA], bias=eps12[:], acc=(half == 1))
                k.act(kA[:], kA[:], AF.Exp, [kA], [kA], scale=-0.5)
                kk = kB
                k.tt("dve", kk[:], tm[:], kA[:], ALU.mult, [tm, kA], [kk])
                k.tt("dve", kA[:], kx[:], bcast("k_a"), ALU.mult, [kx, self.smallp], [kA])
                k.tt("dve", kx[:], kx[:], kA[:], ALU.subtract, [kx, kA], [kx])
                k.tt("dve", kd[:], kA[:], asg[:], ALU.mult, [kA, asg], [kd])
                k.tt("dve", kd[:], kd[:], kx[:], ALU.add, [kd, kx], [kd])
                k.tt("dve", bb[:], kk[:], asg[:], ALU.mult, [kk, asg], [bb])
                if getattr(self, 'rp_cut', 99) < 6:
                    return
                k.tt("dve", tm[:], r_[:], kd[:], ALU.mult, [r_, kd], [tm])
                k.tt("dve", tm[:], tm[:], bcast("r_k"), ALU.mult, [tm, self.smallp], [tm])
                lnb_b = self.spc(l, "ln_b", rows=64).unsqueeze(2).broadcast_to([64, 8, PB])
                for half in range(2):
                    pb = nextpu()
                    for hh in range(4):
                        h = half * 4 + hh
                        k.mm(pb[0:64, hh, :], self.ones_f[:, 0:64], tm[:, h, :], True, True, [self.ones_f, tm], [pb], inc=(hh == 3))
                    k.tt("dve", bv[:, half * 4:(half + 1) * 4, :], pb[0:64, :, :], v_[0:64, half * 4:(half + 1) * 4, :], ALU.mult, [pb, v_], [bv], acc=(half == 1))
                k.tt("dve", bv[:], bv[:], lnb_b, ALU.add, [bv, self.smallp], [bv])
                k.dma("pool", self.RBV[:, :, n0:n0 + PB], bv[:], R=[bv])
                if getattr(self, 'rp_cut', 99) < 7:
                    return
                k.op("dve", lambda: nc.vector.tensor_tensor_scan(out=P_[:].rearrange("p h t -> p (h t)"), data0=rsm[:].rearrange("p h t -> p (h t)"),
                                                                 data1=sig[:].rearrange("p h t -> p (h t)"), initial=0.0, op0=ALU.mult, op1=ALU.add),
                     [rsm, sig], [P_])
                k.tt("dve", Q_[:], P_[:], sig[:], ALU.subtract, [P_, sig], [Q_])
                Pc = P_[:].rearrange("p h (j t) -> p h j t", t=64)
                tot = Pc[:, :, :, 63:64]
                totb = tot.broadcast_to([128, 8, 2, 64])
                k.act(wc[:].rearrange("p j h -> p h j"), P_[:].rearrange("p h (j t) -> p h j t", t=64)[:, :, :, 63], AF.Exp, [P_], [wc], scale=-CC)
                k.tt("dve", TP[:].rearrange("p h (j t) -> p h j t", t=64), totb, Pc, ALU.subtract, [P_], [TP])
                k.tt("dve", P_[64:128], TP[64:128], sig[64:128], ALU.add, [TP, sig], [P_])
                if getattr(self, 'rp_cut', 99) < 8:
                    return
                Ee, Ei = E[0], E[1]
                k.act(Ee[0:64], Q_[0:64], AF.Exp, [Q_], [Ee], scale=-CC)
                k.act(Ee[64:128], TP[64:128], AF.Exp, [TP], [Ee], scale=-CC, acc=True)
                k.act(Ei[:], P_[:], AF.Exp, [P_], [Ei], scale=-CC)
                k.stt(ob[0][:], kk[:], -1.0, Ee[:], ALU.mult, ALU.mult, [kk, Ee], [ob[0]])
                k.tt("dve", ob[1][:], r_[:], Ei[:], ALU.mult, [r_, Ei], [ob[1]])
                k.act(Ee[:], P_[:], AF.Exp, [P_], [Ee], scale=CC)
                k.tt("dve", ob[2][:], bb[:], Ee[:], ALU.mult, [bb, Ee], [ob[2]])
                k.tt("dve", ob[3][:], kd[:], Ee[:], ALU.mult, [kd, Ee], [ob[3]])
                k.act(Ei[0:64], TP[0:64], AF.Exp, [TP], [Ei], scale=-CC)
                k.act(Ei[64:128], Q_[64:128], AF.Exp, [Q_], [Ei], scale=-CC, acc=True)
                k.tt("dve", ob[4][:], bb[:], Ei[:], ALU.mult, [bb, Ei], [ob[4]])
                k.tt("dve", ob[5][:], kd[:], Ei[:], ALU.mult, [kd, Ei], [ob[5]])
                if getattr(self, 'rp_cut', 99) < 9:
                    return
                for q_ in range(4):
                    for dd in range(2):
                        k.dma("pool", self.RA4[dd][:, :, q_, n0:n0 + PB], ob[q_][dd * 64:(dd + 1) * 64], R=[ob[q_]])
                for dd in range(2):
                    for j in range(2):
                        k.dma("pool", self.RWC[dd][n0 // 64 + j], wc[dd * 64:(dd + 1) * 64, j, :], R=[wc])
                if getattr(self, 'rp_cut', 99) < 10:
                    return
                for q_ in range(2):
                    for h in range(8):
                        k.tr(ptr[q_][:, h, :], ob[4 + q_][:, h, :], self.ident_b[:], [ob[4 + q_], self.ident_b], [ptr[q_]], inc=(h == 7))
                    k.cp("act" if q_ == 0 else "dve", tokb[:, :, q_, :], ptr[q_][:], [ptr[q_]], [tokb], acc=(q_ == 1))
                for h in range(8):
                    k.tr(ptv[:, h, :], vb[:, h, :], self.ident_b[0:64, 0:64], [vb, self.ident_b], [ptv], inc=(h == 7))
                k.cp("act", tokv[:], ptv[:], [ptv], [tokv])
                k.dma("pool", self.TOKB[n0:n0 + PB].rearrange("n h q f -> n (h q f)"), tokb[:].rearrange("p h q f -> p (h q f)"), R=[tokb])
                k.dma("pool", self.TOKV[n0:n0 + PB].rearrange("n h f -> n (h f)"), tokv[:].rearrange("p h f -> p (h f)"), R=[tokv])
                if getattr(self, 'rp_cut', 99) < 11 + bi:
                    return

    def phase_rwkv_scan(self, l):
        k, nc = self.k, self.nc
        NTOK = self.NTOK
        NCH = NTOK // CH
        ncx = LCTX // CH
        order = [list(range(NCH)), list(range(ncx - 1, -1, -1)) + list(range(NCH - 1, ncx - 1, -1))]
        with ExitStack() as st:
            rm = k.sb(st, [64, 4, 64], F32, "rm")
            k.dma("sp", rm[:], self.cd["rmask"], W=[rm])
            class _Half:
                def __init__(self, tt_):
                    self.tt_ = tt_
                    self.tok = tt_.tok

                def __getitem__(self, key):
                    if not isinstance(key, tuple):
                        key = (key,)
                    assert key[0] == slice(None)
                    return self.tt_.t[(slice(0, 64),) + tuple(key[1:])]
            B = [_Half(k.ps(st, [128, 8, 64], F32, "B%d" % i)) for i in range(8)]
            B01 = None
            D_ = []
            for d_ in range(2):
                o = {}
                o["arq"] = [k.sb(st, [64, 8, 4, 64], BF16, "arq") for _ in range(2)]
                o["bh"] = [k.sb(st, [64, 8, 64], BF16, "bh") for _ in range(2)]
                o["kh"] = [k.sb(st, [64, 8, 64], BF16, "kh") for _ in range(2)]
                o["vt"] = [k.sb(st, [64, 8, 64], BF16, "vt") for _ in range(2)]
                o["wc"] = [k.sb(st, [64, 8], F32, "wcl") for _ in range(2)]
                o["NL"] = k.sb(st, [64, 8, 2, 64], BF16, "NL")
                o["KL"] = k.sb(st, [64, 8, 2, 64], BF16, "KL")
                o["Nb"] = [k.sb(st, [64, 8, 64], BF16, "Nb") for _ in range(2)]
                o["Pb"] = [k.sb(st, [64, 8, 64], BF16, "Pb") for _ in range(2)]
                o["TTf"] = k.sb(st, [64, 8, 64], F32, "TTf")
                o["TTb"] = [k.sb(st, [64, 8, 64], BF16, "TTb") for _ in range(2)]
                o["Xb"] = k.sb(st, [64, 8, 64], BF16, "Xb")
                o["Ub"] = k.sb(st, [64, 8, 64], BF16, "Ub")
                o["Ys"] = [k.sb(st, [64, 8, 64], F32, "Ys") for _ in range(2)]
                o["STf"] = k.sb(st, [64, 8, 64], F32, "STf")
                o["STb"] = k.sb(st, [64, 8, 64], BF16, "STb")
                k.memset("dve", o["STf"][:], 0.0, [o["STf"]])
                k.memset("dve", o["STb"][:], 0.0, [o["STb"]])
                D_.append(o)
            idb64 = self.ident_f[0:64, 0:64].unsqueeze(1).broadcast_to([64, 8, 64])

            def load(d_, i):
                o = D_[d_]
                ck = order[d_][i]
                n0 = ck * CH
                s = i % 2
                k.dma("sp", o["arq"][s][:].rearrange("c h q t -> c (h q) t"),
                      self.RA4[d_][:, :, :, n0:n0 + CH].rearrange("c h q t -> c (h q) t"), W=[o["arq"][s]])
                k.dma("sp", o["bh"][s][:], self.TOKB[n0:n0 + CH, :, 0, d_ * 64:(d_ + 1) * 64], W=[o["bh"][s]])
                k.dma("sp", o["kh"][s][:], self.TOKB[n0:n0 + CH, :, 1, d_ * 64:(d_ + 1) * 64], W=[o["kh"][s]])
                k.dma("sp", o["vt"][s][:], self.TOKV[n0:n0 + CH], W=[o["vt"][s]])
                k.dma("sp", o["wc"][s][:], self.RWC[d_][ck], W=[o["wc"][s]])

            for d_ in range(2):
                load(d_, 0)
            for i in range(NCH):
                if i + 1 < NCH:
                    for d_ in range(2):
                        load(d_, i + 1)
                s = i % 2
                for d_ in range(2):
                    o = D_[d_]
                    arq = o["arq"][s]
                    mp = rm[:, 0:2, :] if d_ == 0 else rm[:, 2:4, :]
                    mA = rm[:, 2, :] if d_ == 0 else rm[:, 0, :]
                    mpb = mp.unsqueeze(1).broadcast_to([64, 8, 2, 64])
                    mAb = mA.unsqueeze(1).broadcast_to([64, 8, 64])
                    for h in range(8):
                        bn = B[0] if h < 4 else B[1]
                        k.mm(bn[:, 2 * (h % 4):2 * (h % 4) + 2, :], arq[:, h, 2, :], arq[:, h, 0:2, :], True, True, [arq], [bn], inc=(h % 4 == 3))
                    for h in range(8):
                        kn = B[2] if h < 4 else B[3]
                        k.mm(kn[:, 2 * (h % 4):2 * (h % 4) + 2, :], arq[:, h, 3, :], arq[:, h, 0:2, :], True, True, [arq], [kn], inc=(h % 4 == 3))
                    for h in range(8):
                        k.mm(B[4][:, h, :], arq[:, h, 0, :], arq[:, h, 2, :], True, True, [arq], [B[4]], inc=(h == 7))
                    for hf in range(2):
                        hs = slice(hf * 4, (hf + 1) * 4)
                        k.tt("dve", o["NL"][:, hs, :, :], B[hf][:].rearrange("s (h q) t -> s h q t", q=2), mpb[:, hs], ALU.mult, [B[hf], rm], [o["NL"]], acc=(hf == 1))
                        k.tt("dve", o["KL"][:, hs, :, :], B[2 + hf][:].rearrange("s (h q) t -> s h q t", q=2), mpb[:, hs], ALU.mult, [B[2 + hf], rm], [o["KL"]], acc=(hf == 1))
                    k.tt("dve", o["Pb"][0][:], B[4][:], mAb, ALU.mult, [B[4], rm], [o["Pb"][0]])
                    k.tt("dve", o["TTf"][:], o["NL"][:, :, 0, :], idb64, ALU.add, [o["NL"], self.ident_f], [o["TTf"]])
                    k.tt("dve", o["TTb"][0][:], o["NL"][:, :, 0, :], idb64, ALU.add, [o["NL"], self.ident_f], [o["TTb"][0]])
                for j in range(1, 6):
                    cur, prv = j % 2, (j - 1) % 2
                    for d_ in range(2):
                        o = D_[d_]
                        pN, pP, pT = B[3 * d_], B[3 * d_ + 1], B[3 * d_ + 2]
                        Nprev = (lambda h: o["NL"][:, h, 0, :]) if j == 1 else (lambda h: o["Nb"][prv][:, h, :])
                        Ntok = o["NL"] if j == 1 else o["Nb"][prv]
                        Pprev = o["Pb"][prv]
                        if j < 5:
                            for h in range(8):
                                k.mm(pN[:, h, :], Pprev[:, h, :], Nprev(h), True, True, [Pprev, Ntok], [pN], inc=(h == 7))
                        for h in range(8):
                            k.mm(pP[:, h, :], Nprev(h), Pprev[:, h, :], True, True, [Pprev, Ntok], [pP], inc=(h == 7))
                    for d_ in range(2):
                        o = D_[d_]
                        pN, pP, pT = B[3 * d_], B[3 * d_ + 1], B[3 * d_ + 2]
                        if j < 5:
                            k.cp("act", o["Nb"][cur][:], pN[:], [pN], [o["Nb"][cur]])
                        k.cp("act", o["Pb"][cur][:], pP[:], [pP], [o["Pb"][cur]])
                    for d_ in range(2):
                        o = D_[d_]
                        pT = B[3 * d_ + 2]
                        for h in range(8):
                            k.mm(pT[:, h, :], o["Pb"][cur][:, h, :], o["TTb"][prv][:, h, :], True, True, [o["Pb"][cur], o["TTb"][prv]], [pT], inc=(h == 7))
                    for d_ in range(2):
                        o = D_[d_]
                        pT = B[3 * d_ + 2]
                        k.tt("dve", o["TTf"][:], o["TTf"][:], pT[:], ALU.add, [o["TTf"], pT], [o["TTf"]])
                        k.cp("act", o["TTb"][cur][:], o["TTf"][:], [o["TTf"]], [o["TTb"][cur]])
                TTfin = 5 % 2
                for d_ in range(2):
                    o = D_[d_]
                    pX = B[4 * d_]
                    arq = o["arq"][s]
                    for h in range(8):
                        k.mm(pX[:, h, :], arq[:, h, 0, :], o["STb"][:, h, :], True, False, [arq, o["STb"]], [pX], inc=False)
                        k.mm(pX[:, h, :], o["KL"][:, h, 0, :], o["vt"][s][:, h, :], False, True, [o["KL"], o["vt"][s]], [pX], inc=(h == 7))
                for d_ in range(2):
                    o = D_[d_]
                    k.cp("act", o["Xb"][:], B[4 * d_][:], [B[4 * d_]], [o["Xb"]])
                for d_ in range(2):
                    o = D_[d_]
                    pU = B[4 * d_ + 1]
                    for h in range(8):
                        k.mm(pU[:, h, :], o["TTb"][TTfin][:, h, :], o["Xb"][:, h, :], True, True, [o["TTb"][TTfin], o["Xb"]], [pU], inc=(h == 7))
                for d_ in range(2):
                    o = D_[d_]
                    k.cp("dve", o["Ub"][:], B[4 * d_ + 1][:], [B[4 * d_ + 1]], [o["Ub"]])
                for d_ in range(2):
                    o = D_[d_]
                    pY, pS = B[4 * d_ + 2], B[4 * d_ + 3]
                    arq = o["arq"][s]
                    for h in range(8):
                        k.mm(pS[:, h, :], o["bh"][s][:, h, :], o["Ub"][:, h, :], True, False, [o["bh"][s], o["Ub"]], [pS], inc=False)
                        k.mm(pS[:, h, :], o["kh"][s][:, h, :], o["vt"][s][:, h, :], False, True, [o["kh"][s], o["vt"][s]], [pS], inc=(h == 7))
                    for h in range(8):
                        k.mm(pY[:, h, :], o["STb"][:, h, :], arq[:, h, 1, :], True, False, [o["STb"], arq], [pY], inc=False)
                        k.mm(pY[:, h, :], o["Ub"][:, h, :], o["NL"][:, h, 1, :], False, False, [o["Ub"], o["NL"]], [pY], inc=False)
                        k.mm(pY[:, h, :], o["vt"][s][:, h, :], o["KL"][:, h, 1, :], False, True, [o["vt"][s], o["KL"]], [pY], inc=(h == 7))
                for d_ in range(2):
                    o = D_[d_]
                    pY, pS = B[4 * d_ + 2], B[4 * d_ + 3]
                    ck = order[d_][i]
                    k.tt("dve", o["STf"][:], o["STf"][:], o["wc"][s][:].unsqueeze(2).broadcast_to([64, 8, 64]), ALU.mult, [o["STf"], o["wc"][s]], [o["STf"]])
                    k.tt("dve", o["STf"][:], o["STf"][:], pS[:], ALU.add, [o["STf"], pS], [o["STf"]])
                    k.cp("act", o["STb"][:], o["STf"][:], [o["STf"]], [o["STb"]])
                    ys = o["Ys"][s]
                    k.cp("act", ys[:], pY[:], [pY], [ys])
                    k.dma("pool", self.YD[d_][:, :, ck * CH:(ck + 1) * CH], ys[:], R=[ys])

    def phase_rwkv_post(self, l):
        k, nc = self.k, self.nc
        NTOK = self.NTOK
        PB = 256
        with ExitStack() as st:
            T_ = lambda nm, dt=F32: k.sb(st, [64, 8, PB], dt, nm)
            y0, y1, yc, sq, rs_, tmp, bvt, gt = T_("y0"), T_("y1"), T_("yc"), T_("sq"), T_("rs"), T_("tmp"), T_("bvt"), T_("gt")
            ob = [T_("orw", BF16) for _ in range(2)]
            pu_full = [k.ps(st, [128, 4, PB], F32, "ppu") for _ in range(4)]

            class _H2:
                def __init__(self, tt_):
                    self.tt_ = tt_
                    self.tok = tt_.tok

                def __getitem__(self, key):
                    if not isinstance(key, tuple):
                        key = (key,)
                    return self.tt_.t[(slice(0, 64),) + tuple(key[1:])]
            pu = [_H2(x_) for x_ in pu_full]
            ones64 = self.ones_f[0:64, 0:64]
            lnw = self.spc(l, "ln_w", rows=64)
            for bi in range(NTOK // PB):
                n0 = bi * PB
                k.dma("sp", y0[:], self.YD[0][:, :, n0:n0 + PB], W=[y0])
                k.dma("sp", y1[:], self.YD[1][:, :, n0:n0 + PB], W=[y1])
                k.dma("sp", bvt[:], self.RBV[:, :, n0:n0 + PB], W=[bvt])
                k.dma("sp", gt[:], self.RG[:, :, n0:n0 + PB], W=[gt])
                k.tt("dve", y0[:], y0[:], y1[:], ALU.add, [y0, y1], [y0])
                for hf in range(2):
                    hs = slice(hf * 4, (hf + 1) * 4)
                    pb = pu[hf]
                    for hh in range(4):
                        k.mm(pb[:, hh, :], ones64, y0[:, hf * 4 + hh, :], True, True, [self.ones_f, y0], [pb], inc=(hh == 3))
                    k.stt(yc[:, hs, :], pb[:], -1.0 / 64, y0[:, hs, :], ALU.mult, ALU.add, [pb, y0], [yc], acc=(hf == 1))
                k.act(sq[:], yc[:], AF.Square, [yc], [sq])
                for hf in range(2):
                    hs = slice(hf * 4, (hf + 1) * 4)
                    pb = pu[2 + hf]
                    for hh in range(4):
                        k.mm(pb[:, hh, :], ones64, sq[:, hf * 4 + hh, :], True, True, [self.ones_f, sq], [pb], inc=(hh == 3))
                    k.act(rs_[:, hs, :], pb[:], AF.Ln, [pb, self.epsG], [rs_], bias=self.epsG[0:64], scale=1.0 / 64, acc=(hf == 1))
                k.act(rs_[:], rs_[:], AF.Exp, [rs_], [rs_], scale=-0.5)
                for h in range(8):
                    k.stt(tmp[:, h, :], yc[:, h, :], lnw[:, h:h + 1], rs_[:, h, :], ALU.mult, ALU.mult, [yc, rs_, self.smallp], [tmp], acc=(h > 0))
                k.tt("dve", tmp[:], tmp[:], bvt[:], ALU.add, [tmp, bvt], [tmp])
                o_ = ob[bi % 2]
                k.tt("dve", o_[:], tmp[:], gt[:], ALU.mult, [tmp, gt], [o_])
                k.dma("pool", self.ORT[:, :, n0:n0 + PB], o_[:], R=[o_])

    def phase_merge(self, l, last):
        k, nc = self.k, self.nc
        XTv = self.XT.rearrange("k p n -> p k n")
        HTv = self.HT.rearrange("k p n -> p k n")
        OATv = self.OAT.rearrange("c p n -> p c n")
        OSTv = self.OST.rearrange("g p n -> p g n")
        with ExitStack() as st:
            wG = k.sb(st, [128, 8, 3072], BF16, "wG")
            for kc in range(8):
                k.dma("sp", wG[:, kc, :], self.WIN[l][kc * 128:(kc + 1) * 128, 3584:6656], W=[wG], acc=(kc > 0))
            wOA = k.sb(st, [128, 4, D], BF16, "wOA")
            wOS = k.sb(st, [128, 4, D], BF16, "wOS")
            wOR = k.sb(st, [64, 8, D], BF16, "wOR")
            wOU = k.sb(st, [128, 8, D], BF16, "wOU")
            k.dma("sp", wOA[:], self.WOA[l].rearrange("(k p) f -> p k f", p=128), W=[wOA])
            k.dma("sp", wOS[:], self.WOS[l].rearrange("(k p) f -> p k f", p=128), W=[wOS])
            k.dma("sp", wOR[:], self.WOR[l].rearrange("(h c) f -> c h f", c=64), W=[wOR])
            k.dma("sp", wOU[:], self.WOUT[l].rearrange("(k p) f -> p k f", p=128), W=[wOU])
            NB = 512
            xT = k.sb(st, [128, 8, NB], F32, "mxT")
            hT = [k.sb(st, [128, 8, NB], BF16, "mhT") for _ in range(2)]
            oaT = [k.sb(st, [128, 4, NB], BF16, "moa") for _ in range(2)]
            osT = [k.sb(st, [128, 4, NB], BF16, "mos") for _ in range(2)]
            orT = [k.sb(st, [64, 8, NB], BF16, "mor") for _ in range(2)]
            mT = k.sb(st, [128, 8, NB], BF16, "mmT")
            gs = [k.sb(st, [128, NB], F32, "mgs") for _ in range(3)]
            acc_ = k.sb(st, [128, NB], F32, "macc")
            tmp = k.sb(st, [128, NB], F32, "mtmp")
            pg = [k.ps(st, [128, NB], F32, "mpg") for _ in range(3)]
            pp = [k.ps(st, [128, NB], F32, "mpp") for _ in range(3)]
            px = [k.ps(st, [128, NB], F32, "mpx") for _ in range(2)]
            blocks = [b for b in self.blocks if not (last and b[2])]
            for bi, blk in enumerate(blocks):
                n0, nb, is_ctx, t0 = blk
                m = 1 if is_ctx else 0
                s = bi % 2
                k.dma("sp", hT[s][:, :, 0:nb], HTv[:, :, n0:n0 + nb], W=[hT[s]])
                k.dma("sp", oaT[s][:, :, 0:nb], OATv[:, :, n0:n0 + nb], W=[oaT[s]])
                k.dma("sp", osT[s][:, :, 0:nb], OSTv[:, :, n0:n0 + nb], W=[osT[s]])
                k.dma("sp", orT[s][:, :, 0:nb], self.ORT[:, :, n0:n0 + nb], W=[orT[s]])
                k.dma("sp", xT[:, :, 0:nb], XTv[:, :, n0:n0 + nb], W=[xT])
                for fc in range(8):
                    fs = slice(fc * 128, (fc + 1) * 128)
                    for b in range(3):
                        for kc in range(8):
                            k.mm(pg[b][:, 0:nb], wG[:, kc, b * 1024 + fc * 128:b * 1024 + (fc + 1) * 128], hT[s][:, kc, 0:nb], kc == 0, kc == 7, [wG, hT[s]], [pg[b]])
                        k.act(gs[b][:, 0:nb], pg[b][:, 0:nb], AF.Sigmoid, [pg[b]], [gs[b]])
                    for kc in range(4):
                        k.mm(pp[0][:, 0:nb], wOA[:, kc, fs], oaT[s][:, kc, 0:nb], kc == 0, kc == 3, [wOA, oaT[s]], [pp[0]])
                    for h in range(8):
                        k.mm(pp[1][:, 0:nb], wOR[:, h, fs], orT[s][:, h, 0:nb], h == 0, h == 7, [wOR, orT[s]], [pp[1]])
                    for kc in range(4):
                        k.mm(pp[2][:, 0:nb], wOS[:, kc, fs], osT[s][:, kc, 0:nb], kc == 0, kc == 3, [wOS, osT[s]], [pp[2]])
                    k.tt("dve", acc_[:, 0:nb], gs[0][:, 0:nb], pp[0][:, 0:nb], ALU.mult, [gs[0], pp[0]], [acc_])
                    k.tt("dve", tmp[:, 0:nb], gs[1][:, 0:nb], pp[1][:, 0:nb], ALU.mult, [gs[1], pp[1]], [tmp])
                    k.tt("dve", acc_[:, 0:nb], acc_[:, 0:nb], tmp[:, 0:nb], ALU.add, [acc_, tmp], [acc_])
                    k.tt("dve", tmp[:, 0:nb], gs[2][:, 0:nb], pp[2][:, 0:nb], ALU.mult, [gs[2], pp[2]], [tmp])
                    k.tt("dve", mT[:, fc, 0:nb], acc_[:, 0:nb], tmp[:, 0:nb], ALU.add, [acc_, tmp], [mT], acc=(fc > 0))
                for fc in range(8):
                    fs = slice(fc * 128, (fc + 1) * 128)
                    p_ = px[fc % 2]
                    for kc in range(8):
                        k.mm(p_[:, 0:nb], wOU[:, kc, fs], mT[:, kc, 0:nb], kc == 0, kc == 7, [wOU, mT], [p_])
                    k.stt(xT[:, fc, 0:nb], p_[:, 0:nb], self.mods[:, m, 2, fc:fc + 1], xT[:, fc, 0:nb], ALU.mult, ALU.add, [p_, self.mods, xT], [xT])
                k.dma("pool", XTv[:, :, n0:n0 + nb], xT[:, :, 0:nb], R=[xT])

    def phase_ffn(self, l, last):
        k, nc = self.k, self.nc
        XTv = self.XT.rearrange("k p n -> p k n")
        NB = 256
        with ExitStack() as st:
            w1 = k.sb(st, [128, 8, FFN], BF16, "w1")
            w3 = k.sb(st, [128, 8, FFN], BF16, "w3")
            w2 = k.sb(st, [128, NFC, D], BF16, "w2")
            for kc in range(8):
                k.dma("sp", w1[:, kc, :], self.W1[l][kc * 128:(kc + 1) * 128, :], W=[w1], acc=(kc > 0))
                k.dma("sp", w3[:, kc, :], self.W3[l][kc * 128:(kc + 1) * 128, :], W=[w3], acc=(kc > 0))
            for hc in range(NFC):
                k.dma("sp", w2[:, hc, :], self.W2[l][hc * 128:(hc + 1) * 128, :], W=[w2], acc=(hc > 0))
            xTs = [k.sb(st, [128, 8, NB], F32, "fxT") for _ in range(2)]
            work = k.sb(st, [128, 8, NB], F32, "fwork")
            hTs = [k.sb(st, [128, 8, NB], BF16, "fhT") for _ in range(2)]
            rstd = k.sb(st, [128, NB], F32, "frstd")
            hid = k.sb(st, [128, NFC, NB], BF16, "fhid")
            sl = [k.sb(st, [128, NB], F32, "fsl") for _ in range(2)]
            pss = k.ps(st, [128, 512], F32, "fpss")
            pa = [k.ps(st, [128, 512], F32, "fpa") for _ in range(2)]
            pb_ = [k.ps(st, [128, 512], F32, "fpb") for _ in range(2)]
            po = [k.ps(st, [128, 512], F32, "fpo") for _ in range(2)]
            if last:
                ptr = k.ps(st, [128, 512], F32, "fptr")
                xo = [k.sb(st, [128, D], F32, "fxo") for _ in range(1)]
            nblk = self.NTOK // NB
            blist = [bi for bi in range(nblk) if not (last and bi * NB < LCTX)]

            def do_norm(ii):
                bi_ = blist[ii]
                n0_ = bi_ * NB
                xT_ = xTs[ii % 2]
                k.dma("sp", xT_[:], XTv[:, :, n0_:n0_ + NB], W=[xT_])
                self.norm_mod((xT_, work, hTs[ii % 2], rstd, pss), (n0_, NB, n0_ < LCTX, 0), l, 1)

            do_norm(0)
            for ii, bi in enumerate(blist):
                n0 = bi * NB
                is_ctx = n0 < LCTX
                m = 1 if is_ctx else 0
                xT = xTs[ii % 2]
                hT = hTs[ii % 2]
                if ii + 1 < len(blist):
                    do_norm(ii + 1)
                for hc in range(NFC):
                    hs = slice(hc * 128, (hc + 1) * 128)
                    a_, b_ = pa[hc % 2], pb_[hc % 2]
                    for kc in range(8):
                        k.mm(a_[:, 0:NB], w1[:, kc, hs], hT[:, kc, :], kc == 0, kc == 7, [w1, hT], [a_])
                    for kc in range(8):
                        k.mm(b_[:, 0:NB], w3[:, kc, hs], hT[:, kc, :], kc == 0, kc == 7, [w3, hT], [b_])
                    s_ = sl[hc % 2]
                    k.act(s_[:], a_[:, 0:NB], AF.Silu, [a_], [s_])
                    k.tt("dve", hid[:, hc, :], s_[:], b_[:, 0:NB], ALU.mult, [s_, b_], [hid], acc=(hc > 0))
                for fc in range(8):
                    fs = slice(fc * 128, (fc + 1) * 128)
                    p_ = po[fc % 2]
                    for hc in range(NFC):
                        k.mm(p_[:, 0:NB], w2[:, hc, fs], hid[:, hc, :], hc == 0, hc == NFC - 1, [w2, hid], [p_])
                    k.stt(xT[:, fc, :], p_[:, 0:NB], self.mods[:, m, 5, fc:fc + 1], xT[:, fc, :], ALU.mult, ALU.add, [p_, self.mods, xT], [xT])
                if not last:
                    k.dma("pool", XTv[:, :, n0:n0 + NB], xT[:], R=[xT])
                else:
                    for tt_ in range(NB // 128):
                        xo_ = xo[0]
                        for half in range(2):
                            for j in range(4):
                                kc = half * 4 + j
                                k.tr(ptr[:, j * 128:(j + 1) * 128], xT[:, kc, tt_ * 128:(tt_ + 1) * 128], self.ident_f[:], [xT, self.ident_f], [ptr], inc=(j == 3))
                            k.cp("act" if half == 0 else "dve", xo_[:, half * 512:(half + 1) * 512], ptr[:], [ptr], [xo_], acc=(half == 1))
                        t0 = n0 - LCTX + tt_ * 128
                        k.dma("pool", self.out[t0:t0 + 128, :], xo_[:], R=[xo_])


_SMALL_NAMES = ("b_mod", "norm_mix", "norm_ffn", "q_gain", "k_gain", "attn_sink", "rwkv_conv", "rwkv_w0", "rwkv_a0",
                "rwkv_k_k", "rwkv_k_a", "rwkv_r_k", "rwkv_ln_w", "rwkv_ln_b", "rwkv_vres_b", "sgu_ln_w", "sgu_ln_b", "sgu_b")


def make_in_maps(inputs, T, ncores):
    inp = {k_: np.asarray(v) for k_, v in inputs.items()}
    hc = host_consts(T)
    smallp = np.stack([pack_small(inp, l) for l in range(DEPTH)]).astype(np.float32)
    shared = {"smallp": smallp}
    for name, arr in hc.items():
        shared["c_" + name] = np.ascontiguousarray(arr, dtype=np.float32)
    for name, shape in PARAMS:
        if name in _SMALL_NAMES:
            continue
        shared[name] = np.ascontiguousarray(inp[name], dtype=np.float32)
    maps = []
    for b in range(ncores):
        m = dict(shared)
        m["x"] = np.ascontiguousarray(inp["x"][b, :T], dtype=np.float32)
        m["ctx"] = np.ascontiguousarray(inp["ctx"][b], dtype=np.float32)
        cf = np.concatenate([inp["c"][b].reshape(8, 128).T, inp["c_ctx"].reshape(8, 128).T], axis=1)
        m["cfm"] = np.ascontiguousarray(cf, dtype=np.float32)
        maps.append(m)
    return maps


def kernel(**inputs):
    T = 4096
    B = 8
    prog = Prog(T)
    nc = prog.build()
    maps = make_in_maps(inputs, T, B)
    res = run_bass_kernel_spmd(nc, maps, core_ids=list(range(B)))
    out = np.stack([np.asarray(res.results[b]["out"], dtype=np.float32) for b in range(B)])
    return out
```

```python
import numpy as np
import ml_dtypes
from contextlib import ExitStack
import concourse.bass as bass
import concourse.mybir as mybir
from concourse.bass_utils import run_bass_kernel_spmd

F32 = mybir.dt.float32
BF16 = mybir.dt.bfloat16
AF = mybir.ActivationFunctionType
ALU = mybir.AluOpType
AX = mybir.AxisListType

D = 1024
DEPTH = 2
LCTX = 256
NH = 8
HD = 64
IN_DIM = 6656
FFN = 2816
NFC = FFN // 128
EPS = 1e-6
GN_EPS = 64e-5
CH = 64
NEG = -30000.0


class Tok:
    __slots__ = ("w", "r", "pw", "pr")

    def __init__(self):
        self.w = []
        self.r = {}
        self.pw = []
        self.pr = {}


class TT:
    def __init__(self, t):
        self.t = t
        self.tok = Tok()

    def __getitem__(self, k):
        return self.t[k]


class KB:
    def __init__(self, nc, es):
        self.nc = nc
        self.es = es
        self.E = dict(pe=nc.tensor, dve=nc.vector, act=nc.scalar, pool=nc.gpsimd, sp=nc.sync)
        self.semobj = {}
        self.cnt = {}
        for k in self.E:
            self.semobj[k] = es.enter_context(nc.semaphore("c_" + k))
            self.cnt[k] = 0
        self.known = {k: {} for k in self.E}
        self.NDS = 8
        self.dq = {}
        for q in ("sp", "pool", "act"):
            names = []
            for i in range(self.NDS):
                nm = "d_%s%d" % (q, i)
                self.semobj[nm] = es.enter_context(nc.semaphore(nm))
                self.cnt[nm] = 0
                names.append(nm)
            self.dq[q] = [names, 0]
        self.uid = 0

    def sb(self, st, shape, dt, name=None):
        self.uid += 1
        return TT(st.enter_context(self.nc.sbuf_tensor("%s_%d" % (name or "t", self.uid), list(shape), dt)))

    def ps(self, st, shape, dt=F32, name=None):
        self.uid += 1
        return TT(st.enter_context(self.nc.psum_tensor("%s_%d" % (name or "p", self.uid), list(shape), dt)))

    def _wait(self, eng, ev):
        s, v = ev
        if v <= 0:
            return
        if self.known[eng].get(s, 0) >= v:
            return
        self.E[eng].wait_ge(self.semobj[s], v)
        self.known[eng][s] = v

    def _deps(self, eng, R, W, acc=False):
        for t in R:
            tk = t.tok if hasattr(t, 'tok') else t
            for ev in tk.w:
                if not (eng == "pe" and ev[0] == "pe"):
                    self._wait(eng, ev)
        for t in W:
            tk = t.tok if hasattr(t, 'tok') else t
            for ev in (tk.pw if acc else tk.w):
                if not (eng == "pe" and ev[0] == "pe"):
                    self._wait(eng, ev)
            rr = list(tk.r.items()) + (list(tk.pr.items()) if acc else [])
            for s, v in rr:
                if eng == "pe" and s == "pe":
                    continue
                self._wait(eng, (s, v))

    def _record(self, ev, R, W, acc=False):
        for t in R:
            tk = t.tok if hasattr(t, 'tok') else t
            if tk.r.get(ev[0], 0) < ev[1]:
                tk.r[ev[0]] = ev[1]
        for t in W:
            tk = t.tok if hasattr(t, 'tok') else t
            if acc:
                tk.w = [e for e in tk.w if e[0] != ev[0]] + [ev]
            else:
                tk.pw = tk.w
                tk.pr = tk.r
                tk.w = [ev]
                tk.r = {}

    def op(self, eng, fn, R=(), W=(), inc=True, acc=False):
        self._deps(eng, R, W, acc)
        ins = fn()
        ev = (eng, self.cnt[eng] + 1)
        if inc:
            ins.then_inc(self.semobj[eng], 1)
            self.cnt[eng] += 1
        self._record(ev, R, W, acc)
        return ins

    def dma(self, q, out, in_, R=(), W=(), acc=False, **kw):
        self._deps(q, R, W, acc)
        names, i = self.dq[q]
        nm = names[i % self.NDS]
        self.dq[q][1] = i + 1
        self._wait(q, (nm, self.cnt[nm]))
        self.E[q].dma_start(out=out, in_=in_, **kw).then_inc(self.semobj[nm], 16)
        self.cnt[nm] += 16
        self._record((nm, self.cnt[nm]), R, W, acc)

    def barrier(self):
        for e in self.E:
            for s, v in self.cnt.items():
                if s == e and e != "pe":
                    pass
                self._wait(e, (s, v))

    def mm(self, out, lhsT, rhs, start, stop, R, W, inc=None):
        if inc is None:
            inc = stop
        return self.op("pe", lambda: self.nc.tensor.matmul(out, lhsT=lhsT, rhs=rhs, start=start, stop=stop), R, W, inc)

    def tr(self, out, in_, ident, R, W, inc=True):
        return self.op("pe", lambda: self.nc.tensor.transpose(out, in_, ident), R, W, inc)

    def act(self, out, in_, func, R, W, bias=None, scale=None, accum=None, acc=False):
        kw = {}
        if bias is not None:
            kw["bias"] = bias
        if scale is not None:
            kw["scale"] = scale
        if accum is not None:
            kw["accum_out"] = accum
        return self.op("act", lambda: self.nc.scalar.activation(out=out, in_=in_, func=func, **kw), R, W, acc=acc)

    def tt(self, eng, out, in0, in1, op, R, W, acc=False):
        e = self.E[eng]
        return self.op(eng, lambda: e.tensor_tensor(out=out, in0=in0, in1=in1, op=op), R, W, acc=acc)

    def ts(self, eng, out, in0, s1, op0, R, W, s2=None, op1=None, accum=None, acc=False):
        e = self.E[eng]
        kw = {}
        if op1 is not None:
            kw["op1"] = op1
        if accum is not None:
            kw["accum_out"] = accum
        return self.op(eng, lambda: e.tensor_scalar(out=out, in0=in0, scalar1=s1, scalar2=s2, op0=op0, **kw), R, W, acc=acc)

    def stt(self, out, in0, scalar, in1, op0, op1, R, W, acc=False):
        return self.op("dve", lambda: self.nc.vector.scalar_tensor_tensor(out=out, in0=in0, scalar=scalar, in1=in1, op0=op0, op1=op1), R, W, acc=acc)

    def cp(self, eng, out, in_, R, W, acc=False):
        e = self.E[eng]
        if eng == "act":
            return self.op(eng, lambda: e.copy(out=out, in_=in_), R, W, acc=acc)
        return self.op(eng, lambda: e.tensor_copy(out=out, in_=in_), R, W, acc=acc)

    def recip(self, out, in_, R, W):
        return self.op("dve", lambda: self.nc.vector.reciprocal(out=out, in_=in_), R, W)

    def memset(self, eng, ap, val, W):
        e = self.E[eng]
        return self.op(eng, lambda: e.memset(ap, val), (), W)


def host_consts(T):
    c = {}
    c["ident_f"] = np.eye(128, dtype=np.float32)
    ob = np.zeros((128, 128), np.float32)
    ob[:64, :64] = 1.0
    ob[64:, 64:] = 1.0
    c["onesblk"] = ob
    c["ones_f"] = np.ones((128, 128), np.float32)
    rm = np.zeros((128, 128), np.float32)
    for h in range(2):
        for d in range(64):
            if d < 32:
                rm[h * 64 + d + 32, h * 64 + d] = -1.0
            else:
                rm[h * 64 + d - 32, h * 64 + d] = 1.0
    c["rotm"] = rm
    rows = T // 64
    row = np.repeat(np.arange(rows), 64).astype(np.float32)
    col = np.tile(np.arange(64), rows).astype(np.float32)
    inv = (10000.0 ** (-np.arange(16, dtype=np.float32) / 16)).astype(np.float32)
    ang = np.concatenate([row[:, None] * inv, col[:, None] * inv], -1)
    cos = np.cos(ang).astype(np.float32).T
    sin = np.sin(ang).astype(np.float32).T
    c["ropecos"] = np.ascontiguousarray(np.tile(cos, (4, 1)))
    c["ropesin"] = np.ascontiguousarray(np.tile(sin, (4, 1)))
    j = np.arange(128)[:, None]
    p = np.arange(128)[None, :]
    mprev = np.where(j >= p, 0.0, NEG).astype(np.float32)
    mnext = np.where(j <= p, 0.0, NEG).astype(np.float32)
    c["maskprev"] = np.ascontiguousarray(np.tile(mprev, (1, 4)))
    c["masknext"] = np.ascontiguousarray(np.tile(mnext, (1, 4)))
    s = np.arange(64)[:, None]
    t = np.arange(64)[None, :]
    m = np.zeros((64, 4, 64), np.float32)
    m[:, 0, :] = (s < t)
    m[:, 1, :] = (s <= t)
    m[:, 2, :] = (s > t)
    m[:, 3, :] = (s >= t)
    c["rmask"] = m
    rs = np.ones((128, 8, 128), np.float32)
    rs[:, :, 0::64] = 0.0
    c["scanreset"] = rs
    return c


CONST_DT = {"ident_f": F32, "onesblk": F32, "ones_f": F32, "rotm": F32, "ropecos": F32, "ropesin": F32,
            "maskprev": F32, "masknext": F32, "rmask": F32, "scanreset": F32}

PARAMS = [
    ("w_mod", (DEPTH, D, 6 * D)), ("b_mod", (DEPTH, 6 * D)), ("norm_mix", (DEPTH, D)), ("norm_ffn", (DEPTH, D)),
    ("w_in", (DEPTH, D, IN_DIM)), ("q_gain", (DEPTH, 64)), ("k_gain", (DEPTH, 64)), ("attn_sink", (DEPTH, 8)),
    ("rwkv_conv", (DEPTH, 3, 1792)), ("rwkv_w0", (DEPTH, 2, 512)), ("rwkv_w_up", (DEPTH, 2, 64, 512)),
    ("rwkv_a0", (DEPTH, 2, 512)), ("rwkv_a_up", (DEPTH, 2, 64, 512)), ("rwkv_k_k", (DEPTH, 512)),
    ("rwkv_k_a", (DEPTH, 512)), ("rwkv_r_k", (DEPTH, 8, 64)), ("rwkv_g_up", (DEPTH, 128, 512)),
    ("rwkv_ln_w", (DEPTH, 512)), ("rwkv_ln_b", (DEPTH, 512)), ("rwkv_vres_down", (DEPTH - 1, D, 32)),
    ("rwkv_vres_up", (DEPTH - 1, 32, 512)), ("rwkv_vres_b", (DEPTH - 1, 512)), ("sgu_ln_w", (DEPTH, 512)),
    ("sgu_ln_b", (DEPTH, 512)), ("sgu_w", (DEPTH, 4, 128, 128)), ("sgu_b", (DEPTH, 4, 128)),
    ("w_o_attn", (DEPTH, 512, D)), ("w_o_rwkv", (DEPTH, 512, D)), ("w_o_sgu", (DEPTH, 512, D)),
    ("w_out", (DEPTH, D, D)), ("ffn_w1", (DEPTH, D, FFN)), ("ffn_w3", (DEPTH, D, FFN)), ("ffn_w2", (DEPTH, FFN, D)),
]


SP_LAYOUT = [("bmod", 48), ("nmix", 8), ("nffn", 8), ("gain", 2), ("sink", 8), ("conv_rkv", 72), ("conv_wa", 6),
             ("conv_g", 3), ("w0", 8), ("a0", 8), ("k_k", 8), ("k_a", 8), ("r_k", 8), ("ln_w", 8), ("ln_b", 8),
             ("vres_b", 8), ("sgu_lnw", 512), ("sgu_lnb", 512), ("sgu_b", 512)]
SP_OFF = {}
_o = 0
for _n, _w in SP_LAYOUT:
    SP_OFF[_n] = (_o, _w)
    _o += _w
SP_N = _o


def pack_small(inp, l):
    a = np.zeros((128, SP_N), np.float32)

    def put(name, arr):
        o, w = SP_OFF[name]
        arr = np.asarray(arr, np.float32)
        a[: arr.shape[0], o:o + w] = arr.reshape(arr.shape[0], -1)

    put("bmod", inp["b_mod"][l].reshape(48, 128).T)
    put("nmix", inp["norm_mix"][l].reshape(8, 128).T)
    put("nffn", inp["norm_ffn"][l].reshape(8, 128).T)
    put("gain", np.stack([np.tile(inp["q_gain"][l], 2), np.tile(inp["k_gain"][l], 2)], 1))
    put("sink", np.tile(inp["attn_sink"][l][None, :], (128, 1)))
    cv = inp["rwkv_conv"][l]
    put("conv_rkv", cv[:, :1536].reshape(3, 3, 8, 64).transpose(3, 0, 1, 2))
    put("conv_wa", cv[:, 1536:1664].reshape(3, 2, 64).transpose(2, 0, 1))
    put("conv_g", cv[:, 1664:1792].T)
    dup = lambda a_: np.concatenate([a_, a_], 0)
    put("w0", inp["rwkv_w0"][l].reshape(2, 8, 64).transpose(0, 2, 1).reshape(128, 8))
    put("a0", inp["rwkv_a0"][l].reshape(2, 8, 64).transpose(0, 2, 1).reshape(128, 8))
    put("k_k", dup(inp["rwkv_k_k"][l].reshape(8, 64).T))
    put("k_a", dup(inp["rwkv_k_a"][l].reshape(8, 64).T))
    put("r_k", dup(inp["rwkv_r_k"][l].reshape(8, 64).T))
    put("ln_w", inp["rwkv_ln_w"][l].reshape(8, 64).T)
    put("ln_b", inp["rwkv_ln_b"][l].reshape(8, 64).T)
    if l > 0:
        put("vres_b", dup(inp["rwkv_vres_b"][l - 1].reshape(8, 64).T))
    put("sgu_lnw", np.tile(inp["sgu_ln_w"][l][None, :], (128, 1)))
    put("sgu_lnb", np.tile(inp["sgu_ln_b"][l][None, :], (128, 1)))
    put("sgu_b", inp["sgu_b"][l].reshape(1, 512))
    return a


class Prog:
    def __init__(self, T, nlayers=DEPTH, dbg=(), stop_after=None):
        self.T = T
        self.NTOK = LCTX + T
        self.nlayers = nlayers
        self.dbg = set(dbg)
        self.stop_after = stop_after
        self.blocks = [(0, LCTX, True, 0)] + [(LCTX + i * 512, 512, False, i * 512) for i in range(T // 512)]
        self.nc = bass.Bass("TRN2", target_bir_lowering=False)
        self.es = ExitStack()

    def din(self, name, shape, dt=F32):
        return self.nc.dram_tensor(name, list(shape), dt, kind="ExternalInput").ap()

    def scr(self, name, shape, dt):
        kind = "ExternalOutput" if name in self.dbg else "Internal"
        return self.nc.dram_tensor(name, list(shape), dt, kind=kind).ap()

    def build(self):
        nc, T, NTOK = self.nc, self.T, self.NTOK
        with self.es as es:
            k = self.k = KB(nc, es)
            self.x_in = self.din("x", [T, D])
            self.ctx_in = self.din("ctx", [LCTX, D])
            self.cfm = self.din("cfm", [128, 16])
            self.smallp_d = self.din("smallp", [DEPTH, 128, SP_N])
            self.cd = {}
            hc = host_consts(T)
            for name, arr in hc.items():
                self.cd[name] = self.din("c_" + name, arr.shape)
            self.pd = {}
            for name, shape in PARAMS:
                if name in ("b_mod", "norm_mix", "norm_ffn", "q_gain", "k_gain", "attn_sink", "rwkv_conv", "rwkv_w0",
                            "rwkv_a0", "rwkv_k_k", "rwkv_k_a", "rwkv_r_k", "rwkv_ln_w", "rwkv_ln_b", "rwkv_vres_b",
                            "sgu_ln_w", "sgu_ln_b", "sgu_b"):
                    continue
                self.pd[name] = self.din(name, shape)
            self.out = nc.dram_tensor("out", [T, D], F32, kind="ExternalOutput").ap()
            L = self.nlayers
            self.XT = self.scr("XT", [8, 128, NTOK], F32)
            self.HT = self.scr("HT", [8, 128, NTOK], BF16)
            self.QT = self.scr("QT", [4, 128, NTOK], BF16)
            self.KT = self.scr("KT", [128, NTOK], BF16)
            self.VA = self.scr("VA", [NTOK, 130], BF16)
            self.PRW = self.scr("PRW", [1792, NTOK], F32)
            self.PRB = self.scr("PRB", [1536, NTOK], BF16)
            self.OST = self.scr("OST", [4, 128, NTOK], BF16)
            self.OAT = self.scr("OAT", [4, 128, NTOK], BF16)
            self.ORT = self.scr("ORT", [64, 8, NTOK], BF16)
            NCH = NTOK // CH
            self.RA4 = [self.scr("RA4_%d" % d_, [64, 8, 4, NTOK], BF16) for d_ in range(2)]
            self.TOKB = self.scr("TOKB", [NTOK, 8, 2, 128], BF16)
            self.TOKV = self.scr("TOKV", [NTOK, 8, 64], BF16)
            self.RWC = [self.scr("RWC_%d" % d_, [NCH, 64, 8], F32) for d_ in range(2)]
            self.RBV = self.scr("RBV", [64, 8, NTOK], F32)
            self.RG = self.scr("RG", [64, 8, NTOK], F32)
            self.VF = self.scr("VF", [64, 8, NTOK], F32)
            self.YD = [self.scr("YD_%d" % d_, [64, 8, NTOK], F32) for d_ in range(2)]
            self.WIN = [self.scr("WIN%d" % l, [D, IN_DIM], BF16) for l in range(L)]
            self.WOA = [self.scr("WOA%d" % l, [512, D], BF16) for l in range(L)]
            self.WOR = [self.scr("WOR%d" % l, [512, D], BF16) for l in range(L)]
            self.WOS = [self.scr("WOS%d" % l, [512, D], BF16) for l in range(L)]
            self.WOUT = [self.scr("WOUT%d" % l, [D, D], BF16) for l in range(L)]
            self.W1 = [self.scr("W1_%d" % l, [D, FFN], BF16) for l in range(L)]
            self.W3 = [self.scr("W3_%d" % l, [D, FFN], BF16) for l in range(L)]
            self.W2 = [self.scr("W2_%d" % l, [FFN, D], BF16) for l in range(L)]

            g = es
            self.smallp = k.sb(g, [128, DEPTH, SP_N], F32, "smallp")
            for l in range(DEPTH):
                k.dma("sp", self.smallp[:, l, :], self.smallp_d[l], W=[self.smallp], acc=(l > 0))
            self.ident_f = k.sb(g, [128, 128], F32, "identf")
            self.ones_f = k.sb(g, [128, 128], F32, "onesf")
            self.onesblk_f = k.sb(g, [128, 128], F32, "onesblkf")
            self.rotm_f = k.sb(g, [128, 128], F32, "rotmf")
            k.dma("sp", self.ident_f[:], self.cd["ident_f"], W=[self.ident_f])
            k.dma("sp", self.ones_f[:], self.cd["ones_f"], W=[self.ones_f])
            k.dma("sp", self.onesblk_f[:], self.cd["onesblk"], W=[self.onesblk_f])
            k.dma("sp", self.rotm_f[:], self.cd["rotm"], W=[self.rotm_f])
            self.ident_b = k.sb(g, [128, 128], BF16, "identb")
            self.onesblk_b = k.sb(g, [128, 128], BF16, "onesblkb")
            self.rotm_b = k.sb(g, [128, 128], BF16, "rotmb")
            k.cp("dve", self.ident_b[:], self.ident_f[:], [self.ident_f], [self.ident_b])
            k.cp("dve", self.onesblk_b[:], self.onesblk_f[:], [self.onesblk_f], [self.onesblk_b])
            k.cp("dve", self.rotm_b[:], self.rotm_f[:], [self.rotm_f], [self.rotm_b])
            self.mods = k.sb(g, [128, 2, 6, 8], F32, "mods")
            self.epsD = k.sb(g, [128, 1], F32, "epsD")
            k.memset("dve", self.epsD[:], EPS, [self.epsD])
            self.epsG = k.sb(g, [128, 1], F32, "epsG")
            k.memset("dve", self.epsG[:], GN_EPS, [self.epsG])

            sa = self.stop_after
            if sa is None or sa[0] != "consts":
                self.phase_weights()
            k.barrier()
            if sa is not None and sa[0] in ("weights", "consts"):
                L = 0
            if L > 0:
                self.phase_transpose_in()
                k.barrier()
            if sa is not None and sa[0] == "tin":
                L = 0
            for l in range(L):
                self.phase_mod(l)
                k.barrier()
                if sa == ("mod", l):
                    break
                self.phase_inproj(l)
                k.barrier()
                if self.stop_after == ("inproj", l):
                    break
                last = (l == self.nlayers - 1)
                self.phase_attn(l, last)
                k.barrier()
                if self.stop_after == ("attn", l):
                    break
                self.phase_rwkv_prep(l)
                k.barrier()
                if self.stop_after == ("rprep", l):
                    break
                self.phase_rwkv_scan(l)
                k.barrier()
                self.phase_rwkv_post(l)
                k.barrier()
                if self.stop_after == ("rwkv", l):
                    break
                self.phase_merge(l, last)
                k.barrier()
                self.phase_ffn(l, last)
                k.barrier()
                if self.stop_after == ("layer", l):
                    break
            self.finish()
        return nc

    def spc(self, l, name, rows=128):
        o, w = SP_OFF[name]
        return self.smallp[0:rows, l, o:o + w]

    def finish(self):
        k = self.k
        k.barrier()

    def phase_weights(self):
        k, nc = self.k, self.nc
        CW = 3328
        with ExitStack() as st:
            stg = [k.sb(st, [128, CW], F32, "wstg") for _ in range(3)]
            bfb = [k.sb(st, [128, CW], BF16, "wbf") for _ in range(3)]
            engs = ["dve", "act", "dve"]
            ctr = [0]

            def conv(src, dst, R, C, permq=False):
                for r0 in range(0, R, 128):
                    rr = min(128, R - r0)
                    for c0 in range(0, C, CW):
                        cc = min(CW, C - c0)
                        i = ctr[0] % 3
                        ctr[0] += 1
                        k.dma("sp", stg[i][0:rr, 0:cc], src[r0:r0 + rr, c0:c0 + cc], W=[stg[i]])
                        k.cp(engs[i], bfb[i][0:rr, 0:cc], stg[i][0:rr, 0:cc], [stg[i]], [bfb[i]])
                        if permq and c0 == 0:
                            for g_ in range(2):
                                k.dma("pool", dst[r0:r0 + rr, 0:512].rearrange("p (j g d) -> p j g d", j=4, g=2, d=64)[:, :, g_, :],
                                      bfb[i][0:rr, 0:512].rearrange("p (g j d) -> p g j d", g=2, j=4, d=64)[:, g_, :, :], R=[bfb[i]])
                            k.dma("pool", dst[r0:r0 + rr, 512:cc], bfb[i][0:rr, 512:cc], R=[bfb[i]])
                        else:
                            k.dma("pool", dst[r0:r0 + rr, c0:c0 + cc], bfb[i][0:rr, 0:cc], R=[bfb[i]])

            for l in range(self.nlayers):
                conv(self.pd["w_in"][l], self.WIN[l], D, IN_DIM, permq=True)
                conv(self.pd["w_o_attn"][l], self.WOA[l], 512, D)
                conv(self.pd["w_o_rwkv"][l], self.WOR[l], 512, D)
                conv(self.pd["w_o_sgu"][l], self.WOS[l], 512, D)
                conv(self.pd["w_out"][l], self.WOUT[l], D, D)
                conv(self.pd["ffn_w1"][l], self.W1[l], D, FFN)
                conv(self.pd["ffn_w3"][l], self.W3[l], D, FFN)
                conv(self.pd["ffn_w2"][l], self.W2[l], FFN, D)

    def phase_transpose_in(self):
        k, nc = self.k, self.nc
        XTv = self.XT.rearrange("k p n -> p k n")
        with ExitStack() as st:
            xin = [k.sb(st, [128, D], F32, "xin") for _ in range(2)]
            xo = [k.sb(st, [128, 8, 128], F32, "xo") for _ in range(2)]
            pbs = [k.ps(st, [128, 4, 128], F32, "ptr") for _ in range(4)]
            ntile = self.NTOK // 128
            for i in range(ntile):
                src = self.ctx_in[i * 128:(i + 1) * 128, :] if i < 2 else self.x_in[(i - 2) * 128:(i - 1) * 128, :]
                xi, xoo = xin[i % 2], xo[i % 2]
                k.dma("sp", xi[:], src, W=[xi])
                for hlf in range(2):
                    pb = pbs[(2 * i + hlf) % 4]
                    for j in range(4):
                        kc = hlf * 4 + j
                        k.tr(pb[:, j, :], xi[:, kc * 128:(kc + 1) * 128], self.ident_f[:], [xi, self.ident_f], [pb], inc=(j == 3))
                    eng = "act" if hlf == 0 else "dve"
                    k.cp(eng, xoo[:, hlf * 4:(hlf + 1) * 4, :], pb[:], [pb], [xoo], acc=(hlf == 1))
                k.dma("pool", XTv[:, :, i * 128:(i + 1) * 128], xoo[:], R=[xoo])

    def phase_mod(self, l):
        k, nc = self.k, self.nc
        with ExitStack() as st:
            cs = k.sb(st, [128, 16], F32, "cs")
            sc = k.sb(st, [128, 8, 2], F32, "sc")
            k.dma("sp", cs[:], self.cfm, W=[cs])
            k.act(sc[:].rearrange("p k m -> p m k"), cs[:].rearrange("p (m k) -> p m k", m=2), AF.Silu, [cs], [sc])
            wm = [k.sb(st, [128, 6 * D], F32, "wm") for _ in range(2)]
            pm = k.ps(st, [128, 48, 2], F32, "pmod")
            for kc in range(8):
                w = wm[kc % 2]
                k.dma("sp", w[:, 0:3072], self.pd["w_mod"][l][kc * 128:(kc + 1) * 128, 0:3072], W=[w])
                k.dma("sp", w[:, 3072:6144], self.pd["w_mod"][l][kc * 128:(kc + 1) * 128, 3072:6144], W=[w], acc=True)
                for fc in range(48):
                    first = (kc == 0 and fc == 0)
                    last = (kc == 7 and fc == 47)
                    k.op("pe", lambda: nc.tensor.matmul(pm[:, fc, :], lhsT=w[:, fc * 128:(fc + 1) * 128], rhs=sc[:, kc, :],
                                                        start=first, stop=last, skip_group_check=True),
                         [w, sc], [pm], inc=(fc == 47))
            raw = k.sb(st, [128, 2, 6, 8], F32, "modraw")
            bm = self.spc(l, "bmod")
            for m in range(2):
                k.tt("dve", raw[:, m, :, :], pm[:, :, m].rearrange("p (j k) -> p j k", j=6),
                     bm.rearrange("p (j k) -> p j k", j=6), ALU.add, [pm, self.smallp], [raw], acc=(m == 1))
            mods = self.mods
            for m in range(2):
                for j in (0, 2, 3, 5):
                    k.cp("dve", mods[:, m, j, :], raw[:, m, j, :], [raw], [mods], acc=not (m == 0 and j == 0))
                k.stt(mods[:, m, 1, :], raw[:, m, 1, :], 1.0, self.spc(l, "nmix"), ALU.add, ALU.mult, [raw, self.smallp], [mods], acc=True)
                k.stt(mods[:, m, 4, :], raw[:, m, 4, :], 1.0, self.spc(l, "nffn"), ALU.add, ALU.mult, [raw, self.smallp], [mods], acc=True)

    def norm_mod(self, st_tiles, blk, l, which):
        k, nc = self.k, self.nc
        xT, work, hT, rstd, pss = st_tiles
        n0, nb, is_ctx, t0 = blk
        m = 1 if is_ctx else 0
        jA, jS = (1, 0) if which == 0 else (4, 3)
        k.act(work[:, :, 0:nb], xT[:, :, 0:nb], AF.Square, [xT], [work])
        for kc in range(8):
            k.mm(pss[:, 0:nb], self.ones_f[:], work[:, kc, 0:nb], kc == 0, kc == 7, [self.ones_f, work], [pss])
        k.act(rstd[:, 0:nb], pss[:, 0:nb], AF.Ln, [pss, self.epsD], [rstd], bias=self.epsD[:], scale=1.0 / D)
        k.act(rstd[:, 0:nb], rstd[:, 0:nb], AF.Exp, [rstd], [rstd], scale=-0.5)
        for kc in range(8):
            k.stt(work[:, kc, 0:nb], xT[:, kc, 0:nb], self.mods[:, m, jA, kc:kc + 1], rstd[:, 0:nb], ALU.mult, ALU.mult,
                  [xT, self.mods, rstd], [work], acc=(kc > 0))
        for kc in range(8):
            k.act(hT[:, kc, 0:nb], work[:, kc, 0:nb], AF.Identity, [work, self.mods], [hT], bias=self.mods[:, m, jS, kc:kc + 1],
                  acc=(kc > 0))

    def phase_inproj(self, l):
        k, nc = self.k, self.nc
        XTv = self.XT.rearrange("k p n -> p k n")
        HTv = self.HT.rearrange("k p n -> p k n")
        QTv = self.QT.rearrange("j p n -> p j n")
        OSTv = self.OST.rearrange("g p n -> p g n")
        with ExitStack() as st:
            wA = k.sb(st, [128, 8, 3584], BF16, "wA")
            for kc in range(8):
                k.dma("sp", wA[:, kc, :], self.WIN[l][kc * 128:(kc + 1) * 128, 0:3584], W=[wA], acc=(kc > 0))
            wsf = k.sb(st, [128, 4, 128], F32, "wsf")
            k.dma("sp", wsf[:], self.pd["sgu_w"][l].rearrange("g p q -> p g q"), W=[wsf])
            WsT = k.sb(st, [128, 4, 128], BF16, "WsT")
            xT = k.sb(st, [128, 8, 512], F32, "xT")
            work = k.sb(st, [128, 8, 512], F32, "work")
            hTs = [k.sb(st, [128, 8, 512], BF16, "hT") for _ in range(2)]
            rstd = k.sb(st, [128, 512], F32, "rstd")
            pss = k.ps(st, [128, 512], F32, "pss")
            pproj = [k.ps(st, [128, 512], F32, "pproj") for _ in range(2)]
            paux = [k.ps(st, [128, 512], F32, "paux") for _ in range(1)]
            ptok = [k.ps(st, [128, 512], F32, "ptok") for _ in range(3)]
            pso = k.ps(st, [128, 4, 128], F32, "pso")
            for g_ in range(4):
                k.tr(ptok[0][:, g_ * 128:(g_ + 1) * 128], wsf[:, g_, :], self.ident_f[:], [wsf, self.ident_f], [ptok[0]], inc=(g_ == 3))
            k.cp("dve", WsT[:].rearrange("p g q -> p (g q)"), ptok[0][:], [ptok[0]], [WsT])
            sqb_l = [k.sb(st, [128, 512], BF16, "sqb") for _ in range(2)]
            r1_l = [k.sb(st, [128, 512], F32, "r1") for _ in range(2)]
            qn_l = [k.sb(st, [128, 512], BF16, "qn") for _ in range(2)]
            t1_l = [k.sb(st, [128, 512], F32, "t1") for _ in range(2)]
            t2_l = [k.sb(st, [128, 512], F32, "t2") for _ in range(2)]
            qr = [k.sb(st, [128, 512], BF16, "qr") for _ in range(2)]
            cosb = k.sb(st, [128, 512], F32, "cosb")
            sinb = k.sb(st, [128, 512], F32, "sinb")
            rstg = [k.sb(st, [128, 512], F32, "rstg") for _ in range(3)]
            rstb = [k.sb(st, [128, 512], BF16, "rstb") for _ in range(3)]
            uT = k.sb(st, [128, 4, 512], BF16, "uT")
            ost = k.sb(st, [128, 4, 512], BF16, "ost")
            va = [k.sb(st, [128, 2, 65], BF16, "va") for _ in range(2)]
            for v_ in va:
                k.memset("dve", v_[:], 1.0, [v_])
            gt_l = [k.sb(st, [128, 512], F32, "gt") for _ in range(2)]
            junk_l = [k.sb(st, [128, 512], BF16, "junk") for _ in range(2)]
            vn_l = [k.sb(st, [128, 512], BF16, "vn") for _ in range(2)]
            vnf_l = [k.sb(st, [128, 512], F32, "vnf") for _ in range(2)]
            stat_l = [k.sb(st, [128, 8], F32, "stat") for _ in range(2)]
            ones_row = self.ones_f[0:1, :]
            sgub = self.spc(l, "sgu_b", rows=1)
            lnw = self.spc(l, "sgu_lnw")
            lnb = self.spc(l, "sgu_lnb")
            gain = self.spc(l, "gain")
            ectr = [0]
            def do_norm(bi_):
                n0_, nb_, _c, _t = self.blocks[bi_]
                hT_ = hTs[bi_ % 2]
                k.dma("sp", xT[:, :, 0:nb_], XTv[:, :, n0_:n0_ + nb_], W=[xT])
                self.norm_mod((xT, work, hT_, rstd, pss), self.blocks[bi_], l, 0)
                k.dma("pool", HTv[:, :, n0_:n0_ + nb_], hT_[:, :, 0:nb_], R=[hT_])

            do_norm(0)
            tctr_l = [0]
            for bi, blk in enumerate(self.blocks):
                n0, nb, is_ctx, t0 = blk
                hT = hTs[bi % 2]
                if not is_ctx:
                    k.dma("sp", cosb[:, 0:nb], self.cd["ropecos"][:, t0:t0 + nb], W=[cosb])
                    k.dma("sp", sinb[:, 0:nb], self.cd["ropesin"][:, t0:t0 + nb], W=[sinb])
                if bi + 1 < len(self.blocks):
                    do_norm(bi + 1)
                chunks = [("q", j, j * 128) for j in range(4)] + [("k", 0, 512)] + \
                         [("r", j, 768 + j * 128) for j in range(14)] + [("u", j, 2560 + j * 128) for j in range(4)]
                for ci, (kind, j, c0) in enumerate(chunks):
                    pb = pproj[ci % 2]
                    for kc in range(8):
                        k.mm(pb[:, 0:nb], wA[:, kc, c0:c0 + 128], hT[:, kc, 0:nb], kc == 0, kc == 7, [wA, hT], [pb])
                    if kind in ("q", "k"):
                        sqb, r1, qn, t1, t2 = sqb_l[ci % 2], r1_l[ci % 2], qn_l[ci % 2], t1_l[ci % 2], t2_l[ci % 2]
                        pss2 = paux[0]
                        prot = paux[0]
                        gcol = gain[:, 0:1] if kind == "q" else gain[:, 1:2]
                        k.act(sqb[:, 0:nb], pb[:, 0:nb], AF.Square, [pb], [sqb])
                        k.mm(pss2[:, 0:nb], self.onesblk_b[:], sqb[:, 0:nb], True, True, [self.onesblk_b, sqb], [pss2])
                        k.act(r1[:, 0:nb], pss2[:, 0:nb], AF.Ln, [pss2, self.epsD], [r1], bias=self.epsD[:], scale=1.0 / HD)
                        k.act(r1[:, 0:nb], r1[:, 0:nb], AF.Exp, [r1], [r1], scale=-0.5)
                        dst = qr[ci % 2]
                        if is_ctx:
                            k.stt(dst[:, 0:nb], pb[:, 0:nb], gcol, r1[:, 0:nb], ALU.mult, ALU.mult, [pb, r1, self.smallp], [dst])
                        else:
                            k.stt(qn[:, 0:nb], pb[:, 0:nb], gcol, r1[:, 0:nb], ALU.mult, ALU.mult, [pb, r1, self.smallp], [qn])
                            k.mm(prot[:, 0:nb], self.rotm_b[:], qn[:, 0:nb], True, True, [self.rotm_b, qn], [prot])
                            k.tt("dve", t1[:, 0:nb], qn[:, 0:nb], cosb[:, 0:nb], ALU.mult, [qn, cosb], [t1])
                            k.tt("dve", t2[:, 0:nb], prot[:, 0:nb], sinb[:, 0:nb], ALU.mult, [prot, sinb], [t2])
                            k.tt("dve", dst[:, 0:nb], t1[:, 0:nb], t2[:, 0:nb], ALU.add, [t1, t2], [dst])
                        if kind == "q":
                            k.dma("pool", QTv[:, j, n0:n0 + nb], dst[:, 0:nb], R=[dst])
                        else:
                            k.dma("pool", self.KT[:, n0:n0 + nb], dst[:, 0:nb], R=[dst])
                    elif kind == "r":
                        eng = "act" if ectr[0] % 2 == 0 else "dve"
                        if j < 12:
                            sg = rstb[ectr[0] % 3]
                            k.cp(eng, sg[:, 0:nb], pb[:, 0:nb], [pb], [sg])
                            k.dma("pool", self.PRB[j * 128:(j + 1) * 128, n0:n0 + nb], sg[:, 0:nb], R=[sg])
                        else:
                            sg = rstg[ectr[0] % 3]
                            k.cp(eng, sg[:, 0:nb], pb[:, 0:nb], [pb], [sg])
                            k.dma("pool", self.PRW[j * 128:(j + 1) * 128, n0:n0 + nb], sg[:, 0:nb], R=[sg])
                        ectr[0] += 1
                    else:
                        k.act(uT[:, j, 0:nb], pb[:, 0:nb], AF.Gelu_apprx_tanh, [pb], [uT], acc=(j > 0))
                ntile = nb // 128
                vns = {}

                def tm_front(tt_, hT=hT, n0=n0, vns=vns):
                    ts_ = slice(tt_ * 128, (tt_ + 1) * 128)
                    pv = ptok[0]
                    pg = ptok[1 + tctr_l[0] % 2]
                    gt, junk, vn, vnf, stat = (x_[tctr_l[0] % 2] for x_ in (gt_l, junk_l, vn_l, vnf_l, stat_l))
                    tctr_l[0] += 1
                    for kc in range(8):
                        k.mm(pv[:, 0:128], hT[:, kc, ts_], wA[:, kc, 640:768], kc == 0, kc == 7, [wA, hT], [pv])
                    for kc in range(8):
                        k.mm(pg[:, :], hT[:, kc, ts_], wA[:, kc, 3072:3584], kc == 0, kc == 7, [wA, hT], [pg])
                    vt = va[tt_ % 2]
                    k.cp("dve", vt[:, :, 0:64], pv[:, 0:128].rearrange("p (h d) -> p h d", h=2), [pv], [vt])
                    k.dma("pool", self.VA[n0 + tt_ * 128:n0 + (tt_ + 1) * 128, :], vt[:].rearrange("p h d -> p (h d)"), R=[vt])
                    k.act(gt[:], pg[:], AF.Gelu_apprx_tanh, [pg], [gt, stat], accum=stat[:, 0:1])
                    k.ts("dve", stat[:, 1:2], stat[:, 0:1], -1.0 / 512, ALU.mult, [stat], [stat])
                    k.act(junk[:], gt[:], AF.Square, [gt, stat], [junk, stat], bias=stat[:, 1:2], accum=stat[:, 2:3])
                    k.act(stat[:, 3:4], stat[:, 2:3], AF.Ln, [stat, self.epsD], [stat], bias=self.epsD[:], scale=1.0 / 512)
                    k.act(stat[:, 4:5], stat[:, 3:4], AF.Exp, [stat], [stat], scale=-0.5)
                    k.ts("dve", vnf[:], gt[:], stat[:, 1:2], ALU.add, [gt, stat], [vnf], s2=stat[:, 4:5], op1=ALU.mult)
                    k.tt("dve", vnf[:], vnf[:], lnw, ALU.mult, [vnf, self.smallp], [vnf])
                    k.tt("dve", vn[:], vnf[:], lnb, ALU.add, [vnf, self.smallp], [vn])
                    vns[tt_] = vn

                def tm_back(tt_, vns=vns):
                    ts_ = slice(tt_ * 128, (tt_ + 1) * 128)
                    vn = vns[tt_]
                    for g_ in range(4):
                        gs = slice(g_ * 128, (g_ + 1) * 128)
                        k.mm(pso[:, g_, :], vn[:, gs], WsT[:, g_, :], True, False, [vn, WsT], [pso], inc=False)
                        k.mm(pso[:, g_, :], ones_row, sgub[:, gs], False, True, [self.ones_f, self.smallp], [pso], inc=(g_ == 3))
                    k.tt("dve", ost[:, :, ts_], uT[:, :, ts_], pso[:], ALU.mult, [uT, pso], [ost], acc=(tt_ > 0))

                for tt_ in range(ntile):
                    tm_front(tt_)
                    if tt_ >= 1:
                        tm_back(tt_ - 1)
                tm_back(ntile - 1)
                k.dma("pool", OSTv[:, :, n0:n0 + nb], ost[:, :, 0:nb], R=[ost])


    def phase_attn(self, l, last):
        k, nc = self.k, self.nc
        NTOK = self.NTOK
        NT = NTOK // 128
        nlat = self.T // 128
        OATv = self.OAT.rearrange("c p n -> p c n")
        with ExitStack() as st:
            Qs = k.sb(st, [128, 4, NTOK], BF16, "Qs")
            Ks = k.sb(st, [128, NTOK], BF16, "Ks")
            Vs = k.sb(st, [128, NT, 130], BF16, "Vs")
            QTv = self.QT.rearrange("j p n -> p j n")
            for j in range(4):
                k.dma("sp", Qs[:, j, :], QTv[:, j, :], W=[Qs], acc=(j > 0))
            k.dma("sp", Ks[:], self.KT, W=[Ks])
            k.dma("sp", Vs[:], self.VA.rearrange("(t p) f -> p t f", p=128), W=[Vs])
            mf = k.sb(st, [128, 2, 512], F32, "mf")
            k.dma("sp", mf[:, 0, :], self.cd["maskprev"], W=[mf])
            k.dma("sp", mf[:, 1, :], self.cd["masknext"], W=[mf], acc=True)
            mb = k.sb(st, [128, 2, 512], BF16, "mb")
            k.cp("dve", mb[:], mf[:], [mf], [mb])
            esink = k.sb(st, [128, 8], F32, "esink")
            k.act(esink[:], self.spc(l, "sink"), AF.Exp, [self.smallp], [esink])
            pS = [k.ps(st, [128, 512], F32, "pS") for _ in range(3)]
            pO = [[k.ps(st, [128, 4, 128], F32, "pO") for _ in range(2)] for _ in range(2)]
            pT = k.ps(st, [128, 4, 128], BF16, "pT")
            PT = [k.sb(st, [128, 512], BF16, "PT") for _ in range(3)]
            den = k.sb(st, [128, 8], F32, "den")
            oa = [k.sb(st, [128, 8, 64], BF16, "oa") for _ in range(2)]
            oat = [k.sb(st, [128, 4, 128], BF16, "oat") for _ in range(2)]
            sctr_l = [0]
            pending = None
            qtiles = list(range(2, NT)) if last else list(range(NT))
            for qn_, qi in enumerate(qtiles):
                qs = slice(qi * 128, (qi + 1) * 128)
                if qi < 2:
                    kbs = [(0, None), (1, None)]
                else:
                    b = qi - 2
                    kbs = []
                    if b > 0:
                        kbs.append((qi - 1, 0))
                    kbs.append((qi, None))
                    if b < nlat - 1:
                        kbs.append((qi + 1, 1))
                    kbs += [(0, None), (1, None)]
                oat_ = oat[qn_ % 2]
                oa_ = oa[qn_ % 2]
                items = [(g_, ki, kb, msk) for g_ in range(2) for ki, (kb, msk) in enumerate(kbs)]
                slots = []
                nkb = len(kbs)

                def emit_S(it, qs=qs, slots=slots):
                    g_, ki, kb, msk = it
                    gp = slice(g_ * 64, (g_ + 1) * 64)
                    ps_ = pS[sctr_l[0] % 3]
                    pt_ = PT[sctr_l[0] % 3]
                    sctr_l[0] += 1
                    k.mm(ps_[:], Ks[gp, kb * 128:(kb + 1) * 128], Qs[gp, :, qs], True, msk is None, [Ks, Qs], [ps_])
                    if msk is not None:
                        k.mm(ps_[:], self.ident_b[:], mb[:, msk, :], False, True, [self.ident_b, mb], [ps_])
                    k.act(pt_[:], ps_[:], AF.Exp, [ps_], [pt_], scale=0.125)
                    slots.append(pt_)

                def emit_PV(idx, items=items, slots=slots, nkb=nkb, qn_=qn_, oa_=oa_):
                    g_, ki, kb, msk = items[idx]
                    pt_ = slots[idx]
                    po = pO[qn_ % 2][g_]
                    for jh in range(4):
                        first = (ki == 0 and jh == 0)
                        lastmm = (ki == nkb - 1 and jh == 3)
                        k.op("pe", lambda: nc.tensor.matmul(po[:, jh, 0:65], lhsT=pt_[:, jh * 128:(jh + 1) * 128],
                                                            rhs=Vs[:, kb, g_ * 65:(g_ + 1) * 65], start=first, stop=lastmm,
                                                            skip_group_check=True),
                             [pt_, Vs], [po], inc=(jh == 3))
                    if ki == nkb - 1:
                        hs = slice(g_ * 4, (g_ + 1) * 4)
                        k.tt("dve", den[:, hs], po[:, :, 64], esink[:, hs], ALU.add, [po, esink], [den], acc=(g_ == 1))
                        k.recip(den[:, hs], den[:, hs], [den], [den])
                        k.tt("dve", oa_[:, hs, :], po[:, :, 0:64], den[:, hs].unsqueeze(2).broadcast_to([128, 4, 64]), ALU.mult,
                             [po, den], [oa_], acc=(g_ == 1))

                for idx in range(len(items)):
                    emit_S(items[idx])
                    if idx >= 1:
                        emit_PV(idx - 1)
                emit_PV(len(items) - 1)
                def finish_tile(oa__=oa_, oat__=oat_, qs_=qs):
                    for c_ in range(4):
                        k.tr(pT[:, c_, :], oa__[:, 2 * c_:2 * c_ + 2, :].rearrange("p h d -> p (h d)"), self.ident_b[:], [oa__, self.ident_b], [pT],
                             inc=(c_ == 3))
                    k.cp("act", oat__[:], pT[:], [pT], [oat__])
                    k.dma("pool", OATv[:, :, qs_], oat__[:], R=[oat__])
                if pending is not None:
                    pending()
                pending = finish_tile
            if pending is not None:
                pending()

    def phase_rwkv_prep(self, l):
        k, nc = self.k, self.nc
        NTOK = self.NTOK
        PB = 128
        CC = 0.6065306597126334
        PRWv = self.PRB.rearrange("(w c) n -> c w n", c=64)
        PWAv = self.PRW[1536:1664].rearrange("(w c) n -> c w n", c=64)
        PGv = self.PRW[1664:1792]
        HTv = self.HT.rearrange("k p n -> p k n")
        with ExitStack() as st:
            T_ = lambda nm, dt=F32, shp=(128, 8, PB): k.sb(st, list(shp), dt, nm)
            dg = k.sb(st, [64, 72, 2, 64], BF16, "dg")
            cw = self.spc(l, "conv_rkv", rows=64)
            idb = self.ident_f[0:64, 0:64].unsqueeze(1).broadcast_to([64, 72, 64])
            cwb = cw.unsqueeze(2).broadcast_to([64, 72, 64])
            for dd in range(2):
                k.tt("dve", dg[:, :, dd, :], idb, cwb, ALU.mult, [self.ident_f, self.smallp], [dg], acc=(dd == 1))
            dg2 = k.sb(st, [64, 6, 64], F32, "dg2")
            k.tt("dve", dg2[:], self.ident_f[0:64, 0:64].unsqueeze(1).broadcast_to([64, 6, 64]),
                 self.spc(l, "conv_wa", rows=64).unsqueeze(2).broadcast_to([64, 6, 64]), ALU.mult, [self.ident_f, self.smallp], [dg2])
            dg3 = k.sb(st, [128, 3, 128], F32, "dg3")
            k.tt("dve", dg3[:], self.ident_f[:].unsqueeze(1).broadcast_to([128, 3, 128]),
                 self.spc(l, "conv_g").unsqueeze(2).broadcast_to([128, 3, 128]), ALU.mult, [self.ident_f, self.smallp], [dg3])
            stg = k.sb(st, [128, 1024], F32, "lstg")
            wup = k.sb(st, [64, 8, 2, 64], BF16, "wup")
            aup = k.sb(st, [64, 8, 2, 64], BF16, "aup")
            for (dst, nm) in ((wup, "rwkv_w_up"), (aup, "rwkv_a_up")):
                for dd in range(2):
                    k.dma("sp", stg[0:64, dd * 512:(dd + 1) * 512], self.pd[nm][l, dd], W=[stg], acc=(dd == 1))
                k.cp("dve", dst[:].rearrange("r h d c -> r d h c"), stg[0:64, :].rearrange("r (d h c) -> r d h c", d=2, h=8), [stg], [dst])
            gup = k.sb(st, [128, 8, 64], BF16, "gup")
            k.dma("sp", stg[:, 0:512], self.pd["rwkv_g_up"][l], W=[stg])
            k.cp("dve", gup[:].rearrange("r h c -> r (h c)"), stg[:, 0:512], [stg], [gup])
            if l > 0:
                dwn = k.sb(st, [128, 8, 32], BF16, "dwn")
                k.dma("sp", stg[:, 0:256].rearrange("p (k r) -> p k r", k=8), self.pd["rwkv_vres_down"][l - 1].rearrange("(k p) r -> p k r", p=128), W=[stg])
                k.cp("dve", dwn[:].rearrange("p k r -> p (k r)"), stg[:, 0:256], [stg], [dwn])
                vup = k.sb(st, [32, 8, 2, 64], BF16, "vup")
                k.dma("sp", stg[0:32, 0:512], self.pd["rwkv_vres_up"][l - 1], W=[stg])
                for dd in range(2):
                    k.cp("dve", vup[:, :, dd, :], stg[0:32, 0:512].rearrange("r (h c) -> r h c", h=8), [stg], [vup], acc=(dd == 1))
            rsm = k.sb(st, [128, 8, PB], F32, "rsm")
            k.dma("sp", rsm[:], self.cd["scanreset"], W=[rsm])
            eps12 = k.sb(st, [128, 1], F32, "eps12")
            k.memset("dve", eps12[:], 1e-12, [eps12])
            hal = [k.sb(st, [64, 8, PB + 2], BF16, "hal") for _ in range(3)]
            hwa = k.sb(st, [64, 2, PB + 2], F32, "hwa")
            hg = k.sb(st, [128, PB + 2], F32, "hg")
            r_, kx, v_ = T_("r"), T_("kx"), T_("v")
            kA, kB, sig, asg = T_("kA"), T_("kB"), T_("sig"), T_("asg")
            P_, Q_, TP = T_("P"), T_("Q"), T_("TP")
            E = [T_("E") for _ in range(2)]
            kd, bb, tm, tm2 = T_("kd"), T_("bb"), T_("tm"), T_("tm2")
            wc = k.sb(st, [128, 2, 8], F32, "wc")
            xwt = k.sb(st, [64, PB], BF16, "xwt")
            xab = k.sb(st, [64, PB], BF16, "xab")
            sgx = k.sb(st, [128, PB], BF16, "sgx")
            hdb = k.sb(st, [32, PB], BF16, "hdb")
            hTb = k.sb(st, [128, 8, PB], BF16, "hTb")
            ob = [k.sb(st, [128, 8, PB], BF16, "ob") for _ in range(6)]
            vb = k.sb(st, [64, 8, PB], BF16, "vb")
            tokb = k.sb(st, [128, 8, 2, 128], BF16, "tokb")
            tokv = k.sb(st, [128, 8, 64], BF16, "tokv")
            gT = k.sb(st, [64, 8, PB], F32, "gT")
            bv = k.sb(st, [64, 8, PB], F32, "bv")
            pu = [k.ps(st, [128, 4, PB], F32, "pu") for _ in range(4)]
            ptr = [k.ps(st, [128, 8, 128], BF16, "ptr") for _ in range(2)]
            pm = k.ps(st, [128, 512], F32, "pm")
            ptv = k.ps(st, [128, 8, 64], BF16, "ptv")
            puc = [0]

            def nextpu():
                puc[0] += 1
                return pu[puc[0] % 4]

            bcast = lambda name: self.spc(l, name).unsqueeze(2).broadcast_to([128, 8, PB])
            nblk = NTOK // PB
            ectr = 0
            for bi in range(nblk):
                n0 = bi * PB
                lz = (n0 == 0 or n0 == LCTX)
                rz = (n0 + PB == LCTX or n0 + PB == NTOK)
                lo = 1 if lz else 0
                hi = PB + 1 if rz else PB + 2

                def load_halo(tile, src, a3):
                    first = True
                    if lz:
                        ap = tile[:, :, 0:1] if a3 else tile[:, 0:1]
                        k.op("pool", lambda: nc.gpsimd.memset(ap, 0.0), (), [tile])
                        first = False
                    if rz:
                        ap2 = tile[:, :, PB + 1:PB + 2] if a3 else tile[:, PB + 1:PB + 2]
                        k.op("pool", lambda: nc.gpsimd.memset(ap2, 0.0), (), [tile], acc=not first)
                        first = False
                    dst = tile[:, :, lo:hi] if a3 else tile[:, lo:hi]
                    k.dma("sp", dst, src, W=[tile], acc=not first)

                outs3 = (r_, kx, v_)
                for wch in range(3):
                    h_ = hal[wch]
                    load_halo(h_, PRWv[:, wch * 8:(wch + 1) * 8, n0 - 1 + lo:n0 - 1 + hi], True)
                    for half in range(2):
                        pb = nextpu()
                        for hh in range(4):
                            h = half * 4 + hh
                            w = wch * 8 + h
                            for tap in range(3):
                                k.mm(pb[:, hh, :], dg[:, tap * 24 + w, :, :].rearrange("p d c -> p (d c)"), h_[:, h, tap:tap + PB],
                                     tap == 0, tap == 2, [dg, h_], [pb], inc=(tap == 2 and hh == 3))
                        eng = "act" if ectr % 2 == 0 else "dve"
                        ectr += 1
                        k.cp(eng, outs3[wch][:, half * 4:(half + 1) * 4, :], pb[:], [pb], [outs3[wch]], acc=(half == 1))
                if getattr(self, 'rp_cut', 99) < 1:
                    return
                load_halo(hwa, PWAv[:, :, n0 - 1 + lo:n0 - 1 + hi], True)
                load_halo(hg, PGv[:, n0 - 1 + lo:n0 - 1 + hi], False)
                if getattr(self, 'rp_cut', 99) < 0.3:
                    return
                for wch in range(2):
                    for tap in range(3):
                        k.mm(pm[0:64, wch * PB:(wch + 1) * PB], dg2[:, tap * 2 + wch, :], hwa[:, wch, tap:tap + PB], tap == 0, tap == 2,
                             [dg2, hwa], [pm], inc=(tap == 2 and wch == 1))
                k.act(xwt[:], pm[0:64, 0:PB], AF.Tanh, [pm], [xwt])
                k.cp("act", xab[:], pm[0:64, PB:2 * PB], [pm], [xab])
                if getattr(self, 'rp_cut', 99) < 0.6:
                    return
                pbg = nextpu()
                for tap in range(3):
                    k.mm(pbg[:, 0, :], dg3[:, tap, :], hg[:, tap:tap + PB], tap == 0, tap == 2, [dg3, hg], [pbg])
                k.act(sgx[:], pbg[:, 0, :], AF.Sigmoid, [pbg], [sgx])
                if getattr(self, 'rp_cut', 99) < 2:
                    return
                for (dst, wt, src, bname) in ((sig, wup, xwt, "w0"), (asg, aup, xab, "a0")):
                    bcol = self.spc(l, bname)
                    for half in range(2):
                        pb = nextpu()
                        for hh in range(4):
                            h = half * 4 + hh
                            k.mm(pb[:, hh, :], wt[:, h, :, :].rearrange("r d c -> r (d c)"), src[:], True, True, [wt, src], [pb], inc=(hh == 3))
                        for hh in range(4):
                            h = half * 4 + hh
                            k.act(dst[:, h, :], pb[:, hh, :], AF.Sigmoid, [pb, self.smallp], [dst], bias=bcol[:, h:h + 1], acc=(h > 0))
                if getattr(self, 'rp_cut', 99) < 3:
                    return
                for half in range(2):
                    pb = nextpu()
                    for hh in range(4):
                        h = half * 4 + hh
                        k.mm(pb[0:64, hh, :], gup[:, h, :], sgx[:], True, True, [gup, sgx], [pb], inc=(hh == 3))
                    k.cp("act", gT[:, half * 4:(half + 1) * 4, :], pb[0:64, :, :], [pb], [gT], acc=(half == 1))
                k.dma("pool", self.RG[:, :, n0:n0 + PB], gT[:], R=[gT])
                if getattr(self, 'rp_cut', 99) < 4:
                    return
                if l > 0:
                    k.dma("sp", hTb[:], HTv[:, :, n0:n0 + PB], W=[hTb])
                    for kc in range(8):
                        k.mm(pm[0:32, 3 * PB:4 * PB], dwn[:, kc, :], hTb[:, kc, :], kc == 0, kc == 7, [dwn, hTb], [pm])
                    k.cp("act", hdb[:], pm[0:32, 3 * PB:4 * PB], [pm], [hdb])
                    vb_col = self.spc(l, "vres_b")
                    for half in range(2):
                        pb = nextpu()
                        for hh in range(4):
                            h = half * 4 + hh
                            k.mm(pb[:, hh, :], vup[:, h, :, :].rearrange("r d c -> r (d c)"), hdb[:], True, True, [vup, hdb], [pb], inc=(hh == 3))
                        for hh in range(4):
                            h = half * 4 + hh
                            k.act(tm[:, h, :], pb[:, hh, :], AF.Sigmoid, [pb, self.smallp], [tm], bias=vb_col[:, h:h + 1], acc=(h > 0))
                    for dd in range(2):
                        k.dma("sp", tm2[dd * 64:(dd + 1) * 64], self.VF[:, :, n0:n0 + PB], W=[tm2], acc=(dd == 1))
                    k.tt("dve", tm2[:], tm2[:], v_[:], ALU.subtract, [tm2, v_], [tm2])
                    k.tt("dve", tm2[:], tm2[:], tm[:], ALU.mult, [tm2, tm], [tm2])
                    k.tt("dve", v_[:], v_[:], tm2[:], ALU.add, [v_, tm2], [v_])
                else:
                    k.dma("pool", self.VF[:, :, n0:n0 + PB], v_[0:64], R=[v_])
                k.cp("act", vb[:], v_[0:64], [v_], [vb])
                if getattr(self, 'rp_cut', 99) < 5:
                    return
                k.tt("dve", tm[:], kx[:], bcast("k_k"), ALU.mult, [kx, self.smallp], [tm])
                k.act(tm2[:], tm[:], AF.Square, [tm], [tm2])
                for half in range(2):
                    pb = nextpu()
                    for hh in range(4):
                        h = half * 4 + hh
                        k.mm(pb[:, hh, :], self.onesblk_f[:], tm2[:, h, :], True, True, [self.onesblk_f, tm2], [pb], inc=(hh == 3))
                    k.act(kA[:, half * 4:(half + 1) * 4, :], pb[:], AF.Ln, [pb, eps12], [kA], bias=eps12[:], acc=(half == 1))
                k.act(kA[:], kA[:], AF.Exp, [kA], [kA], scale=-0.5)
                kk = kB
                k.tt("dve", kk[:], tm[:], kA[:], ALU.mult, [tm, kA], [kk])
                k.tt("dve", kA[:], kx[:], bcast("k_a"), ALU.mult, [kx, self.smallp], [kA])
                k.tt("dve", kx[:], kx[:], kA[:], ALU.subtract, [kx, kA], [kx])
                k.tt("dve", kd[:], kA[:], asg[:], ALU.mult, [kA, asg], [kd])
                k.tt("dve", kd[:], kd[:], kx[:], ALU.add, [kd, kx], [kd])
                k.tt("dve", bb[:], kk[:], asg[:], ALU.mult, [kk, asg], [bb])
                if getattr(self, 'rp_cut', 99) < 6:
                    return
                k.tt("dve", tm[:], r_[:], kd[:], ALU.mult, [r_, kd], [tm])
                k.tt("dve", tm[:], tm[:], bcast("r_k"), ALU.mult, [tm, self.smallp], [tm])
                lnb_b = self.spc(l, "ln_b", rows=64).unsqueeze(2).broadcast_to([64, 8, PB])
                for half in range(2):
                    pb = nextpu()
                    for hh in range(4):
                        h = half * 4 + hh
                        k.mm(pb[0:64, hh, :], self.ones_f[:, 0:64], tm[:, h, :], True, True, [self.ones_f, tm], [pb], inc=(hh == 3))
                    k.tt("dve", bv[:, half * 4:(half + 1) * 4, :], pb[0:64, :, :], v_[0:64, half * 4:(half + 1) * 4, :], ALU.mult, [pb, v_], [bv], acc=(half == 1))
                k.tt("dve", bv[:], bv[:], lnb_b, ALU.add, [bv, self.smallp], [bv])
                k.dma("pool", self.RBV[:, :, n0:n0 + PB], bv[:], R=[bv])
                if getattr(self, 'rp_cut', 99) < 7:
                    return
                k.op("dve", lambda: nc.vector.tensor_tensor_scan(out=P_[:].rearrange("p h t -> p (h t)"), data0=rsm[:].rearrange("p h t -> p (h t)"),
                                                                 data1=sig[:].rearrange("p h t -> p (h t)"), initial=0.0, op0=ALU.mult, op1=ALU.add),
                     [rsm, sig], [P_])
                k.tt("dve", Q_[:], P_[:], sig[:], ALU.subtract, [P_, sig], [Q_])
                Pc = P_[:].rearrange("p h (j t) -> p h j t", t=64)
                tot = Pc[:, :, :, 63:64]
                totb = tot.broadcast_to([128, 8, 2, 64])
                k.act(wc[:].rearrange("p j h -> p h j"), P_[:].rearrange("p h (j t) -> p h j t", t=64)[:, :, :, 63], AF.Exp, [P_], [wc], scale=-CC)
                k.tt("dve", TP[:].rearrange("p h (j t) -> p h j t", t=64), totb, Pc, ALU.subtract, [P_], [TP])
                k.tt("dve", P_[64:128], TP[64:128], sig[64:128], ALU.add, [TP, sig], [P_])
                if getattr(self, 'rp_cut', 99) < 8:
                    return
                Ee, Ei = E[0], E[1]
                k.act(Ee[0:64], Q_[0:64], AF.Exp, [Q_], [Ee], scale=-CC)
                k.act(Ee[64:128], TP[64:128], AF.Exp, [TP], [Ee], scale=-CC, acc=True)
                k.act(Ei[:], P_[:], AF.Exp, [P_], [Ei], scale=-CC)
                k.stt(ob[0][:], kk[:], -1.0, Ee[:], ALU.mult, ALU.mult, [kk, Ee], [ob[0]])
                k.tt("dve", ob[1][:], r_[:], Ei[:], ALU.mult, [r_, Ei], [ob[1]])
                k.act(Ee[:], P_[:], AF.Exp, [P_], [Ee], scale=CC)
                k.tt("dve", ob[2][:], bb[:], Ee[:], ALU.mult, [bb, Ee], [ob[2]])
                k.tt("dve", ob[3][:], kd[:], Ee[:], ALU.mult, [kd, Ee], [ob[3]])
                k.act(Ei[0:64], TP[0:64], AF.Exp, [TP], [Ei], scale=-CC)
                k.act(Ei[64:128], Q_[64:128], AF.Exp, [Q_], [Ei], scale=-CC, acc=True)
                k.tt("dve", ob[4][:], bb[:], Ei[:], ALU.mult, [bb, Ei], [ob[4]])
                k.tt("dve", ob[5][:], kd[:], Ei[:], ALU.mult, [kd, Ei], [ob[5]])
                if getattr(self, 'rp_cut', 99) < 9:
                    return
                for q_ in range(4):
                    for dd in range(2):
                        k.dma("pool", self.RA4[dd][:, :, q_, n0:n0 + PB], ob[q_][dd * 64:(dd + 1) * 64], R=[ob[q_]])
                for dd in range(2):
                    for j in range(2):
                        k.dma("pool", self.RWC[dd][n0 // 64 + j], wc[dd * 64:(dd + 1) * 64, j, :], R=[wc])
                if getattr(self, 'rp_cut', 99) < 10:
                    return
                for q_ in range(2):
                    for h in range(8):
                        k.tr(ptr[q_][:, h, :], ob[4 + q_][:, h, :], self.ident_b[:], [ob[4 + q_], self.ident_b], [ptr[q_]], inc=(h == 7))
                    k.cp("act" if q_ == 0 else "dve", tokb[:, :, q_, :], ptr[q_][:], [ptr[q_]], [tokb], acc=(q_ == 1))
                for h in range(8):
                    k.tr(ptv[:, h, :], vb[:, h, :], self.ident_b[0:64, 0:64], [vb, self.ident_b], [ptv], inc=(h == 7))
                k.cp("act", tokv[:], ptv[:], [ptv], [tokv])
                k.dma("pool", self.TOKB[n0:n0 + PB].rearrange("n h q f -> n (h q f)"), tokb[:].rearrange("p h q f -> p (h q f)"), R=[tokb])
                k.dma("pool", self.TOKV[n0:n0 + PB].rearrange("n h f -> n (h f)"), tokv[:].rearrange("p h f -> p (h f)"), R=[tokv])
                if getattr(self, 'rp_cut', 99) < 11 + bi:
                    return

    def phase_rwkv_scan(self, l):
        k, nc = self.k, self.nc
        NTOK = self.NTOK
        NCH = NTOK // CH
        ncx = LCTX // CH
        order = [list(range(NCH)), list(range(ncx - 1, -1, -1)) + list(range(NCH - 1, ncx - 1, -1))]
        with ExitStack() as st:
            rm = k.sb(st, [64, 4, 64], F32, "rm")
            k.dma("sp", rm[:], self.cd["rmask"], W=[rm])
            class _Half:
                def __init__(self, tt_):
                    self.tt_ = tt_
                    self.tok = tt_.tok

                def __getitem__(self, key):
                    if not isinstance(key, tuple):
                        key = (key,)
                    assert key[0] == slice(None)
                    return self.tt_.t[(slice(0, 64),) + tuple(key[1:])]
            B = [_Half(k.ps(st, [128, 8, 64], F32, "B%d" % i)) for i in range(8)]
            B01 = None
            D_ = []
            for d_ in range(2):
                o = {}
                o["arq"] = [k.sb(st, [64, 8, 4, 64], BF16, "arq") for _ in range(2)]
                o["bh"] = [k.sb(st, [64, 8, 64], BF16, "bh") for _ in range(2)]
                o["kh"] = [k.sb(st, [64, 8, 64], BF16, "kh") for _ in range(2)]
                o["vt"] = [k.sb(st, [64, 8, 64], BF16, "vt") for _ in range(2)]
                o["wc"] = [k.sb(st, [64, 8], F32, "wcl") for _ in range(2)]
                o["NL"] = k.sb(st, [64, 8, 2, 64], BF16, "NL")
                o["KL"] = k.sb(st, [64, 8, 2, 64], BF16, "KL")
                o["Nb"] = [k.sb(st, [64, 8, 64], BF16, "Nb") for _ in range(2)]
                o["Pb"] = [k.sb(st, [64, 8, 64], BF16, "Pb") for _ in range(2)]
                o["TTf"] = k.sb(st, [64, 8, 64], F32, "TTf")
                o["TTb"] = [k.sb(st, [64, 8, 64], BF16, "TTb") for _ in range(2)]
                o["Xb"] = k.sb(st, [64, 8, 64], BF16, "Xb")
                o["Ub"] = k.sb(st, [64, 8, 64], BF16, "Ub")
                o["Ys"] = [k.sb(st, [64, 8, 64], F32, "Ys") for _ in range(2)]
                o["STf"] = k.sb(st, [64, 8, 64], F32, "STf")
                o["STb"] = k.sb(st, [64, 8, 64], BF16, "STb")
                k.memset("dve", o["STf"][:], 0.0, [o["STf"]])
                k.memset("dve", o["STb"][:], 0.0, [o["STb"]])
                D_.append(o)
            idb64 = self.ident_f[0:64, 0:64].unsqueeze(1).broadcast_to([64, 8, 64])

            def load(d_, i):
                o = D_[d_]
                ck = order[d_][i]
                n0 = ck * CH
                s = i % 2
                k.dma("sp", o["arq"][s][:].rearrange("c h q t -> c (h q) t"),
                      self.RA4[d_][:, :, :, n0:n0 + CH].rearrange("c h q t -> c (h q) t"), W=[o["arq"][s]])
                k.dma("sp", o["bh"][s][:], self.TOKB[n0:n0 + CH, :, 0, d_ * 64:(d_ + 1) * 64], W=[o["bh"][s]])
                k.dma("sp", o["kh"][s][:], self.TOKB[n0:n0 + CH, :, 1, d_ * 64:(d_ + 1) * 64], W=[o["kh"][s]])
                k.dma("sp", o["vt"][s][:], self.TOKV[n0:n0 + CH], W=[o["vt"][s]])
                k.dma("sp", o["wc"][s][:], self.RWC[d_][ck], W=[o["wc"][s]])

            for d_ in range(2):
                load(d_, 0)
            for i in range(NCH):
                if i + 1 < NCH:
                    for d_ in range(2):
                        load(d_, i + 1)
                s = i % 2
                for d_ in range(2):
                    o = D_[d_]
                    arq = o["arq"][s]
                    mp = rm[:, 0:2, :] if d_ == 0 else rm[:, 2:4, :]
                    mA = rm[:, 2, :] if d_ == 0 else rm[:, 0, :]
                    mpb = mp.unsqueeze(1).broadcast_to([64, 8, 2, 64])
                    mAb = mA.unsqueeze(1).broadcast_to([64, 8, 64])
                    for h in range(8):
                        bn = B[0] if h < 4 else B[1]
                        k.mm(bn[:, 2 * (h % 4):2 * (h % 4) + 2, :], arq[:, h, 2, :], arq[:, h, 0:2, :], True, True, [arq], [bn], inc=(h % 4 == 3))
                    for h in range(8):
                        kn = B[2] if h < 4 else B[3]
                        k.mm(kn[:, 2 * (h % 4):2 * (h % 4) + 2, :], arq[:, h, 3, :], arq[:, h, 0:2, :], True, True, [arq], [kn], inc=(h % 4 == 3))
                    for h in range(8):
                        k.mm(B[4][:, h, :], arq[:, h, 0, :], arq[:, h, 2, :], True, True, [arq], [B[4]], inc=(h == 7))
                    for hf in range(2):
                        hs = slice(hf * 4, (hf + 1) * 4)
                        k.tt("dve", o["NL"][:, hs, :, :], B[hf][:].rearrange("s (h q) t -> s h q t", q=2), mpb[:, hs], ALU.mult, [B[hf], rm], [o["NL"]], acc=(hf == 1))
                        k.tt("dve", o["KL"][:, hs, :, :], B[2 + hf][:].rearrange("s (h q) t -> s h q t", q=2), mpb[:, hs], ALU.mult, [B[2 + hf], rm], [o["KL"]], acc=(hf == 1))
                    k.tt("dve", o["Pb"][0][:], B[4][:], mAb, ALU.mult, [B[4], rm], [o["Pb"][0]])
                    k.tt("dve", o["TTf"][:], o["NL"][:, :, 0, :], idb64, ALU.add, [o["NL"], self.ident_f], [o["TTf"]])
                    k.tt("dve", o["TTb"][0][:], o["NL"][:, :, 0, :], idb64, ALU.add, [o["NL"], self.ident_f], [o["TTb"][0]])
                for j in range(1, 6):
                    cur, prv = j % 2, (j - 1) % 2
                    for d_ in range(2):
                        o = D_[d_]
                        pN, pP, pT = B[3 * d_], B[3 * d_ + 1], B[3 * d_ + 2]
                        Nprev = (lambda h: o["NL"][:, h, 0, :]) if j == 1 else (lambda h: o["Nb"][prv][:, h, :])
                        Ntok = o["NL"] if j == 1 else o["Nb"][prv]
                        Pprev = o["Pb"][prv]
                        if j < 5:
                            for h in range(8):
                                k.mm(pN[:, h, :], Pprev[:, h, :], Nprev(h), True, True, [Pprev, Ntok], [pN], inc=(h == 7))
                        for h in range(8):
                            k.mm(pP[:, h, :], Nprev(h), Pprev[:, h, :], True, True, [Pprev, Ntok], [pP], inc=(h == 7))
                    for d_ in range(2):
                        o = D_[d_]
                        pN, pP, pT = B[3 * d_], B[3 * d_ + 1], B[3 * d_ + 2]
                        if j < 5:
                            k.cp("act", o["Nb"][cur][:], pN[:], [pN], [o["Nb"][cur]])
                        k.cp("act", o["Pb"][cur][:], pP[:], [pP], [o["Pb"][cur]])
                    for d_ in range(2):
                        o = D_[d_]
                        pT = B[3 * d_ + 2]
                        for h in range(8):
                            k.mm(pT[:, h, :], o["Pb"][cur][:, h, :], o["TTb"][prv][:, h, :], True, True, [o["Pb"][cur], o["TTb"][prv]], [pT], inc=(h == 7))
                    for d_ in range(2):
                        o = D_[d_]
                        pT = B[3 * d_ + 2]
                        k.tt("dve", o["TTf"][:], o["TTf"][:], pT[:], ALU.add, [o["TTf"], pT], [o["TTf"]])
                        k.cp("act", o["TTb"][cur][:], o["TTf"][:], [o["TTf"]], [o["TTb"][cur]])
                TTfin = 5 % 2
                for d_ in range(2):
                    o = D_[d_]
                    pX = B[4 * d_]
                    arq = o["arq"][s]
                    for h in range(8):
                        k.mm(pX[:, h, :], arq[:, h, 0, :], o["STb"][:, h, :], True, False, [arq, o["STb"]], [pX], inc=False)
                        k.mm(pX[:, h, :], o["KL"][:, h, 0, :], o["vt"][s][:, h, :], False, True, [o["KL"], o["vt"][s]], [pX], inc=(h == 7))
                for d_ in range(2):
                    o = D_[d_]
                    k.cp("act", o["Xb"][:], B[4 * d_][:], [B[4 * d_]], [o["Xb"]])
                for d_ in range(2):
                    o = D_[d_]
                    pU = B[4 * d_ + 1]
                    for h in range(8):
                        k.mm(pU[:, h, :], o["TTb"][TTfin][:, h, :], o["Xb"][:, h, :], True, True, [o["TTb"][TTfin], o["Xb"]], [pU], inc=(h == 7))
                for d_ in range(2):
                    o = D_[d_]
                    k.cp("dve", o["Ub"][:], B[4 * d_ + 1][:], [B[4 * d_ + 1]], [o["Ub"]])
                for d_ in range(2):
                    o = D_[d_]
                    pY, pS = B[4 * d_ + 2], B[4 * d_ + 3]
                    arq = o["arq"][s]
                    for h in range(8):
                        k.mm(pS[:, h, :], o["bh"][s][:, h, :], o["Ub"][:, h, :], True, False, [o["bh"][s], o["Ub"]], [pS], inc=False)
                        k.mm(pS[:, h, :], o["kh"][s][:, h, :], o["vt"][s][:, h, :], False, True, [o["kh"][s], o["vt"][s]], [pS], inc=(h == 7))
                    for h in range(8):
                        k.mm(pY[:, h, :], o["STb"][:, h, :], arq[:, h, 1, :], True, False, [o["STb"], arq], [pY], inc=False)
                        k.mm(pY[:, h, :], o["Ub"][:, h, :], o["NL"][:, h, 1, :], False, False, [o["Ub"], o["NL"]], [pY], inc=False)
                        k.mm(pY[:, h, :], o["vt"][s][:, h, :], o["KL"][:, h, 1, :], False, True, [o["vt"][s], o["KL"]], [pY], inc=(h == 7))
                for d_ in range(2):
                    o = D_[d_]
                    pY, pS = B[4 * d_ + 2], B[4 * d_ + 3]
                    ck = order[d_][i]
                    k.tt("dve", o["STf"][:], o["STf"][:], o["wc"][s][:].unsqueeze(2).broadcast_to([64, 8, 64]), ALU.mult, [o["STf"], o["wc"][s]], [o["STf"]])
                    k.tt("dve", o["STf"][:], o["STf"][:], pS[:], ALU.add, [o["STf"], pS], [o["STf"]])
                    k.cp("act", o["STb"][:], o["STf"][:], [o["STf"]], [o["STb"]])
                    ys = o["Ys"][s]
                    k.cp("act", ys[:], pY[:], [pY], [ys])
                    k.dma("pool", self.YD[d_][:, :, ck * CH:(ck + 1) * CH], ys[:], R=[ys])

    def phase_rwkv_post(self, l):
        k, nc = self.k, self.nc
        NTOK = self.NTOK
        PB = 256
        with ExitStack() as st:
            T_ = lambda nm, dt=F32: k.sb(st, [64, 8, PB], dt, nm)
            y0, y1, yc, sq, rs_, tmp, bvt, gt = T_("y0"), T_("y1"), T_("yc"), T_("sq"), T_("rs"), T_("tmp"), T_("bvt"), T_("gt")
            ob = [T_("orw", BF16) for _ in range(2)]
            pu_full = [k.ps(st, [128, 4, PB], F32, "ppu") for _ in range(4)]

            class _H2:
                def __init__(self, tt_):
                    self.tt_ = tt_
                    self.tok = tt_.tok

                def __getitem__(self, key):
                    if not isinstance(key, tuple):
                        key = (key,)
                    return self.tt_.t[(slice(0, 64),) + tuple(key[1:])]
            pu = [_H2(x_) for x_ in pu_full]
            ones64 = self.ones_f[0:64, 0:64]
            lnw = self.spc(l, "ln_w", rows=64)
            for bi in range(NTOK // PB):
                n0 = bi * PB
                k.dma("sp", y0[:], self.YD[0][:, :, n0:n0 + PB], W=[y0])
                k.dma("sp", y1[:], self.YD[1][:, :, n0:n0 + PB], W=[y1])
                k.dma("sp", bvt[:], self.RBV[:, :, n0:n0 + PB], W=[bvt])
                k.dma("sp", gt[:], self.RG[:, :, n0:n0 + PB], W=[gt])
                k.tt("dve", y0[:], y0[:], y1[:], ALU.add, [y0, y1], [y0])
                for hf in range(2):
                    hs = slice(hf * 4, (hf + 1) * 4)
                    pb = pu[hf]
                    for hh in range(4):
                        k.mm(pb[:, hh, :], ones64, y0[:, hf * 4 + hh, :], True, True, [self.ones_f, y0], [pb], inc=(hh == 3))
                    k.stt(yc[:, hs, :], pb[:], -1.0 / 64, y0[:, hs, :], ALU.mult, ALU.add, [pb, y0], [yc], acc=(hf == 1))
                k.act(sq[:], yc[:], AF.Square, [yc], [sq])
                for hf in range(2):
                    hs = slice(hf * 4, (hf + 1) * 4)
                    pb = pu[2 + hf]
                    for hh in range(4):
                        k.mm(pb[:, hh, :], ones64, sq[:, hf * 4 + hh, :], True, True, [self.ones_f, sq], [pb], inc=(hh == 3))
                    k.act(rs_[:, hs, :], pb[:], AF.Ln, [pb, self.epsG], [rs_], bias=self.epsG[0:64], scale=1.0 / 64, acc=(hf == 1))
                k.act(rs_[:], rs_[:], AF.Exp, [rs_], [rs_], scale=-0.5)
                for h in range(8):
                    k.stt(tmp[:, h, :], yc[:, h, :], lnw[:, h:h + 1], rs_[:, h, :], ALU.mult, ALU.mult, [yc, rs_, self.smallp], [tmp], acc=(h > 0))
                k.tt("dve", tmp[:], tmp[:], bvt[:], ALU.add, [tmp, bvt], [tmp])
                o_ = ob[bi % 2]
                k.tt("dve", o_[:], tmp[:], gt[:], ALU.mult, [tmp, gt], [o_])
                k.dma("pool", self.ORT[:, :, n0:n0 + PB], o_[:], R=[o_])

    def phase_merge(self, l, last):
        k, nc = self.k, self.nc
        XTv = self.XT.rearrange("k p n -> p k n")
        HTv = self.HT.rearrange("k p n -> p k n")
        OATv = self.OAT.rearrange("c p n -> p c n")
        OSTv = self.OST.rearrange("g p n -> p g n")
        with ExitStack() as st:
            wG = k.sb(st, [128, 8, 3072], BF16, "wG")
            for kc in range(8):
                k.dma("sp", wG[:, kc, :], self.WIN[l][kc * 128:(kc + 1) * 128, 3584:6656], W=[wG], acc=(kc > 0))
            wOA = k.sb(st, [128, 4, D], BF16, "wOA")
            wOS = k.sb(st, [128, 4, D], BF16, "wOS")
            wOR = k.sb(st, [64, 8, D], BF16, "wOR")
            wOU = k.sb(st, [128, 8, D], BF16, "wOU")
            k.dma("sp", wOA[:], self.WOA[l].rearrange("(k p) f -> p k f", p=128), W=[wOA])
            k.dma("sp", wOS[:], self.WOS[l].rearrange("(k p) f -> p k f", p=128), W=[wOS])
            k.dma("sp", wOR[:], self.WOR[l].rearrange("(h c) f -> c h f", c=64), W=[wOR])
            k.dma("sp", wOU[:], self.WOUT[l].rearrange("(k p) f -> p k f", p=128), W=[wOU])
            NB = 512
            xT = k.sb(st, [128, 8, NB], F32, "mxT")
            hT = [k.sb(st, [128, 8, NB], BF16, "mhT") for _ in range(2)]
            oaT = [k.sb(st, [128, 4, NB], BF16, "moa") for _ in range(2)]
            osT = [k.sb(st, [128, 4, NB], BF16, "mos") for _ in range(2)]
            orT = [k.sb(st, [64, 8, NB], BF16, "mor") for _ in range(2)]
            mT = k.sb(st, [128, 8, NB], BF16, "mmT")
            gs = [k.sb(st, [128, NB], F32, "mgs") for _ in range(3)]
            acc_ = k.sb(st, [128, NB], F32, "macc")
            tmp = k.sb(st, [128, NB], F32, "mtmp")
            pg = [k.ps(st, [128, NB], F32, "mpg") for _ in range(3)]
            pp = [k.ps(st, [128, NB], F32, "mpp") for _ in range(3)]
            px = [k.ps(st, [128, NB], F32, "mpx") for _ in range(2)]
            blocks = [b for b in self.blocks if not (last and b[2])]
            for bi, blk in enumerate(blocks):
                n0, nb, is_ctx, t0 = blk
                m = 1 if is_ctx else 0
                s = bi % 2
                k.dma("sp", hT[s][:, :, 0:nb], HTv[:, :, n0:n0 + nb], W=[hT[s]])
                k.dma("sp", oaT[s][:, :, 0:nb], OATv[:, :, n0:n0 + nb], W=[oaT[s]])
                k.dma("sp", osT[s][:, :, 0:nb], OSTv[:, :, n0:n0 + nb], W=[osT[s]])
                k.dma("sp", orT[s][:, :, 0:nb], self.ORT[:, :, n0:n0 + nb], W=[orT[s]])
                k.dma("sp", xT[:, :, 0:nb], XTv[:, :, n0:n0 + nb], W=[xT])
                for fc in range(8):
                    fs = slice(fc * 128, (fc + 1) * 128)
                    for b in range(3):
                        for kc in range(8):
                            k.mm(pg[b][:, 0:nb], wG[:, kc, b * 1024 + fc * 128:b * 1024 + (fc + 1) * 128], hT[s][:, kc, 0:nb], kc == 0, kc == 7, [wG, hT[s]], [pg[b]])
                        k.act(gs[b][:, 0:nb], pg[b][:, 0:nb], AF.Sigmoid, [pg[b]], [gs[b]])
                    for kc in range(4):
                        k.mm(pp[0][:, 0:nb], wOA[:, kc, fs], oaT[s][:, kc, 0:nb], kc == 0, kc == 3, [wOA, oaT[s]], [pp[0]])
                    for h in range(8):
                        k.mm(pp[1][:, 0:nb], wOR[:, h, fs], orT[s][:, h, 0:nb], h == 0, h == 7, [wOR, orT[s]], [pp[1]])
                    for kc in range(4):
                        k.mm(pp[2][:, 0:nb], wOS[:, kc, fs], osT[s][:, kc, 0:nb], kc == 0, kc == 3, [wOS, osT[s]], [pp[2]])
                    k.tt("dve", acc_[:, 0:nb], gs[0][:, 0:nb], pp[0][:, 0:nb], ALU.mult, [gs[0], pp[0]], [acc_])
                    k.tt("dve", tmp[:, 0:nb], gs[1][:, 0:nb], pp[1][:, 0:nb], ALU.mult, [gs[1], pp[1]], [tmp])
                    k.tt("dve", acc_[:, 0:nb], acc_[:, 0:nb], tmp[:, 0:nb], ALU.add, [acc_, tmp], [acc_])
                    k.tt("dve", tmp[:, 0:nb], gs[2][:, 0:nb], pp[2][:, 0:nb], ALU.mult, [gs[2], pp[2]], [tmp])
                    k.tt("dve", mT[:, fc, 0:nb], acc_[:, 0:nb], tmp[:, 0:nb], ALU.add, [acc_, tmp], [mT], acc=(fc > 0))
                for fc in range(8):
                    fs = slice(fc * 128, (fc + 1) * 128)
                    p_ = px[fc % 2]
                    for kc in range(8):
                        k.mm(p_[:, 0:nb], wOU[:, kc, fs], mT[:, kc, 0:nb], kc == 0, kc == 7, [wOU, mT], [p_])
                    k.stt(xT[:, fc, 0:nb], p_[:, 0:nb], self.mods[:, m, 2, fc:fc + 1], xT[:, fc, 0:nb], ALU.mult, ALU.add, [p_, self.mods, xT], [xT])
                k.dma("pool", XTv[:, :, n0:n0 + nb], xT[:, :, 0:nb], R=[xT])

    def phase_ffn(self, l, last):
        k, nc = self.k, self.nc
        XTv = self.XT.rearrange("k p n -> p k n")
        NB = 256
        with ExitStack() as st:
            w1 = k.sb(st, [128, 8, FFN], BF16, "w1")
            w3 = k.sb(st, [128, 8, FFN], BF16, "w3")
            w2 = k.sb(st, [128, NFC, D], BF16, "w2")
            for kc in range(8):
                k.dma("sp", w1[:, kc, :], self.W1[l][kc * 128:(kc + 1) * 128, :], W=[w1], acc=(kc > 0))
                k.dma("sp", w3[:, kc, :], self.W3[l][kc * 128:(kc + 1) * 128, :], W=[w3], acc=(kc > 0))
            for hc in range(NFC):
                k.dma("sp", w2[:, hc, :], self.W2[l][hc * 128:(hc + 1) * 128, :], W=[w2], acc=(hc > 0))
            xTs = [k.sb(st, [128, 8, NB], F32, "fxT") for _ in range(2)]
            work = k.sb(st, [128, 8, NB], F32, "fwork")
            hTs = [k.sb(st, [128, 8, NB], BF16, "fhT") for _ in range(2)]
            rstd = k.sb(st, [128, NB], F32, "frstd")
            hid = k.sb(st, [128, NFC, NB], BF16, "fhid")
            sl = [k.sb(st, [128, NB], F32, "fsl") for _ in range(2)]
            pss = k.ps(st, [128, 512], F32, "fpss")
            pa = [k.ps(st, [128, 512], F32, "fpa") for _ in range(2)]
            pb_ = [k.ps(st, [128, 512], F32, "fpb") for _ in range(2)]
            po = [k.ps(st, [128, 512], F32, "fpo") for _ in range(2)]
            if last:
                ptr = k.ps(st, [128, 512], F32, "fptr")
                xo = [k.sb(st, [128, D], F32, "fxo") for _ in range(1)]
            nblk = self.NTOK // NB
            blist = [bi for bi in range(nblk) if not (last and bi * NB < LCTX)]

            def do_norm(ii):
                bi_ = blist[ii]
                n0_ = bi_ * NB
                xT_ = xTs[ii % 2]
                k.dma("sp", xT_[:], XTv[:, :, n0_:n0_ + NB], W=[xT_])
                self.norm_mod((xT_, work, hTs[ii % 2], rstd, pss), (n0_, NB, n0_ < LCTX, 0), l, 1)

            do_norm(0)
            for ii, bi in enumerate(blist):
                n0 = bi * NB
                is_ctx = n0 < LCTX
                m = 1 if is_ctx else 0
                xT = xTs[ii % 2]
                hT = hTs[ii % 2]
                if ii + 1 < len(blist):
                    do_norm(ii + 1)
                for hc in range(NFC):
                    hs = slice(hc * 128, (hc + 1) * 128)
                    a_, b_ = pa[hc % 2], pb_[hc % 2]
                    for kc in range(8):
                        k.mm(a_[:, 0:NB], w1[:, kc, hs], hT[:, kc, :], kc == 0, kc == 7, [w1, hT], [a_])
                    for kc in range(8):
                        k.mm(b_[:, 0:NB], w3[:, kc, hs], hT[:, kc, :], kc == 0, kc == 7, [w3, hT], [b_])
                    s_ = sl[hc % 2]
                    k.act(s_[:], a_[:, 0:NB], AF.Silu, [a_], [s_])
                    k.tt("dve", hid[:, hc, :], s_[:], b_[:, 0:NB], ALU.mult, [s_, b_], [hid], acc=(hc > 0))
                for fc in range(8):
                    fs = slice(fc * 128, (fc + 1) * 128)
                    p_ = po[fc % 2]
                    for hc in range(NFC):
                        k.mm(p_[:, 0:NB], w2[:, hc, fs], hid[:, hc, :], hc == 0, hc == NFC - 1, [w2, hid], [p_])
                    k.stt(xT[:, fc, :], p_[:, 0:NB], self.mods[:, m, 5, fc:fc + 1], xT[:, fc, :], ALU.mult, ALU.add, [p_, self.mods, xT], [xT])
                if not last:
                    k.dma("pool", XTv[:, :, n0:n0 + NB], xT[:], R=[xT])
                else:
                    for tt_ in range(NB // 128):
                        xo_ = xo[0]
                        for half in range(2):
                            for j in range(4):
                                kc = half * 4 + j
                                k.tr(ptr[:, j * 128:(j + 1) * 128], xT[:, kc, tt_ * 128:(tt_ + 1) * 128], self.ident_f[:], [xT, self.ident_f], [ptr], inc=(j == 3))
                            k.cp("act" if half == 0 else "dve", xo_[:, half * 512:(half + 1) * 512], ptr[:], [ptr], [xo_], acc=(half == 1))
                        t0 = n0 - LCTX + tt_ * 128
                        k.dma("pool", self.out[t0:t0 + 128, :], xo_[:], R=[xo_])


_SMALL_NAMES = ("b_mod", "norm_mix", "norm_ffn", "q_gain", "k_gain", "attn_sink", "rwkv_conv", "rwkv_w0", "rwkv_a0",
                "rwkv_k_k", "rwkv_k_a", "rwkv_r_k", "rwkv_ln_w", "rwkv_ln_b", "rwkv_vres_b", "sgu_ln_w", "sgu_ln_b", "sgu_b")


def make_in_maps(inputs, T, ncores):
    inp = {k_: np.asarray(v) for k_, v in inputs.items()}
    hc = host_consts(T)
    smallp = np.stack([pack_small(inp, l) for l in range(DEPTH)]).astype(np.float32)
    shared = {"smallp": smallp}
    for name, arr in hc.items():
        shared["c_" + name] = np.ascontiguousarray(arr, dtype=np.float32)
    for name, shape in PARAMS:
        if name in _SMALL_NAMES:
            continue
        shared[name] = np.ascontiguousarray(inp[name], dtype=np.float32)
    maps = []
    for b in range(ncores):
        m = dict(shared)
        m["x"] = np.ascontiguousarray(inp["x"][b, :T], dtype=np.float32)
        m["ctx"] = np.ascontiguousarray(inp["ctx"][b], dtype=np.float32)
        cf = np.concatenate([inp["c"][b].reshape(8, 128).T, inp["c_ctx"].reshape(8, 128).T], axis=1)
        m["cfm"] = np.ascontiguousarray(cf, dtype=np.float32)
        maps.append(m)
    return maps


def kernel(**inputs):
    T = 4096
    B = 8
    prog = Prog(T)
    nc = prog.build()
    maps = make_in_maps(inputs, T, B)
    res = run_bass_kernel_spmd(nc, maps, core_ids=list(range(B)))
    out = np.stack([np.asarray(res.results[b]["out"], dtype=np.float32) for b in range(B)])
    return out
```

```python
import numpy as np
import ml_dtypes
from contextlib import ExitStack
import concourse.bass as bass
import concourse.mybir as mybir
from concourse.bass_utils import run_bass_kernel_spmd

F32 = mybir.dt.float32
BF16 = mybir.dt.bfloat16
AF = mybir.ActivationFunctionType
ALU = mybir.AluOpType
AX = mybir.AxisListType

D = 1024
DEPTH = 2
LCTX = 256
NH = 8
HD = 64
IN_DIM = 6656
FFN = 2816
NFC = FFN // 128
EPS = 1e-6
GN_EPS = 64e-5
CH = 64
NEG = -30000.0


class Tok:
    __slots__ = ("w", "r", "pw", "pr")

    def __init__(self):
        self.w = []
        self.r = {}
        self.pw = []
        self.pr = {}


class TT:
    def __init__(self, t):
        self.t = t
        self.tok = Tok()

    def __getitem__(self, k):
        return self.t[k]


class KB:
    def __init__(self, nc, es):
        self.nc = nc
        self.es = es
        self.E = dict(pe=nc.tensor, dve=nc.vector, act=nc.scalar, pool=nc.gpsimd, sp=nc.sync)
        self.semobj = {}
        self.cnt = {}
        for k in self.E:
            self.semobj[k] = es.enter_context(nc.semaphore("c_" + k))
            self.cnt[k] = 0
        self.known = {k: {} for k in self.E}
        self.NDS = 8
        self.dq = {}
        for q in ("sp", "pool", "act"):
            names = []
            for i in range(self.NDS):
                nm = "d_%s%d" % (q, i)
                self.semobj[nm] = es.enter_context(nc.semaphore(nm))
                self.cnt[nm] = 0
                names.append(nm)
            self.dq[q] = [names, 0]
        self.uid = 0

    def sb(self, st, shape, dt, name=None):
        self.uid += 1
        return TT(st.enter_context(self.nc.sbuf_tensor("%s_%d" % (name or "t", self.uid), list(shape), dt)))

    def ps(self, st, shape, dt=F32, name=None):
        self.uid += 1
        return TT(st.enter_context(self.nc.psum_tensor("%s_%d" % (name or "p", self.uid), list(shape), dt)))

    def _wait(self, eng, ev):
        s, v = ev
        if v <= 0:
            return
        if self.known[eng].get(s, 0) >= v:
            return
        self.E[eng].wait_ge(self.semobj[s], v)
        self.known[eng][s] = v

    def _deps(self, eng, R, W, acc=False):
        for t in R:
            tk = t.tok if hasattr(t, 'tok') else t
            for ev in tk.w:
                if not (eng == "pe" and ev[0] == "pe"):
                    self._wait(eng, ev)
        for t in W:
            tk = t.tok if hasattr(t, 'tok') else t
            for ev in (tk.pw if acc else tk.w):
                if not (eng == "pe" and ev[0] == "pe"):
                    self._wait(eng, ev)
            rr = list(tk.r.items()) + (list(tk.pr.items()) if acc else [])
            for s, v in rr:
                if eng == "pe" and s == "pe":
                    continue
                self._wait(eng, (s, v))

    def _record(self, ev, R, W, acc=False):
        for t in R:
            tk = t.tok if hasattr(t, 'tok') else t
            if tk.r.get(ev[0], 0) < ev[1]:
                tk.r[ev[0]] = ev[1]
        for t in W:
            tk = t.tok if hasattr(t, 'tok') else t
            if acc:
                tk.w = [e for e in tk.w if e[0] != ev[0]] + [ev]
            else:
                tk.pw = tk.w
                tk.pr = tk.r
                tk.w = [ev]
                tk.r = {}

    def op(self, eng, fn, R=(), W=(), inc=True, acc=False):
        self._deps(eng, R, W, acc)
        ins = fn()
        ev = (eng, self.cnt[eng] + 1)
        if inc:
            ins.then_inc(self.semobj[eng], 1)
            self.cnt[eng] += 1
        self._record(ev, R, W, acc)
        return ins

    def dma(self, q, out, in_, R=(), W=(), acc=False, **kw):
        self._deps(q, R, W, acc)
        names, i = self.dq[q]
        nm = names[i % self.NDS]
        self.dq[q][1] = i + 1
        self._wait(q, (nm, self.cnt[nm]))
        self.E[q].dma_start(out=out, in_=in_, **kw).then_inc(self.semobj[nm], 16)
        self.cnt[nm] += 16
        self._record((nm, self.cnt[nm]), R, W, acc)

    def barrier(self):
        for e in self.E:
            for s, v in self.cnt.items():
                if s == e and e != "pe":
                    pass
                self._wait(e, (s, v))

    def mm(self, out, lhsT, rhs, start, stop, R, W, inc=None):
        if inc is None:
            inc = stop
        return self.op("pe", lambda: self.nc.tensor.matmul(out, lhsT=lhsT, rhs=rhs, start=start, stop=stop), R, W, inc)

    def tr(self, out, in_, ident, R, W, inc=True):
        return self.op("pe", lambda: self.nc.tensor.transpose(out, in_, ident), R, W, inc)

    def act(self, out, in_, func, R, W, bias=None, scale=None, accum=None, acc=False):
        kw = {}
        if bias is not None:
            kw["bias"] = bias
        if scale is not None:
            kw["scale"] = scale
        if accum is not None:
            kw["accum_out"] = accum
        return self.op("act", lambda: self.nc.scalar.activation(out=out, in_=in_, func=func, **kw), R, W, acc=acc)

    def tt(self, eng, out, in0, in1, op, R, W, acc=False):
        e = self.E[eng]
        return self.op(eng, lambda: e.tensor_tensor(out=out, in0=in0, in1=in1, op=op), R, W, acc=acc)

    def ts(self, eng, out, in0, s1, op0, R, W, s2=None, op1=None, accum=None, acc=False):
        e = self.E[eng]
        kw = {}
        if op1 is not None:
            kw["op1"] = op1
        if accum is not None:
            kw["accum_out"] = accum
        return self.op(eng, lambda: e.tensor_scalar(out=out, in0=in0, scalar1=s1, scalar2=s2, op0=op0, **kw), R, W, acc=acc)

    def stt(self, out, in0, scalar, in1, op0, op1, R, W, acc=False):
        return self.op("dve", lambda: self.nc.vector.scalar_tensor_tensor(out=out, in0=in0, scalar=scalar, in1=in1, op0=op0, op1=op1), R, W, acc=acc)

    def cp(self, eng, out, in_, R, W, acc=False):
        e = self.E[eng]
        if eng == "act":
            return self.op(eng, lambda: e.copy(out=out, in_=in_), R, W, acc=acc)
        return self.op(eng, lambda: e.tensor_copy(out=out, in_=in_), R, W, acc=acc)

    def recip(self, out, in_, R, W):
        return self.op("dve", lambda: self.nc.vector.reciprocal(out=out, in_=in_), R, W)

    def memset(self, eng, ap, val, W):
        e = self.E[eng]
        return self.op(eng, lambda: e.memset(ap, val), (), W)


def host_consts(T):
    c = {}
    c["ident_f"] = np.eye(128, dtype=np.float32)
    ob = np.zeros((128, 128), np.float32)
    ob[:64, :64] = 1.0
    ob[64:, 64:] = 1.0
    c["onesblk"] = ob
    c["ones_f"] = np.ones((128, 128), np.float32)
    rm = np.zeros((128, 128), np.float32)
    for h in range(2):
        for d in range(64):
            if d < 32:
                rm[h * 64 + d + 32, h * 64 + d] = -1.0
            else:
                rm[h * 64 + d - 32, h * 64 + d] = 1.0
    c["rotm"] = rm
    rows = T // 64
    row = np.repeat(np.arange(rows), 64).astype(np.float32)
    col = np.tile(np.arange(64), rows).astype(np.float32)
    inv = (10000.0 ** (-np.arange(16, dtype=np.float32) / 16)).astype(np.float32)
    ang = np.concatenate([row[:, None] * inv, col[:, None] * inv], -1)
    cos = np.cos(ang).astype(np.float32).T
    sin = np.sin(ang).astype(np.float32).T
    c["ropecos"] = np.ascontiguousarray(np.tile(cos, (4, 1)))
    c["ropesin"] = np.ascontiguousarray(np.tile(sin, (4, 1)))
    j = np.arange(128)[:, None]
    p = np.arange(128)[None, :]
    mprev = np.where(j >= p, 0.0, NEG).astype(np.float32)
    mnext = np.where(j <= p, 0.0, NEG).astype(np.float32)
    c["maskprev"] = np.ascontiguousarray(np.tile(mprev, (1, 4)))
    c["masknext"] = np.ascontiguousarray(np.tile(mnext, (1, 4)))
    s = np.arange(64)[:, None]
    t = np.arange(64)[None, :]
    m = np.zeros((64, 4, 64), np.float32)
    m[:, 0, :] = (s < t)
    m[:, 1, :] = (s <= t)
    m[:, 2, :] = (s > t)
    m[:, 3, :] = (s >= t)
    c["rmask"] = m
    rs = np.ones((128, 8, 128), np.float32)
    rs[:, :, 0::64] = 0.0
    c["scanreset"] = rs
    return c


CONST_DT = {"ident_f": F32, "onesblk": F32, "ones_f": F32, "rotm": F32, "ropecos": F32, "ropesin": F32,
            "maskprev": F32, "masknext": F32, "rmask": F32, "scanreset": F32}

PARAMS = [
    ("w_mod", (DEPTH, D, 6 * D)), ("b_mod", (DEPTH, 6 * D)), ("norm_mix", (DEPTH, D)), ("norm_ffn", (DEPTH, D)),
    ("w_in", (DEPTH, D, IN_DIM)), ("q_gain", (DEPTH, 64)), ("k_gain", (DEPTH, 64)), ("attn_sink", (DEPTH, 8)),
    ("rwkv_conv", (DEPTH, 3, 1792)), ("rwkv_w0", (DEPTH, 2, 512)), ("rwkv_w_up", (DEPTH, 2, 64, 512)),
    ("rwkv_a0", (DEPTH, 2, 512)), ("rwkv_a_up", (DEPTH, 2, 64, 512)), ("rwkv_k_k", (DEPTH, 512)),
    ("rwkv_k_a", (DEPTH, 512)), ("rwkv_r_k", (DEPTH, 8, 64)), ("rwkv_g_up", (DEPTH, 128, 512)),
    ("rwkv_ln_w", (DEPTH, 512)), ("rwkv_ln_b", (DEPTH, 512)), ("rwkv_vres_down", (DEPTH - 1, D, 32)),
    ("rwkv_vres_up", (DEPTH - 1, 32, 512)), ("rwkv_vres_b", (DEPTH - 1, 512)), ("sgu_ln_w", (DEPTH, 512)),
    ("sgu_ln_b", (DEPTH, 512)), ("sgu_w", (DEPTH, 4, 128, 128)), ("sgu_b", (DEPTH, 4, 128)),
    ("w_o_attn", (DEPTH, 512, D)), ("w_o_rwkv", (DEPTH, 512, D)), ("w_o_sgu", (DEPTH, 512, D)),
    ("w_out", (DEPTH, D, D)), ("ffn_w1", (DEPTH, D, FFN)), ("ffn_w3", (DEPTH, D, FFN)), ("ffn_w2", (DEPTH, FFN, D)),
]


SP_LAYOUT = [("bmod", 48), ("nmix", 8), ("nffn", 8), ("gain", 2), ("sink", 8), ("conv_rkv", 72), ("conv_wa", 6),
             ("conv_g", 3), ("w0", 8), ("a0", 8), ("k_k", 8), ("k_a", 8), ("r_k", 8), ("ln_w", 8), ("ln_b", 8),
             ("vres_b", 8), ("sgu_lnw", 512), ("sgu_lnb", 512), ("sgu_b", 512)]
SP_OFF = {}
_o = 0
for _n, _w in SP_LAYOUT:
    SP_OFF[_n] = (_o, _w)
    _o += _w
SP_N = _o


def pack_small(inp, l):
    a = np.zeros((128, SP_N), np.float32)

    def put(name, arr):
        o, w = SP_OFF[name]
        arr = np.asarray(arr, np.float32)
        a[: arr.shape[0], o:o + w] = arr.reshape(arr.shape[0], -1)

    put("bmod", inp["b_mod"][l].reshape(48, 128).T)
    put("nmix", inp["norm_mix"][l].reshape(8, 128).T)
    put("nffn", inp["norm_ffn"][l].reshape(8, 128).T)
    put("gain", np.stack([np.tile(inp["q_gain"][l], 2), np.tile(inp["k_gain"][l], 2)], 1))
    put("sink", np.tile(inp["attn_sink"][l][None, :], (128, 1)))
    cv = inp["rwkv_conv"][l]
    put("conv_rkv", cv[:, :1536].reshape(3, 3, 8, 64).transpose(3, 0, 1, 2))
    put("conv_wa", cv[:, 1536:1664].reshape(3, 2, 64).transpose(2, 0, 1))
    put("conv_g", cv[:, 1664:1792].T)
    dup = lambda a_: np.concatenate([a_, a_], 0)
    put("w0", inp["rwkv_w0"][l].reshape(2, 8, 64).transpose(0, 2, 1).reshape(128, 8))
    put("a0", inp["rwkv_a0"][l].reshape(2, 8, 64).transpose(0, 2, 1).reshape(128, 8))
    put("k_k", dup(inp["rwkv_k_k"][l].reshape(8, 64).T))
    put("k_a", dup(inp["rwkv_k_a"][l].reshape(8, 64).T))
    put("r_k", dup(inp["rwkv_r_k"][l].reshape(8, 64).T))
    put("ln_w", inp["rwkv_ln_w"][l].reshape(8, 64).T)
    put("ln_b", inp["rwkv_ln_b"][l].reshape(8, 64).T)
    if l > 0:
        put("vres_b", dup(inp["rwkv_vres_b"][l - 1].reshape(8, 64).T))
    put("sgu_lnw", np.tile(inp["sgu_ln_w"][l][None, :], (128, 1)))
    put("sgu_lnb", np.tile(inp["sgu_ln_b"][l][None, :], (128, 1)))
    put("sgu_b", inp["sgu_b"][l].reshape(1, 512))
    return a


class Prog:
    def __init__(self, T, nlayers=DEPTH, dbg=(), stop_after=None):
        self.T = T
        self.NTOK = LCTX + T
        self.nlayers = nlayers
        self.dbg = set(dbg)
        self.stop_after = stop_after
        self.blocks = [(0, LCTX, True, 0)] + [(LCTX + i * 512, 512, False, i * 512) for i in range(T // 512)]
        self.nc = bass.Bass("TRN2", target_bir_lowering=False)
        self.es = ExitStack()

    def din(self, name, shape, dt=F32):
        return self.nc.dram_tensor(name, list(shape), dt, kind="ExternalInput").ap()

    def scr(self, name, shape, dt):
        kind = "ExternalOutput" if name in self.dbg else "Internal"
        return self.nc.dram_tensor(name, list(shape), dt, kind=kind).ap()

    def build(self):
        nc, T, NTOK = self.nc, self.T, self.NTOK
        with self.es as es:
            k = self.k = KB(nc, es)
            self.x_in = self.din("x", [T, D])
            self.ctx_in = self.din("ctx", [LCTX, D])
            self.cfm = self.din("cfm", [128, 16])
            self.smallp_d = self.din("smallp", [DEPTH, 128, SP_N])
            self.cd = {}
            hc = host_consts(T)
            for name, arr in hc.items():
                self.cd[name] = self.din("c_" + name, arr.shape)
            self.pd = {}
            for name, shape in PARAMS:
                if name in ("b_mod", "norm_mix", "norm_ffn", "q_gain", "k_gain", "attn_sink", "rwkv_conv", "rwkv_w0",
                            "rwkv_a0", "rwkv_k_k", "rwkv_k_a", "rwkv_r_k", "rwkv_ln_w", "rwkv_ln_b", "rwkv_vres_b",
                            "sgu_ln_w", "sgu_ln_b", "sgu_b"):
                    continue
                self.pd[name] = self.din(name, shape)
            self.out = nc.dram_tensor("out", [T, D], F32, kind="ExternalOutput").ap()
            L = self.nlayers
            self.XT = self.scr("XT", [8, 128, NTOK], F32)
            self.HT = self.scr("HT", [8, 128, NTOK], BF16)
            self.QT = self.scr("QT", [4, 128, NTOK], BF16)
            self.KT = self.scr("KT", [128, NTOK], BF16)
            self.VA = self.scr("VA", [NTOK, 130], BF16)
            self.PRW = self.scr("PRW", [1792, NTOK], F32)
            self.PRB = self.scr("PRB", [1536, NTOK], BF16)
            self.OST = self.scr("OST", [4, 128, NTOK], BF16)
            self.OAT = self.scr("OAT", [4, 128, NTOK], BF16)
            self.ORT = self.scr("ORT", [64, 8, NTOK], BF16)
            NCH = NTOK // CH
            self.RA4 = [self.scr("RA4_%d" % d_, [64, 8, 4, NTOK], BF16) for d_ in range(2)]
            self.TOKB = self.scr("TOKB", [NTOK, 8, 2, 128], BF16)
            self.TOKV = self.scr("TOKV", [NTOK, 8, 64], BF16)
            self.RWC = [self.scr("RWC_%d" % d_, [NCH, 64, 8], F32) for d_ in range(2)]
            self.RBV = self.scr("RBV", [64, 8, NTOK], F32)
            self.RG = self.scr("RG", [64, 8, NTOK], F32)
            self.VF = self.scr("VF", [64, 8, NTOK], F32)
            self.YD = [self.scr("YD_%d" % d_, [64, 8, NTOK], F32) for d_ in range(2)]
            self.WIN = [self.scr("WIN%d" % l, [D, IN_DIM], BF16) for l in range(L)]
            self.WOA = [self.scr("WOA%d" % l, [512, D], BF16) for l in range(L)]
            self.WOR = [self.scr("WOR%d" % l, [512, D], BF16) for l in range(L)]
            self.WOS = [self.scr("WOS%d" % l, [512, D], BF16) for l in range(L)]
            self.WOUT = [self.scr("WOUT%d" % l, [D, D], BF16) for l in range(L)]
            self.W1 = [self.scr("W1_%d" % l, [D, FFN], BF16) for l in range(L)]
            self.W3 = [self.scr("W3_%d" % l, [D, FFN], BF16) for l in range(L)]
            self.W2 = [self.scr("W2_%d" % l, [FFN, D], BF16) for l in range(L)]

            g = es
            self.smallp = k.sb(g, [128, DEPTH, SP_N], F32, "smallp")
            for l in range(DEPTH):
                k.dma("sp", self.smallp[:, l, :], self.smallp_d[l], W=[self.smallp], acc=(l > 0))
            self.ident_f = k.sb(g, [128, 128], F32, "identf")
            self.ones_f = k.sb(g, [128, 128], F32, "onesf")
            self.onesblk_f = k.sb(g, [128, 128], F32, "onesblkf")
            self.rotm_f = k.sb(g, [128, 128], F32, "rotmf")
            k.dma("sp", self.ident_f[:], self.cd["ident_f"], W=[self.ident_f])
            k.dma("sp", self.ones_f[:], self.cd["ones_f"], W=[self.ones_f])
            k.dma("sp", self.onesblk_f[:], self.cd["onesblk"], W=[self.onesblk_f])
            k.dma("sp", self.rotm_f[:], self.cd["rotm"], W=[self.rotm_f])
            self.ident_b = k.sb(g, [128, 128], BF16, "identb")
            self.onesblk_b = k.sb(g, [128, 128], BF16, "onesblkb")
            self.rotm_b = k.sb(g, [128, 128], BF16, "rotmb")
            k.cp("dve", self.ident_b[:], self.ident_f[:], [self.ident_f], [self.ident_b])
            k.cp("dve", self.onesblk_b[:], self.onesblk_f[:], [self.onesblk_f], [self.onesblk_b])
            k.cp("dve", self.rotm_b[:], self.rotm_f[:], [self.rotm_f], [self.rotm_b])
            self.mods = k.sb(g, [128, 2, 6, 8], F32, "mods")
            self.epsD = k.sb(g, [128, 1], F32, "epsD")
            k.memset("dve", self.epsD[:], EPS, [self.epsD])
            self.epsG = k.sb(g, [128, 1], F32, "epsG")
            k.memset("dve", self.epsG[:], GN_EPS, [self.epsG])

            sa = self.stop_after
            if sa is None or sa[0] != "consts":
                self.phase_weights()
            k.barrier()
            if sa is not None and sa[0] in ("weights", "consts"):
                L = 0
            if L > 0:
                self.phase_transpose_in()
                k.barrier()
            if sa is not None and sa[0] == "tin":
                L = 0
            for l in range(L):
                self.phase_mod(l)
                k.barrier()
                if sa == ("mod", l):
                    break
                self.phase_inproj(l)
                k.barrier()
                if self.stop_after == ("inproj", l):
                    break
                last = (l == self.nlayers - 1)
                self.phase_attn(l, last)
                k.barrier()
                if self.stop_after == ("attn", l):
                    break
                self.phase_rwkv_prep(l)
                k.barrier()
                if self.stop_after == ("rprep", l):
                    break
                self.phase_rwkv_scan(l)
                k.barrier()
                self.phase_rwkv_post(l)
                k.barrier()
                if self.stop_after == ("rwkv", l):
                    break
                self.phase_merge(l, last)
                k.barrier()
                self.phase_ffn(l, last)
                k.barrier()
                if self.stop_after == ("layer", l):
                    break
            self.finish()
        return nc

    def spc(self, l, name, rows=128):
        o, w = SP_OFF[name]
        return self.smallp[0:rows, l, o:o + w]

    def finish(self):
        k = self.k
        k.barrier()

    def phase_weights(self):
        k, nc = self.k, self.nc
        CW = 3328
        with ExitStack() as st:
            stg = [k.sb(st, [128, CW], F32, "wstg") for _ in range(3)]
            bfb = [k.sb(st, [128, CW], BF16, "wbf") for _ in range(3)]
            engs = ["dve", "act", "dve"]
            ctr = [0]

            def conv(src, dst, R, C, permq=False):
                for r0 in range(0, R, 128):
                    rr = min(128, R - r0)
                    for c0 in range(0, C, CW):
                        cc = min(CW, C - c0)
                        i = ctr[0] % 3
                        ctr[0] += 1
                        k.dma("sp", stg[i][0:rr, 0:cc], src[r0:r0 + rr, c0:c0 + cc], W=[stg[i]])
                        k.cp(engs[i], bfb[i][0:rr, 0:cc], stg[i][0:rr, 0:cc], [stg[i]], [bfb[i]])
                        if permq and c0 == 0:
                            for g_ in range(2):
                                k.dma("pool", dst[r0:r0 + rr, 0:512].rearrange("p (j g d) -> p j g d", j=4, g=2, d=64)[:, :, g_, :],
                                      bfb[i][0:rr, 0:512].rearrange("p (g j d) -> p g j d", g=2, j=4, d=64)[:, g_, :, :], R=[bfb[i]])
                            k.dma("pool", dst[r0:r0 + rr, 512:cc], bfb[i][0:rr, 512:cc], R=[bfb[i]])
                        else:
                            k.dma("pool", dst[r0:r0 + rr, c0:c0 + cc], bfb[i][0:rr, 0:cc], R=[bfb[i]])

            for l in range(self.nlayers):
                conv(self.pd["w_in"][l], self.WIN[l], D, IN_DIM, permq=True)
                conv(self.pd["w_o_attn"][l], self.WOA[l], 512, D)
                conv(self.pd["w_o_rwkv"][l], self.WOR[l], 512, D)
                conv(self.pd["w_o_sgu"][l], self.WOS[l], 512, D)
                conv(self.pd["w_out"][l], self.WOUT[l], D, D)
                conv(self.pd["ffn_w1"][l], self.W1[l], D, FFN)
                conv(self.pd["ffn_w3"][l], self.W3[l], D, FFN)
                conv(self.pd["ffn_w2"][l], self.W2[l], FFN, D)

    def phase_transpose_in(self):
        k, nc = self.k, self.nc
        XTv = self.XT.rearrange("k p n -> p k n")
        with ExitStack() as st:
            xin = [k.sb(st, [128, D], F32, "xin") for _ in range(2)]
            xo = [k.sb(st, [128, 8, 128], F32, "xo") for _ in range(2)]
            pbs = [k.ps(st, [128, 4, 128], F32, "ptr") for _ in range(4)]
            ntile = self.NTOK // 128
            for i in range(ntile):
                src = self.ctx_in[i * 128:(i + 1) * 128, :] if i < 2 else self.x_in[(i - 2) * 128:(i - 1) * 128, :]
                xi, xoo = xin[i % 2], xo[i % 2]
                k.dma("sp", xi[:], src, W=[xi])
                for hlf in range(2):
                    pb = pbs[(2 * i + hlf) % 4]
                    for j in range(4):
                        kc = hlf * 4 + j
                        k.tr(pb[:, j, :], xi[:, kc * 128:(kc + 1) * 128], self.ident_f[:], [xi, self.ident_f], [pb], inc=(j == 3))
                    eng = "act" if hlf == 0 else "dve"
                    k.cp(eng, xoo[:, hlf * 4:(hlf + 1) * 4, :], pb[:], [pb], [xoo], acc=(hlf == 1))
                k.dma("pool", XTv[:, :, i * 128:(i + 1) * 128], xoo[:], R=[xoo])

    def phase_mod(self, l):
        k, nc = self.k, self.nc
        with ExitStack() as st:
            cs = k.sb(st, [128, 16], F32, "cs")
            sc = k.sb(st, [128, 8, 2], F32, "sc")
            k.dma("sp", cs[:], self.cfm, W=[cs])
            k.act(sc[:].rearrange("p k m -> p m k"), cs[:].rearrange("p (m k) -> p m k", m=2), AF.Silu, [cs], [sc])
            wm = [k.sb(st, [128, 6 * D], F32, "wm") for _ in range(2)]
            pm = k.ps(st, [128, 48, 2], F32, "pmod")
            for kc in range(8):
                w = wm[kc % 2]
                k.dma("sp", w[:, 0:3072], self.pd["w_mod"][l][kc * 128:(kc + 1) * 128, 0:3072], W=[w])
                k.dma("sp", w[:, 3072:6144], self.pd["w_mod"][l][kc * 128:(kc + 1) * 128, 3072:6144], W=[w], acc=True)
                for fc in range(48):
                    first = (kc == 0 and fc == 0)
                    last = (kc == 7 and fc == 47)
                    k.op("pe", lambda: nc.tensor.matmul(pm[:, fc, :], lhsT=w[:, fc * 128:(fc + 1) * 128], rhs=sc[:, kc, :],
                                                        start=first, stop=last, skip_group_check=True),
                         [w, sc], [pm], inc=(fc == 47))
            raw = k.sb(st, [128, 2, 6, 8], F32, "modraw")
            bm = self.spc(l, "bmod")
            for m in range(2):
                k.tt("dve", raw[:, m, :, :], pm[:, :, m].rearrange("p (j k) -> p j k", j=6),
                     bm.rearrange("p (j k) -> p j k", j=6), ALU.add, [pm, self.smallp], [raw], acc=(m == 1))
            mods = self.mods
            for m in range(2):
                for j in (0, 2, 3, 5):
                    k.cp("dve", mods[:, m, j, :], raw[:, m, j, :], [raw], [mods], acc=not (m == 0 and j == 0))
                k.stt(mods[:, m, 1, :], raw[:, m, 1, :], 1.0, self.spc(l, "nmix"), ALU.add, ALU.mult, [raw, self.smallp], [mods], acc=True)
                k.stt(mods[:, m, 4, :], raw[:, m, 4, :], 1.0, self.spc(l, "nffn"), ALU.add, ALU.mult, [raw, self.smallp], [mods], acc=True)

    def norm_mod(self, st_tiles, blk, l, which):
        k, nc = self.k, self.nc
        xT, work, hT, rstd, pss = st_tiles
        n0, nb, is_ctx, t0 = blk
        m = 1 if is_ctx else 0
        jA, jS = (1, 0) if which == 0 else (4, 3)
        k.act(work[:, :, 0:nb], xT[:, :, 0:nb], AF.Square, [xT], [work])
        for kc in range(8):
            k.mm(pss[:, 0:nb], self.ones_f[:], work[:, kc, 0:nb], kc == 0, kc == 7, [self.ones_f, work], [pss])
        k.act(rstd[:, 0:nb], pss[:, 0:nb], AF.Ln, [pss, self.epsD], [rstd], bias=self.epsD[:], scale=1.0 / D)
        k.act(rstd[:, 0:nb], rstd[:, 0:nb], AF.Exp, [rstd], [rstd], scale=-0.5)
        for kc in range(8):
            k.stt(work[:, kc, 0:nb], xT[:, kc, 0:nb], self.mods[:, m, jA, kc:kc + 1], rstd[:, 0:nb], ALU.mult, ALU.mult,
                  [xT, self.mods, rstd], [work], acc=(kc > 0))
        for kc in range(8):
            k.act(hT[:, kc, 0:nb], work[:, kc, 0:nb], AF.Identity, [work, self.mods], [hT], bias=self.mods[:, m, jS, kc:kc + 1],
                  acc=(kc > 0))

    def phase_inproj(self, l):
        k, nc = self.k, self.nc
        XTv = self.XT.rearrange("k p n -> p k n")
        HTv = self.HT.rearrange("k p n -> p k n")
        QTv = self.QT.rearrange("j p n -> p j n")
        OSTv = self.OST.rearrange("g p n -> p g n")
        with ExitStack() as st:
            wA = k.sb(st, [128, 8, 3584], BF16, "wA")
            for kc in range(8):
                k.dma("sp", wA[:, kc, :], self.WIN[l][kc * 128:(kc + 1) * 128, 0:3584], W=[wA], acc=(kc > 0))
            wsf = k.sb(st, [128, 4, 128], F32, "wsf")
            k.dma("sp", wsf[:], self.pd["sgu_w"][l].rearrange("g p q -> p g q"), W=[wsf])
            WsT = k.sb(st, [128, 4, 128], BF16, "WsT")
            xT = k.sb(st, [128, 8, 512], F32, "xT")
            work = k.sb(st, [128, 8, 512], F32, "work")
            hTs = [k.sb(st, [128, 8, 512], BF16, "hT") for _ in range(2)]
            rstd = k.sb(st, [128, 512], F32, "rstd")
            pss = k.ps(st, [128, 512], F32, "pss")
            pproj = [k.ps(st, [128, 512], F32, "pproj") for _ in range(2)]
            paux = [k.ps(st, [128, 512], F32, "paux") for _ in range(1)]
            ptok = [k.ps(st, [128, 512], F32, "ptok") for _ in range(3)]
            pso = k.ps(st, [128, 4, 128], F32, "pso")
            for g_ in range(4):
                k.tr(ptok[0][:, g_ * 128:(g_ + 1) * 128], wsf[:, g_, :], self.ident_f[:], [wsf, self.ident_f], [ptok[0]], inc=(g_ == 3))
            k.cp("dve", WsT[:].rearrange("p g q -> p (g q)"), ptok[0][:], [ptok[0]], [WsT])
            sqb_l = [k.sb(st, [128, 512], BF16, "sqb") for _ in range(2)]
            r1_l = [k.sb(st, [128, 512], F32, "r1") for _ in range(2)]
            qn_l = [k.sb(st, [128, 512], BF16, "qn") for _ in range(2)]
            t1_l = [k.sb(st, [128, 512], F32, "t1") for _ in range(2)]
            t2_l = [k.sb(st, [128, 512], F32, "t2") for _ in range(2)]
            qr = [k.sb(st, [128, 512], BF16, "qr") for _ in range(2)]
            cosb = k.sb(st, [128, 512], F32, "cosb")
            sinb = k.sb(st, [128, 512], F32, "sinb")
            rstg = [k.sb(st, [128, 512], F32, "rstg") for _ in range(3)]
            rstb = [k.sb(st, [128, 512], BF16, "rstb") for _ in range(3)]
            uT = k.sb(st, [128, 4, 512], BF16, "uT")
            ost = k.sb(st, [128, 4, 512], BF16, "ost")
            va = [k.sb(st, [128, 2, 65], BF16, "va") for _ in range(2)]
            for v_ in va:
                k.memset("dve", v_[:], 1.0, [v_])
            gt_l = [k.sb(st, [128, 512], F32, "gt") for _ in range(2)]
            junk_l = [k.sb(st, [128, 512], BF16, "junk") for _ in range(2)]
            vn_l = [k.sb(st, [128, 512], BF16, "vn") for _ in range(2)]
            vnf_l = [k.sb(st, [128, 512], F32, "vnf") for _ in range(2)]
            stat_l = [k.sb(st, [128, 8], F32, "stat") for _ in range(2)]
            ones_row = self.ones_f[0:1, :]
            sgub = self.spc(l, "sgu_b", rows=1)
            lnw = self.spc(l, "sgu_lnw")
            lnb = self.spc(l, "sgu_lnb")
            gain = self.spc(l, "gain")
            ectr = [0]
            def do_norm(bi_):
                n0_, nb_, _c, _t = self.blocks[bi_]
                hT_ = hTs[bi_ % 2]
                k.dma("sp", xT[:, :, 0:nb_], XTv[:, :, n0_:n0_ + nb_], W=[xT])
                self.norm_mod((xT, work, hT_, rstd, pss), self.blocks[bi_], l, 0)
                k.dma("pool", HTv[:, :, n0_:n0_ + nb_], hT_[:, :, 0:nb_], R=[hT_])

            do_norm(0)
            tctr_l = [0]
            for bi, blk in enumerate(self.blocks):
                n0, nb, is_ctx, t0 = blk
                hT = hTs[bi % 2]
                if not is_ctx:
                    k.dma("sp", cosb[:, 0:nb], self.cd["ropecos"][:, t0:t0 + nb], W=[cosb])
                    k.dma("sp", sinb[:, 0:nb], self.cd["ropesin"][:, t0:t0 + nb], W=[sinb])
                if bi + 1 < len(self.blocks):
                    do_norm(bi + 1)
                chunks = [("q", j, j * 128) for j in range(4)] + [("k", 0, 512)] + \
                         [("r", j, 768 + j * 128) for j in range(14)] + [("u", j, 2560 + j * 128) for j in range(4)]
                for ci, (kind, j, c0) in enumerate(chunks):
                    pb = pproj[ci % 2]
                    for kc in range(8):
                        k.mm(pb[:, 0:nb], wA[:, kc, c0:c0 + 128], hT[:, kc, 0:nb], kc == 0, kc == 7, [wA, hT], [pb])
                    if kind in ("q", "k"):
                        sqb, r1, qn, t1, t2 = sqb_l[ci % 2], r1_l[ci % 2], qn_l[ci % 2], t1_l[ci % 2], t2_l[ci % 2]
                        pss2 = paux[0]
                        prot = paux[0]
                        gcol = gain[:, 0:1] if kind == "q" else gain[:, 1:2]
                        k.act(sqb[:, 0:nb], pb[:, 0:nb], AF.Square, [pb], [sqb])
                        k.mm(pss2[:, 0:nb], self.onesblk_b[:], sqb[:, 0:nb], True, True, [self.onesblk_b, sqb], [pss2])
                        k.act(r1[:, 0:nb], pss2[:, 0:nb], AF.Ln, [pss2, self.epsD], [r1], bias=self.epsD[:], scale=1.0 / HD)
                        k.act(r1[:, 0:nb], r1[:, 0:nb], AF.Exp, [r1], [r1], scale=-0.5)
                        dst = qr[ci % 2]
                        if is_ctx:
                            k.stt(dst[:, 0:nb], pb[:, 0:nb], gcol, r1[:, 0:nb], ALU.mult, ALU.mult, [pb, r1, self.smallp], [dst])
                        else:
                            k.stt(qn[:, 0:nb], pb[:, 0:nb], gcol, r1[:, 0:nb], ALU.mult, ALU.mult, [pb, r1, self.smallp], [qn])
                            k.mm(prot[:, 0:nb], self.rotm_b[:], qn[:, 0:nb], True, True, [self.rotm_b, qn], [prot])
                            k.tt("dve", t1[:, 0:nb], qn[:, 0:nb], cosb[:, 0:nb], ALU.mult, [qn, cosb], [t1])
                            k.tt("dve", t2[:, 0:nb], prot[:, 0:nb], sinb[:, 0:nb], ALU.mult, [prot, sinb], [t2])
                            k.tt("dve", dst[:, 0:nb], t1[:, 0:nb], t2[:, 0:nb], ALU.add, [t1, t2], [dst])
                        if kind == "q":
                            k.dma("pool", QTv[:, j, n0:n0 + nb], dst[:, 0:nb], R=[dst])
                        else:
                            k.dma("pool", self.KT[:, n0:n0 + nb], dst[:, 0:nb], R=[dst])
                    elif kind == "r":
                        eng = "act" if ectr[0] % 2 == 0 else "dve"
                        if j < 12:
                            sg = rstb[ectr[0] % 3]
                            k.cp(eng, sg[:, 0:nb], pb[:, 0:nb], [pb], [sg])
                            k.dma("pool", self.PRB[j * 128:(j + 1) * 128, n0:n0 + nb], sg[:, 0:nb], R=[sg])
                        else:
                            sg = rstg[ectr[0] % 3]
                            k.cp(eng, sg[:, 0:nb], pb[:, 0:nb], [pb], [sg])
                            k.dma("pool", self.PRW[j * 128:(j + 1) * 128, n0:n0 + nb], sg[:, 0:nb], R=[sg])
                        ectr[0] += 1
                    else:
                        k.act(uT[:, j, 0:nb], pb[:, 0:nb], AF.Gelu_apprx_tanh, [pb], [uT], acc=(j > 0))
                ntile = nb // 128
                vns = {}

                def tm_front(tt_, hT=hT, n0=n0, vns=vns):
                    ts_ = slice(tt_ * 128, (tt_ + 1) * 128)
                    pv = ptok[0]
                    pg = ptok[1 + tctr_l[0] % 2]
                    gt, junk, vn, vnf, stat = (x_[tctr_l[0] % 2] for x_ in (gt_l, junk_l, vn_l, vnf_l, stat_l))
                    tctr_l[0] += 1
                    for kc in range(8):
                        k.mm(pv[:, 0:128], hT[:, kc, ts_], wA[:, kc, 640:768], kc == 0, kc == 7, [wA, hT], [pv])
                    for kc in range(8):
                        k.mm(pg[:, :], hT[:, kc, ts_], wA[:, kc, 3072:3584], kc == 0, kc == 7, [wA, hT], [pg])
                    vt = va[tt_ % 2]
                    k.cp("dve", vt[:, :, 0:64], pv[:, 0:128].rearrange("p (h d) -> p h d", h=2), [pv], [vt])
                    k.dma("pool", self.VA[n0 + tt_ * 128:n0 + (tt_ + 1) * 128, :], vt[:].rearrange("p h d -> p (h d)"), R=[vt])
                    k.act(gt[:], pg[:], AF.Gelu_apprx_tanh, [pg], [gt, stat], accum=stat[:, 0:1])
                    k.ts("dve", stat[:, 1:2], stat[:, 0:1], -1.0 / 512, ALU.mult, [stat], [stat])
                    k.act(junk[:], gt[:], AF.Square, [gt, stat], [junk, stat], bias=stat[:, 1:2], accum=stat[:, 2:3])
                    k.act(stat[:, 3:4], stat[:, 2:3], AF.Ln, [stat, self.epsD], [stat], bias=self.epsD[:], scale=1.0 / 512)
                    k.act(stat[:, 4:5], stat[:, 3:4], AF.Exp, [stat], [stat], scale=-0.5)
                    k.ts("dve", vnf[:], gt[:], stat[:, 1:2], ALU.add, [gt, stat], [vnf], s2=stat[:, 4:5], op1=ALU.mult)
                    k.tt("dve", vnf[:], vnf[:], lnw, ALU.mult, [vnf, self.smallp], [vnf])
                    k.tt("dve", vn[:], vnf[:], lnb, ALU.add, [vnf, self.smallp], [vn])
                    vns[tt_] = vn

                def tm_back(tt_, vns=vns):
                    ts_ = slice(tt_ * 128, (tt_ + 1) * 128)
                    vn = vns[tt_]
                    for g_ in range(4):
                        gs = slice(g_ * 128, (g_ + 1) * 128)
                        k.mm(pso[:, g_, :], vn[:, gs], WsT[:, g_, :], True, False, [vn, WsT], [pso], inc=False)
                        k.mm(pso[:, g_, :], ones_row, sgub[:, gs], False, True, [self.ones_f, self.smallp], [pso], inc=(g_ == 3))
                    k.tt("dve", ost[:, :, ts_], uT[:, :, ts_], pso[:], ALU.mult, [uT, pso], [ost], acc=(tt_ > 0))

                for tt_ in range(ntile):
                    tm_front(tt_)
                    if tt_ >= 1:
                        tm_back(tt_ - 1)
                tm_back(ntile - 1)
                k.dma("pool", OSTv[:, :, n0:n0 + nb], ost[:, :, 0:nb], R=[ost])


    def phase_attn(self, l, last):
        k, nc = self.k, self.nc
        NTOK = self.NTOK
        NT = NTOK // 128
        nlat = self.T // 128
        OATv = self.OAT.rearrange("c p n -> p c n")
        with ExitStack() as st:
            Qs = k.sb(st, [128, 4, NTOK], BF16, "Qs")
            Ks = k.sb(st, [128, NTOK], BF16, "Ks")
            Vs = k.sb(st, [128, NT, 130], BF16, "Vs")
            QTv = self.QT.rearrange("j p n -> p j n")
            for j in range(4):
                k.dma("sp", Qs[:, j, :], QTv[:, j, :], W=[Qs], acc=(j > 0))
            k.dma("sp", Ks[:], self.KT, W=[Ks])
            k.dma("sp", Vs[:], self.VA.rearrange("(t p) f -> p t f", p=128), W=[Vs])
            mf = k.sb(st, [128, 2, 512], F32, "mf")
            k.dma("sp", mf[:, 0, :], self.cd["maskprev"], W=[mf])
            k.dma("sp", mf[:, 1, :], self.cd["masknext"], W=[mf], acc=True)
            mb = k.sb(st, [128, 2, 512], BF16, "mb")
            k.cp("dve", mb[:], mf[:], [mf], [mb])
            esink = k.sb(st, [128, 8], F32, "esink")
            k.act(esink[:], self.spc(l, "sink"), AF.Exp, [self.smallp], [esink])
            pS = [k.ps(st, [128, 512], F32, "pS") for _ in range(3)]
            pO = [[k.ps(st, [128, 4, 128], F32, "pO") for _ in range(2)] for _ in range(2)]
            pT = k.ps(st, [128, 4, 128], BF16, "pT")
            PT = [k.sb(st, [128, 512], BF16, "PT") for _ in range(3)]
            den = k.sb(st, [128, 8], F32, "den")
            oa = [k.sb(st, [128, 8, 64], BF16, "oa") for _ in range(2)]
            oat = [k.sb(st, [128, 4, 128], BF16, "oat") for _ in range(2)]
            sctr_l = [0]
            pending = None
            qtiles = list(range(2, NT)) if last else list(range(NT))
            for qn_, qi in enumerate(qtiles):
                qs = slice(qi * 128, (qi + 1) * 128)
                if qi < 2:
                    kbs = [(0, None), (1, None)]
                else:
                    b = qi - 2
                    kbs = []
                    if b > 0:
                        kbs.append((qi - 1, 0))
                    kbs.append((qi, None))
                    if b < nlat - 1:
                        kbs.append((qi + 1, 1))
                    kbs += [(0, None), (1, None)]
                oat_ = oat[qn_ % 2]
                oa_ = oa[qn_ % 2]
                items = [(g_, ki, kb, msk) for g_ in range(2) for ki, (kb, msk) in enumerate(kbs)]
                slots = []
                nkb = len(kbs)

                def emit_S(it, qs=qs, slots=slots):
                    g_, ki, kb, msk = it
                    gp = slice(g_ * 64, (g_ + 1) * 64)
                    ps_ = pS[sctr_l[0] % 3]
                    pt_ = PT[sctr_l[0] % 3]
                    sctr_l[0] += 1
                    k.mm(ps_[:], Ks[gp, kb * 128:(kb + 1) * 128], Qs[gp, :, qs], True, msk is None, [Ks, Qs], [ps_])
                    if msk is not None:
                        k.mm(ps_[:], self.ident_b[:], mb[:, msk, :], False, True, [self.ident_b, mb], [ps_])
                    k.act(pt_[:], ps_[:], AF.Exp, [ps_], [pt_], scale=0.125)
                    slots.append(pt_)

                def emit_PV(idx, items=items, slots=slots, nkb=nkb, qn_=qn_, oa_=oa_):
                    g_, ki, kb, msk = items[idx]
                    pt_ = slots[idx]
                    po = pO[qn_ % 2][g_]
                    for jh in range(4):
                        first = (ki == 0 and jh == 0)
                        lastmm = (ki == nkb - 1 and jh == 3)
                        k.op("pe", lambda: nc.tensor.matmul(po[:, jh, 0:65], lhsT=pt_[:, jh * 128:(jh + 1) * 128],
                                                            rhs=Vs[:, kb, g_ * 65:(g_ + 1) * 65], start=first, stop=lastmm,
                                                            skip_group_check=True),
                             [pt_, Vs], [po], inc=(jh == 3))
                    if ki == nkb - 1:
                        hs = slice(g_ * 4, (g_ + 1) * 4)
                        k.tt("dve", den[:, hs], po[:, :, 64], esink[:, hs], ALU.add, [po, esink], [den], acc=(g_ == 1))
                        k.recip(den[:, hs], den[:, hs], [den], [den])
                        k.tt("dve", oa_[:, hs, :], po[:, :, 0:64], den[:, hs].unsqueeze(2).broadcast_to([128, 4, 64]), ALU.mult,
                             [po, den], [oa_], acc=(g_ == 1))

                for idx in range(len(items)):
                    emit_S(items[idx])
                    if idx >= 1:
                        emit_PV(idx - 1)
                emit_PV(len(items) - 1)
                def finish_tile(oa__=oa_, oat__=oat_, qs_=qs):
                    for c_ in range(4):
                        k.tr(pT[:, c_, :], oa__[:, 2 * c_:2 * c_ + 2, :].rearrange("p h d -> p (h d)"), self.ident_b[:], [oa__, self.ident_b], [pT],
                             inc=(c_ == 3))
                    k.cp("act", oat__[:], pT[:], [pT], [oat__])
                    k.dma("pool", OATv[:, :, qs_], oat__[:], R=[oat__])
                if pending is not None:
                    pending()
                pending = finish_tile
            if pending is not None:
                pending()

    def phase_rwkv_prep(self, l):
        k, nc = self.k, self.nc
        NTOK = self.NTOK
        PB = 128
        CC = 0.6065306597126334
        PRWv = self.PRB.rearrange("(w c) n -> c w n", c=64)
        PWAv = self.PRW[1536:1664].rearrange("(w c) n -> c w n", c=64)
        PGv = self.PRW[1664:1792]
        HTv = self.HT.rearrange("k p n -> p k n")
        with ExitStack() as st:
            T_ = lambda nm, dt=F32, shp=(128, 8, PB): k.sb(st, list(shp), dt, nm)
            dg = k.sb(st, [64, 72, 2, 64], BF16, "dg")
            cw = self.spc(l, "conv_rkv", rows=64)
            idb = self.ident_f[0:64, 0:64].unsqueeze(1).broadcast_to([64, 72, 64])
            cwb = cw.unsqueeze(2).broadcast_to([64, 72, 64])
            for dd in range(2):
                k.tt("dve", dg[:, :, dd, :], idb, cwb, ALU.mult, [self.ident_f, self.smallp], [dg], acc=(dd == 1))
            dg2 = k.sb(st, [64, 6, 64], F32, "dg2")
            k.tt("dve", dg2[:], self.ident_f[0:64, 0:64].unsqueeze(1).broadcast_to([64, 6, 64]),
                 self.spc(l, "conv_wa", rows=64).unsqueeze(2).broadcast_to([64, 6, 64]), ALU.mult, [self.ident_f, self.smallp], [dg2])
            dg3 = k.sb(st, [128, 3, 128], F32, "dg3")
            k.tt("dve", dg3[:], self.ident_f[:].unsqueeze(1).broadcast_to([128, 3, 128]),
                 self.spc(l, "conv_g").unsqueeze(2).broadcast_to([128, 3, 128]), ALU.mult, [self.ident_f, self.smallp], [dg3])
            stg = k.sb(st, [128, 1024], F32, "lstg")
            wup = k.sb(st, [64, 8, 2, 64], BF16, "wup")
            aup = k.sb(st, [64, 8, 2, 64], BF16, "aup")
            for (dst, nm) in ((wup, "rwkv_w_up"), (aup, "rwkv_a_up")):
                for dd in range(2):
                    k.dma("sp", stg[0:64, dd * 512:(dd + 1) * 512], self.pd[nm][l, dd], W=[stg], acc=(dd == 1))
                k.cp("dve", dst[:].rearrange("r h d c -> r d h c"), stg[0:64, :].rearrange("r (d h c) -> r d h c", d=2, h=8), [stg], [dst])
            gup = k.sb(st, [128, 8, 64], BF16, "gup")
            k.dma("sp", stg[:, 0:512], self.pd["rwkv_g_up"][l], W=[stg])
            k.cp("dve", gup[:].rearrange("r h c -> r (h c)"), stg[:, 0:512], [stg], [gup])
            if l > 0:
                dwn = k.sb(st, [128, 8, 32], BF16, "dwn")
                k.dma("sp", stg[:, 0:256].rearrange("p (k r) -> p k r", k=8), self.pd["rwkv_vres_down"][l - 1].rearrange("(k p) r -> p k r", p=128), W=[stg])
                k.cp("dve", dwn[:].rearrange("p k r -> p (k r)"), stg[:, 0:256], [stg], [dwn])
                vup = k.sb(st, [32, 8, 2, 64], BF16, "vup")
                k.dma("sp", stg[0:32, 0:512], self.pd["rwkv_vres_up"][l - 1], W=[stg])
                for dd in range(2):
                    k.cp("dve", vup[:, :, dd, :], stg[0:32, 0:512].rearrange("r (h c) -> r h c", h=8), [stg], [vup], acc=(dd == 1))
            rsm = k.sb(st, [128, 8, PB], F32, "rsm")
            k.dma("sp", rsm[:], self.cd["scanreset"], W=[rsm])
            eps12 = k.sb(st, [128, 1], F32, "eps12")
            k.memset("dve", eps12[:], 1e-12, [eps12])
            hal = [k.sb(st, [64, 8, PB + 2], BF16, "hal") for _ in range(3)]
            hwa = k.sb(st, [64, 2, PB + 2], F32, "hwa")
            hg = k.sb(st, [128, PB + 2], F32, "hg")
            r_, kx, v_ = T_("r"), T_("kx"), T_("v")
            kA, kB, sig, asg = T_("kA"), T_("kB"), T_("sig"), T_("asg")
            P_, Q_, TP = T_("P"), T_("Q"), T_("TP")
            E = [T_("E") for _ in range(2)]
            kd, bb, tm, tm2 = T_("kd"), T_("bb"), T_("tm"), T_("tm2")
            wc = k.sb(st, [128, 2, 8], F32, "wc")
            xwt = k.sb(st, [64, PB], BF16, "xwt")
            xab = k.sb(st, [64, PB], BF16, "xab")
            sgx = k.sb(st, [128, PB], BF16, "sgx")
            hdb = k.sb(st, [32, PB], BF16, "hdb")
            hTb = k.sb(st, [128, 8, PB], BF16, "hTb")
            ob = [k.sb(st, [128, 8, PB], BF16, "ob") for _ in range(6)]
            vb = k.sb(st, [64, 8, PB], BF16, "vb")
            tokb = k.sb(st, [128, 8, 2, 128], BF16, "tokb")
            tokv = k.sb(st, [128, 8, 64], BF16, "tokv")
            gT = k.sb(st, [64, 8, PB], F32, "gT")
            bv = k.sb(st, [64, 8, PB], F32, "bv")
            pu = [k.ps(st, [128, 4, PB], F32, "pu") for _ in range(4)]
            ptr = [k.ps(st, [128, 8, 128], BF16, "ptr") for _ in range(2)]
            pm = k.ps(st, [128, 512], F32, "pm")
            ptv = k.ps(st, [128, 8, 64], BF16, "ptv")
            puc = [0]

            def nextpu():
                puc[0] += 1
                return pu[puc[0] % 4]

            bcast = lambda name: self.spc(l, name).unsqueeze(2).broadcast_to([128, 8, PB])
            nblk = NTOK // PB
            ectr = 0
            for bi in range(nblk):
                n0 = bi * PB
                lz = (n0 == 0 or n0 == LCTX)
                rz = (n0 + PB == LCTX or n0 + PB == NTOK)
                lo = 1 if lz else 0
                hi = PB + 1 if rz else PB + 2

                def load_halo(tile, src, a3):
                    first = True
                    if lz:
                        ap = tile[:, :, 0:1] if a3 else tile[:, 0:1]
                        k.op("pool", lambda: nc.gpsimd.memset(ap, 0.0), (), [tile])
                        first = False
                    if rz:
                        ap2 = tile[:, :, PB + 1:PB + 2] if a3 else tile[:, PB + 1:PB + 2]
                        k.op("pool", lambda: nc.gpsimd.memset(ap2, 0.0), (), [tile], acc=not first)
                        first = False
                    dst = tile[:, :, lo:hi] if a3 else tile[:, lo:hi]
                    k.dma("sp", dst, src, W=[tile], acc=not first)

                outs3 = (r_, kx, v_)
                for wch in range(3):
                    h_ = hal[wch]
                    load_halo(h_, PRWv[:, wch * 8:(wch + 1) * 8, n0 - 1 + lo:n0 - 1 + hi], True)
                    for half in range(2):
                        pb = nextpu()
                        for hh in range(4):
                            h = half * 4 + hh
                            w = wch * 8 + h
                            for tap in range(3):
                                k.mm(pb[:, hh, :], dg[:, tap * 24 + w, :, :].rearrange("p d c -> p (d c)"), h_[:, h, tap:tap + PB],
                                     tap == 0, tap == 2, [dg, h_], [pb], inc=(tap == 2 and hh == 3))
                        eng = "act" if ectr % 2 == 0 else "dve"
                        ectr += 1
                        k.cp(eng, outs3[wch][:, half * 4:(half + 1) * 4, :], pb[:], [pb], [outs3[wch]], acc=(half == 1))
                if getattr(self, 'rp_cut', 99) < 1:
                    return
                load_halo(hwa, PWAv[:, :, n0 - 1 + lo:n0 - 1 + hi], True)
                load_halo(hg, PGv[:, n0 - 1 + lo:n0 - 1 + hi], False)
                if getattr(self, 'rp_cut', 99) < 0.3:
                    return
                for wch in range(2):
                    for tap in range(3):
                        k.mm(pm[0:64, wch * PB:(wch + 1) * PB], dg2[:, tap * 2 + wch, :], hwa[:, wch, tap:tap + PB], tap == 0, tap == 2,
                             [dg2, hwa], [pm], inc=(tap == 2 and wch == 1))
                k.act(xwt[:], pm[0:64, 0:PB], AF.Tanh, [pm], [xwt])
                k.cp("act", xab[:], pm[0:64, PB:2 * PB], [pm], [xab])
                if getattr(self, 'rp_cut', 99) < 0.6:
                    return
                pbg = nextpu()
                for tap in range(3):
                    k.mm(pbg[:, 0, :], dg3[:, tap, :], hg[:, tap:tap + PB], tap == 0, tap == 2, [dg3, hg], [pbg])
                k.act(sgx[:], pbg[:, 0, :], AF.Sigmoid, [pbg], [sgx])
                if getattr(self, 'rp_cut', 99) < 2:
                    return
                for (dst, wt, src, bname) in ((sig, wup, xwt, "w0"), (asg, aup, xab, "a0")):
                    bcol = self.spc(l, bname)
                    for half in range(2):
                        pb = nextpu()
                        for hh in range(4):
                            h = half * 4 + hh
                            k.mm(pb[:, hh, :], wt[:, h, :, :].rearrange("r d c -> r (d c)"), src[:], True, True, [wt, src], [pb], inc=(hh == 3))
                        for hh in range(4):
                            h = half * 4 + hh
                            k.act(dst[:, h, :], pb[:, hh, :], AF.Sigmoid, [pb, self.smallp], [dst], bias=bcol[:, h:h + 1], acc=(h > 0))
                if getattr(self, 'rp_cut', 99) < 3:
                    return
                for half in range(2):
                    pb = nextpu()
                    for hh in range(4):
                        h = half * 4 + hh
                        k.mm(pb[0:64, hh, :], gup[:, h, :], sgx[:], True, True, [gup, sgx], [pb], inc=(hh == 3))
                    k.cp("act", gT[:, half * 4:(half + 1) * 4, :], pb[0:64, :, :], [pb], [gT], acc=(half == 1))
                k.dma("pool", self.RG[:, :, n0:n0 + PB], gT[:], R=[gT])
                if getattr(self, 'rp_cut', 99) < 4:
                    return
                if l > 0:
                    k.dma("sp", hTb[:], HTv[:, :, n0:n0 + PB], W=[hTb])
                    for kc in range(8):
                        k.mm(pm[0:32, 3 * PB:4 * PB], dwn[:, kc, :], hTb[:, kc, :], kc == 0, kc == 7, [dwn, hTb], [pm])
                    k.cp("act", hdb[:], pm[0:32, 3 * PB:4 * PB], [pm], [hdb])
                    vb_col = self.spc(l, "vres_b")
                    for half in range(2):
                        pb = nextpu()
                        for hh in range(4):
                            h = half * 4 + hh
                            k.mm(pb[:, hh, :], vup[:, h, :, :].rearrange("r d c -> r (d c)"), hdb[:], True, True, [vup, hdb], [pb], inc=(hh == 3))
                        for hh in range(4):
                            h = half * 4 + hh
                            k.act(tm[:, h, :], pb[:, hh, :], AF.Sigmoid, [pb, self.smallp], [tm], bias=vb_col[:, h:h + 1], acc=(h > 0))
                    for dd in range(2):
                        k.dma("sp", tm2[dd * 64:(dd + 1) * 64], self.VF[:, :, n0:n0 + PB], W=[tm2], acc=(dd == 1))
                    k.tt("dve", tm2[:], tm2[:], v_[:], ALU.subtract, [tm2, v_], [tm2])
                    k.tt("dve", tm2[:], tm2[:], tm[:], ALU.mult, [tm2, tm], [tm2])
                    k.tt("dve", v_[:], v_[:], tm2[:], ALU.add, [v_, tm2], [v_])
                else:
                    k.dma("pool", self.VF[:, :, n0:n0 + PB], v_[0:64], R=[v_])
                k.cp("act", vb[:], v_[0:64], [v_], [vb])
                if getattr(self, 'rp_cut', 99) < 5:
                    return
                k.tt("dve", tm[:], kx[:], bcast("k_k"), ALU.mult, [kx, self.smallp], [tm])
                k.act(tm2[:], tm[:], AF.Square, [tm], [tm2])
                for half in range(2):
                    pb = nextpu()
                    for hh in range(4):
                        h = half * 4 + hh
                        k.mm(pb[:, hh, :], self.onesblk_f[:], tm2[:, h, :], True, True, [self.onesblk_f, tm2], [pb], inc=(hh == 3))
                    k.act(kA[:, half * 4:(half + 1) * 4, :], pb[:], AF.Ln, [pb, eps12], [kA], bias=eps12[:], acc=(half == 1))
                k.act(kA[:], kA[:], AF.Exp, [kA], [kA], scale=-0.5)
                kk = kB
                k.tt("dve", kk[:], tm[:], kA[:], ALU.mult, [tm, kA], [kk])
                k.tt("dve", kA[:], kx[:], bcast("k_a"), ALU.mult, [kx, self.smallp], [kA])
                k.tt("dve", kx[:], kx[:], kA[:], ALU.subtract, [kx, kA], [kx])
                k.tt("dve", kd[:], kA[:], asg[:], ALU.mult, [kA, asg], [kd])
                k.tt("dve", kd[:], kd[:], kx[:], ALU.add, [kd, kx], [kd])
                k.tt("dve", bb[:], kk[:], asg[:], ALU.mult, [kk, asg], [bb])
                if getattr(self, 'rp_cut', 99) < 6:
                    return
                k.tt("dve", tm[:], r_[:], kd[:], ALU.mult, [r_, kd], [tm])
                k.tt("dve", tm[:], tm[:], bcast("r_k"), ALU.mult, [tm, self.smallp], [tm])
                lnb_b = self.spc(l, "ln_b", rows=64).unsqueeze(2).broadcast_to([64, 8, PB])
                for half in range(2):
                    pb = nextpu()
                    for hh in range(4):
                        h = half * 4 + hh
                        k.mm(pb[0:64, hh, :], self.ones_f[:, 0:64], tm[:, h, :], True, True, [self.ones_f, tm], [pb], inc=(hh == 3))
                    k.tt("dve", bv[:, half * 4:(half + 1) * 4, :], pb[0:64, :, :], v_[0:64, half * 4:(half + 1) * 4, :], ALU.mult, [pb, v_], [bv], acc=(half == 1))
                k.tt("dve", bv[:], bv[:], lnb_b, ALU.add, [bv, self.smallp], [bv])
                k.dma("pool", self.RBV[:, :, n0:n0 + PB], bv[:], R=[bv])
                if getattr(self, 'rp_cut', 99) < 7:
                    return
                k.op("dve", lambda: nc.vector.tensor_tensor_scan(out=P_[:].rearrange("p h t -> p (h t)"), data0=rsm[:].rearrange("p h t -> p (h t)"),
                                                                 data1=sig[:].rearrange("p h t -> p (h t)"), initial=0.0, op0=ALU.mult, op1=ALU.add),
                     [rsm, sig], [P_])
                k.tt("dve", Q_[:], P_[:], sig[:], ALU.subtract, [P_, sig], [Q_])
                Pc = P_[:].rearrange("p h (j t) -> p h j t", t=64)
                tot = Pc[:, :, :, 63:64]
                totb = tot.broadcast_to([128, 8, 2, 64])
                k.act(wc[:].rearrange("p j h -> p h j"), P_[:].rearrange("p h (j t) -> p h j t", t=64)[:, :, :, 63], AF.Exp, [P_], [wc], scale=-CC)
                k.tt("dve", TP[:].rearrange("p h (j t) -> p h j t", t=64), totb, Pc, ALU.subtract, [P_], [TP])
                k.tt("dve", P_[64:128], TP[64:128], sig[64:128], ALU.add, [TP, sig], [P_])
                if getattr(self, 'rp_cut', 99) < 8:
                    return
                Ee, Ei = E[0], E[1]
                k.act(Ee[0:64], Q_[0:64], AF.Exp, [Q_], [Ee], scale=-CC)
                k.act(Ee[64:128], TP[64:128], AF.Exp, [TP], [Ee], scale=-CC, acc=True)
                k.act(Ei[:], P_[:], AF.Exp, [P_], [Ei], scale=-CC)
                k.stt(ob[0][:], kk[:], -1.0, Ee[:], ALU.mult, ALU.mult, [kk, Ee], [ob[0]])
                k.tt("dve", ob[1][:], r_[:], Ei[:], ALU.mult, [r_, Ei], [ob[1]])
                k.act(Ee[:], P_[:], AF.Exp, [P_], [Ee], scale=CC)
                k.tt("dve", ob[2][:], bb[:], Ee[:], ALU.mult, [bb, Ee], [ob[2]])
                k.tt("dve", ob[3][:], kd[:], Ee[:], ALU.mult, [kd, Ee], [ob[3]])
                k.act(Ei[0:64], TP[0:64], AF.Exp, [TP], [Ei], scale=-CC)
                k.act(Ei[64:128], Q_[64:128], AF.Exp, [Q_], [Ei], scale=-CC, acc=True)
                k.tt("dve", ob[4][:], bb[:], Ei[:], ALU.mult, [bb, Ei], [ob[4]])
                k.tt("dve", ob[5][:], kd[:], Ei[:], ALU.mult, [kd, Ei], [ob[5]])
                if getattr(self, 'rp_cut', 99) < 9:
                    return
                for q_ in range(4):
                    for dd in range(2):
                        k.dma("pool", self.RA4[dd][:, :, q_, n0:n0 + PB], ob[q_][dd * 64:(dd + 1) * 64], R=[ob[q_]])
                for dd in range(2):
                    for j in range(2):
                        k.dma("pool", self.RWC[dd][n0 // 64 + j], wc[dd * 64:(dd + 1) * 64, j, :], R=[wc])
                if getattr(self, 'rp_cut', 99) < 10:
                    return
                for q_ in range(2):
                    for h in range(8):
                        k.tr(ptr[q_][:, h, :], ob[4 + q_][:, h, :], self.ident_b[:], [ob[4 + q_], self.ident_b], [ptr[q_]], inc=(h == 7))
                    k.cp("act" if q_ == 0 else "dve", tokb[:, :, q_, :], ptr[q_][:], [ptr[q_]], [tokb], acc=(q_ == 1))
                for h in range(8):
                    k.tr(ptv[:, h, :], vb[:, h, :], self.ident_b[0:64, 0:64], [vb, self.ident_b], [ptv], inc=(h == 7))
                k.cp("act", tokv[:], ptv[:], [ptv], [tokv])
                k.dma("pool", self.TOKB[n0:n0 + PB].rearrange("n h q f -> n (h q f)"), tokb[:].rearrange("p h q f -> p (h q f)"), R=[tokb])
                k.dma("pool", self.TOKV[n0:n0 + PB].rearrange("n h f -> n (h f)"), tokv[:].rearrange("p h f -> p (h f)"), R=[tokv])
                if getattr(self, 'rp_cut', 99) < 11 + bi:
                    return

    def phase_rwkv_scan(self, l):
        k, nc = self.k, self.nc
        NTOK = self.NTOK
        NCH = NTOK // CH
        ncx = LCTX // CH
        order = [list(range(NCH)), list(range(ncx - 1, -1, -1)) + list(range(NCH - 1, ncx - 1, -1))]
        with ExitStack() as st:
            rm = k.sb(st, [64, 4, 64], F32, "rm")
            k.dma("sp", rm[:], self.cd["rmask"], W=[rm])
            class _Half:
                def __init__(self, tt_):
                    self.tt_ = tt_
                    self.tok = tt_.tok

                def __getitem__(self, key):
                    if not isinstance(key, tuple):
                        key = (key,)
                    assert key[0] == slice(None)
                    return self.tt_.t[(slice(0, 64),) + tuple(key[1:])]
            B = [_Half(k.ps(st, [128, 8, 64], F32, "B%d" % i)) for i in range(8)]
            B01 = None
            D_ = []
            for d_ in range(2):
                o = {}
                o["arq"] = [k.sb(st, [64, 8, 4, 64], BF16, "arq") for _ in range(2)]
                o["bh"] = [k.sb(st, [64, 8, 64], BF16, "bh") for _ in range(2)]
                o["kh"] = [k.sb(st, [64, 8, 64], BF16, "kh") for _ in range(2)]
                o["vt"] = [k.sb(st, [64, 8, 64], BF16, "vt") for _ in range(2)]
                o["wc"] = [k.sb(st, [64, 8], F32, "wcl") for _ in range(2)]
                o["NL"] = k.sb(st, [64, 8, 2, 64], BF16, "NL")
                o["KL"] = k.sb(st, [64, 8, 2, 64], BF16, "KL")
                o["Nb"] = [k.sb(st, [64, 8, 64], BF16, "Nb") for _ in range(2)]
                o["Pb"] = [k.sb(st, [64, 8, 64], BF16, "Pb") for _ in range(2)]
                o["TTf"] = k.sb(st, [64, 8, 64], F32, "TTf")
                o["TTb"] = [k.sb(st, [64, 8, 64], BF16, "TTb") for _ in range(2)]
                o["Xb"] = k.sb(st, [64, 8, 64], BF16, "Xb")
                o["Ub"] = k.sb(st, [64, 8, 64], BF16, "Ub")
                o["Ys"] = [k.sb(st, [64, 8, 64], F32, "Ys") for _ in range(2)]
                o["STf"] = k.sb(st, [64, 8, 64], F32, "STf")
                o["STb"] = k.sb(st, [64, 8, 64], BF16, "STb")
                k.memset("dve", o["STf"][:], 0.0, [o["STf"]])
                k.memset("dve", o["STb"][:], 0.0, [o["STb"]])
                D_.append(o)
            idb64 = self.ident_f[0:64, 0:64].unsqueeze(1).broadcast_to([64, 8, 64])

            def load(d_, i):
                o = D_[d_]
                ck = order[d_][i]
                n0 = ck * CH
                s = i % 2
                k.dma("sp", o["arq"][s][:].rearrange("c h q t -> c (h q) t"),
                      self.RA4[d_][:, :, :, n0:n0 + CH].rearrange("c h q t -> c (h q) t"), W=[o["arq"][s]])
                k.dma("sp", o["bh"][s][:], self.TOKB[n0:n0 + CH, :, 0, d_ * 64:(d_ + 1) * 64], W=[o["bh"][s]])
                k.dma("sp", o["kh"][s][:], self.TOKB[n0:n0 + CH, :, 1, d_ * 64:(d_ + 1) * 64], W=[o["kh"][s]])
                k.dma("sp", o["vt"][s][:], self.TOKV[n0:n0 + CH], W=[o["vt"][s]])
                k.dma("sp", o["wc"][s][:], self.RWC[d_][ck], W=[o["wc"][s]])

            for d_ in range(2):
                load(d_, 0)
            for i in range(NCH):
                if i + 1 < NCH:
                    for d_ in range(2):
                        load(d_, i + 1)
                s = i % 2
                for d_ in range(2):
                    o = D_[d_]
                    arq = o["arq"][s]
                    mp = rm[:, 0:2, :] if d_ == 0 else rm[:, 2:4, :]
                    mA = rm[:, 2, :] if d_ == 0 else rm[:, 0, :]
                    mpb = mp.unsqueeze(1).broadcast_to([64, 8, 2, 64])
                    mAb = mA.unsqueeze(1).broadcast_to([64, 8, 64])
                    for h in range(8):
                        bn = B[0] if h < 4 else B[1]
                        k.mm(bn[:, 2 * (h % 4):2 * (h % 4) + 2, :], arq[:, h, 2, :], arq[:, h, 0:2, :], True, True, [arq], [bn], inc=(h % 4 == 3))
                    for h in range(8):
                        kn = B[2] if h < 4 else B[3]
                        k.mm(kn[:, 2 * (h % 4):2 * (h % 4) + 2, :], arq[:, h, 3, :], arq[:, h, 0:2, :], True, True, [arq], [kn], inc=(h % 4 == 3))
                    for h in range(8):
                        k.mm(B[4][:, h, :], arq[:, h, 0, :], arq[:, h, 2, :], True, True, [arq], [B[4]], inc=(h == 7))
                    for hf in range(2):
                        hs = slice(hf * 4, (hf + 1) * 4)
                        k.tt("dve", o["NL"][:, hs, :, :], B[hf][:].rearrange("s (h q) t -> s h q t", q=2), mpb[:, hs], ALU.mult, [B[hf], rm], [o["NL"]], acc=(hf == 1))
                        k.tt("dve", o["KL"][:, hs, :, :], B[2 + hf][:].rearrange("s (h q) t -> s h q t", q=2), mpb[:, hs], ALU.mult, [B[2 + hf], rm], [o["KL"]], acc=(hf == 1))
                    k.tt("dve", o["Pb"][0][:], B[4][:], mAb, ALU.mult, [B[4], rm], [o["Pb"][0]])
                    k.tt("dve", o["TTf"][:], o["NL"][:, :, 0, :], idb64, ALU.add, [o["NL"], self.ident_f], [o["TTf"]])
                    k.tt("dve", o["TTb"][0][:], o["NL"][:, :, 0, :], idb64, ALU.add, [o["NL"], self.ident_f], [o["TTb"][0]])
                for j in range(1, 6):
                    cur, prv = j % 2, (j - 1) % 2
                    for d_ in range(2):
                        o = D_[d_]
                        pN, pP, pT = B[3 * d_], B[3 * d_ + 1], B[3 * d_ + 2]
                        Nprev = (lambda h: o["NL"][:, h, 0, :]) if j == 1 else (lambda h: o["Nb"][prv][:, h, :])
                        Ntok = o["NL"] if j == 1 else o["Nb"][prv]
                        Pprev = o["Pb"][prv]
                        if j < 5:
                            for h in range(8):
                                k.mm(pN[:, h, :], Pprev[:, h, :], Nprev(h), True, True, [Pprev, Ntok], [pN], inc=(h == 7))
                        for h in range(8):
                            k.mm(pP[:, h, :], Nprev(h), Pprev[:, h, :], True, True, [Pprev, Ntok], [pP], inc=(h == 7))
                    for d_ in range(2):
                        o = D_[d_]
                        pN, pP, pT = B[3 * d_], B[3 * d_ + 1], B[3 * d_ + 2]
                        if j < 5:
                            k.cp("act", o["Nb"][cur][:], pN[:], [pN], [o["Nb"][cur]])
                        k.cp("act", o["Pb"][cur][:], pP[:], [pP], [o["Pb"][cur]])
                    for d_ in range(2):
                        o = D_[d_]
                        pT = B[3 * d_ + 2]
                        for h in range(8):
                            k.mm(pT[:, h, :], o["Pb"][cur][:, h, :], o["TTb"][prv][:, h, :], True, True, [o["Pb"][cur], o["TTb"][prv]], [pT], inc=(h == 7))
                    for d_ in range(2):
                        o = D_[d_]
                        pT = B[3 * d_ + 2]
                        k.tt("dve", o["TTf"][:], o["TTf"][:], pT[:], ALU.add, [o["TTf"], pT], [o["TTf"]])
                        k.cp("act", o["TTb"][cur][:], o["TTf"][:], [o["TTf"]], [o["TTb"][cur]])
                TTfin = 5 % 2
                for d_ in range(2):
                    o = D_[d_]
                    pX = B[4 * d_]
                    arq = o["arq"][s]
                    for h in range(8):
                        k.mm(pX[:, h, :], arq[:, h, 0, :], o["STb"][:, h, :], True, False, [arq, o["STb"]], [pX], inc=False)
                        k.mm(pX[:, h, :], o["KL"][:, h, 0, :], o["vt"][s][:, h, :], False, True, [o["KL"], o["vt"][s]], [pX], inc=(h == 7))
                for d_ in range(2):
                    o = D_[d_]
                    k.cp("act", o["Xb"][:], B[4 * d_][:], [B[4 * d_]], [o["Xb"]])
                for d_ in range(2):
                    o = D_[d_]
                    pU = B[4 * d_ + 1]
                    for h in range(8):
                        k.mm(pU[:, h, :], o["TTb"][TTfin][:, h, :], o["Xb"][:, h, :], True, True, [o["TTb"][TTfin], o["Xb"]], [pU], inc=(h == 7))
                for d_ in range(2):
                    o = D_[d_]
                    k.cp("dve", o["Ub"][:], B[4 * d_ + 1][:], [B[4 * d_ + 1]], [o["Ub"]])
                for d_ in range(2):
                    o = D_[d_]
                    pY, pS = B[4 * d_ + 2], B[4 * d_ + 3]
                    arq = o["arq"][s]
                    for h in range(8):
                        k.mm(pS[:, h, :], o["bh"][s][:, h, :], o["Ub"][:, h, :], True, False, [o["bh"][s], o["Ub"]], [pS], inc=False)
                        k.mm(pS[:, h, :], o["kh"][s][:, h, :], o["vt"][s][:, h, :], False, True, [o["kh"][s], o["vt"][s]], [pS], inc=(h == 7))
                    for h in range(8):
                        k.mm(pY[:, h, :], o["STb"][:, h, :], arq[:, h, 1, :], True, False, [o["STb"], arq], [pY], inc=False)
                        k.mm(pY[:, h, :], o["Ub"][:, h, :], o["NL"][:, h, 1, :], False, False, [o["Ub"], o["NL"]], [pY], inc=False)
                        k.mm(pY[:, h, :], o["vt"][s][:, h, :], o["KL"][:, h, 1, :], False, True, [o["vt"][s], o["KL"]], [pY], inc=(h == 7))
                for d_ in range(2):
                    o = D_[d_]
                    pY, pS = B[4 * d_ + 2], B[4 * d_ + 3]
                    ck = order[d_][i]
                    k.tt("dve", o["STf"][:], o["STf"][:], o["wc"][s][:].unsqueeze(2).broadcast_to([64, 8, 64]), ALU.mult, [o["STf"], o["wc"][s]], [o["STf"]])
                    k.tt("dve", o["STf"][:], o["STf"][:], pS[:], ALU.add, [o["STf"], pS], [o["STf"]])
                    k.cp("act", o["STb"][:], o["STf"][:], [o["STf"]], [o["STb"]])
                    ys = o["Ys"][s]
                    k.cp("act", ys[:], pY[:], [pY], [ys])
                    k.dma("pool", self.YD[d_][:, :, ck * CH:(ck + 1) * CH], ys[:], R=[ys])

    def phase_rwkv_post(self, l):
        k, nc = self.k, self.nc
        NTOK = self.NTOK
        PB = 256
        with ExitStack() as st:
            T_ = lambda nm, dt=F32: k.sb(st, [64, 8, PB], dt, nm)
            sets = []
            for _ in range(2):
                sets.append(dict(y0=T_("y0"), y1=T_("y1"), yc=T_("yc"), sq=T_("sq"), rs=T_("rs"), tmp=T_("tmp"), bvt=T_("bvt"), gt=T_("gt"),
                                 ob=T_("orw", BF16)))
            pu_full = [k.ps(st, [128, 4, PB], F32, "ppu") for _ in range(4)]

            class _H2:
                def __init__(self, tt_):
                    self.tt_ = tt_
                    self.tok = tt_.tok

                def __getitem__(self, key):
                    if not isinstance(key, tuple):
                        key = (key,)
                    return self.tt_.t[(slice(0, 64),) + tuple(key[1:])]
            pu = [_H2(x_) for x_ in pu_full]
            ones64 = self.ones_f[0:64, 0:64]
            lnw = self.spc(l, "ln_w", rows=64)

            def s0(bi, c):
                n0 = bi * PB
                k.dma("sp", c["y0"][:], self.YD[0][:, :, n0:n0 + PB], W=[c["y0"]])
                k.dma("sp", c["y1"][:], self.YD[1][:, :, n0:n0 + PB], W=[c["y1"]])
                k.dma("sp", c["bvt"][:], self.RBV[:, :, n0:n0 + PB], W=[c["bvt"]])
                k.dma("sp", c["gt"][:], self.RG[:, :, n0:n0 + PB], W=[c["gt"]])
                k.tt("dve", c["y0"][:], c["y0"][:], c["y1"][:], ALU.add, [c["y0"], c["y1"]], [c["y0"]])

            def s1(bi, c):
                for hf in range(2):
                    hs = slice(hf * 4, (hf + 1) * 4)
                    pb = pu[hf]
                    for hh in range(4):
                        k.mm(pb[:, hh, :], ones64, c["y0"][:, hf * 4 + hh, :], True, True, [self.ones_f, c["y0"]], [pb], inc=(hh == 3))
                    k.stt(c["yc"][:, hs, :], pb[:], -1.0 / 64, c["y0"][:, hs, :], ALU.mult, ALU.add, [pb, c["y0"]], [c["yc"]], acc=(hf == 1))
                k.act(c["sq"][:], c["yc"][:], AF.Square, [c["yc"]], [c["sq"]])

            def s2(bi, c):
                for hf in range(2):
                    hs = slice(hf * 4, (hf + 1) * 4)
                    pb = pu[2 + hf]
                    for hh in range(4):
                        k.mm(pb[:, hh, :], ones64, c["sq"][:, hf * 4 + hh, :], True, True, [self.ones_f, c["sq"]], [pb], inc=(hh == 3))
                    k.act(c["rs"][:, hs, :], pb[:], AF.Ln, [pb, self.epsG], [c["rs"]], bias=self.epsG[0:64], scale=1.0 / 64, acc=(hf == 1))
                k.act(c["rs"][:], c["rs"][:], AF.Exp, [c["rs"]], [c["rs"]], scale=-0.5)

            def s3(bi, c):
                for h in range(8):
                    k.stt(c["tmp"][:, h, :], c["yc"][:, h, :], lnw[:, h:h + 1], c["rs"][:, h, :], ALU.mult, ALU.mult,
                          [c["yc"], c["rs"], self.smallp], [c["tmp"]], acc=(h > 0))

            def s4(bi, c):
                n0 = bi * PB
                k.tt("dve", c["tmp"][:], c["tmp"][:], c["bvt"][:], ALU.add, [c["tmp"], c["bvt"]], [c["tmp"]])
                k.tt("dve", c["ob"][:], c["tmp"][:], c["gt"][:], ALU.mult, [c["tmp"], c["gt"]], [c["ob"]])
                k.dma("pool", self.ORT[:, :, n0:n0 + PB], c["ob"][:], R=[c["ob"]])

            nblk = NTOK // PB
            bi = 0
            while bi < nblk:
                grp = [bi] if bi + 1 >= nblk else [bi, bi + 1]
                for stage in (s0, s1, s2, s3, s4):
                    for j, b_ in enumerate(grp):
                        stage(b_, sets[j])
                bi += len(grp)

    def phase_merge(self, l, last):
        k, nc = self.k, self.nc
        XTv = self.XT.rearrange("k p n -> p k n")
        HTv = self.HT.rearrange("k p n -> p k n")
        OATv = self.OAT.rearrange("c p n -> p c n")
        OSTv = self.OST.rearrange("g p n -> p g n")
        with ExitStack() as st:
            wG = k.sb(st, [128, 8, 3072], BF16, "wG")
            for kc in range(8):
                k.dma("sp", wG[:, kc, :], self.WIN[l][kc * 128:(kc + 1) * 128, 3584:6656], W=[wG], acc=(kc > 0))
            wOA = k.sb(st, [128, 4, D], BF16, "wOA")
            wOS = k.sb(st, [128, 4, D], BF16, "wOS")
            wOR = k.sb(st, [64, 8, D], BF16, "wOR")
            wOU = k.sb(st, [128, 8, D], BF16, "wOU")
            k.dma("sp", wOA[:], self.WOA[l].rearrange("(k p) f -> p k f", p=128), W=[wOA])
            k.dma("sp", wOS[:], self.WOS[l].rearrange("(k p) f -> p k f", p=128), W=[wOS])
            k.dma("sp", wOR[:], self.WOR[l].rearrange("(h c) f -> c h f", c=64), W=[wOR])
            k.dma("sp", wOU[:], self.WOUT[l].rearrange("(k p) f -> p k f", p=128), W=[wOU])
            NB = 512
            xT = k.sb(st, [128, 8, NB], F32, "mxT")
            hT = [k.sb(st, [128, 8, NB], BF16, "mhT") for _ in range(2)]
            oaT = [k.sb(st, [128, 4, NB], BF16, "moa") for _ in range(2)]
            osT = [k.sb(st, [128, 4, NB], BF16, "mos") for _ in range(2)]
            orT = [k.sb(st, [64, 8, NB], BF16, "mor") for _ in range(2)]
            mT = k.sb(st, [128, 8, NB], BF16, "mmT")
            gs = [k.sb(st, [128, NB], F32, "mgs") for _ in range(3)]
            acc_ = k.sb(st, [128, NB], F32, "macc")
            tmp = k.sb(st, [128, NB], F32, "mtmp")
            pg = [k.ps(st, [128, NB], F32, "mpg") for _ in range(3)]
            pp = [k.ps(st, [128, NB], F32, "mpp") for _ in range(3)]
            px = [k.ps(st, [128, NB], F32, "mpx") for _ in range(2)]
            blocks = [b for b in self.blocks if not (last and b[2])]
            for bi, blk in enumerate(blocks):
                n0, nb, is_ctx, t0 = blk
                m = 1 if is_ctx else 0
                s = bi % 2
                k.dma("sp", hT[s][:, :, 0:nb], HTv[:, :, n0:n0 + nb], W=[hT[s]])
                k.dma("sp", oaT[s][:, :, 0:nb], OATv[:, :, n0:n0 + nb], W=[oaT[s]])
                k.dma("sp", osT[s][:, :, 0:nb], OSTv[:, :, n0:n0 + nb], W=[osT[s]])
                k.dma("sp", orT[s][:, :, 0:nb], self.ORT[:, :, n0:n0 + nb], W=[orT[s]])
                k.dma("sp", xT[:, :, 0:nb], XTv[:, :, n0:n0 + nb], W=[xT])
                for fc in range(8):
                    fs = slice(fc * 128, (fc + 1) * 128)
                    for b in range(3):
                        for kc in range(8):
                            k.mm(pg[b][:, 0:nb], wG[:, kc, b * 1024 + fc * 128:b * 1024 + (fc + 1) * 128], hT[s][:, kc, 0:nb], kc == 0, kc == 7, [wG, hT[s]], [pg[b]])
                        k.act(gs[b][:, 0:nb], pg[b][:, 0:nb], AF.Sigmoid, [pg[b]], [gs[b]])
                    for kc in range(4):
                        k.mm(pp[0][:, 0:nb], wOA[:, kc, fs], oaT[s][:, kc, 0:nb], kc == 0, kc == 3, [wOA, oaT[s]], [pp[0]])
                    for h in range(8):
                        k.mm(pp[1][:, 0:nb], wOR[:, h, fs], orT[s][:, h, 0:nb], h == 0, h == 7, [wOR, orT[s]], [pp[1]])
                    for kc in range(4):
                        k.mm(pp[2][:, 0:nb], wOS[:, kc, fs], osT[s][:, kc, 0:nb], kc == 0, kc == 3, [wOS, osT[s]], [pp[2]])
                    k.tt("dve", acc_[:, 0:nb], gs[0][:, 0:nb], pp[0][:, 0:nb], ALU.mult, [gs[0], pp[0]], [acc_])
                    k.tt("dve", tmp[:, 0:nb], gs[1][:, 0:nb], pp[1][:, 0:nb], ALU.mult, [gs[1], pp[1]], [tmp])
                    k.tt("dve", acc_[:, 0:nb], acc_[:, 0:nb], tmp[:, 0:nb], ALU.add, [acc_, tmp], [acc_])
                    k.tt("dve", tmp[:, 0:nb], gs[2][:, 0:nb], pp[2][:, 0:nb], ALU.mult, [gs[2], pp[2]], [tmp])
                    k.tt("dve", mT[:, fc, 0:nb], acc_[:, 0:nb], tmp[:, 0:nb], ALU.add, [acc_, tmp], [mT], acc=(fc > 0))
                for fc in range(8):
                    fs = slice(fc * 128, (fc + 1) * 128)
                    p_ = px[fc % 2]
                    for kc in range(8):
                        k.mm(p_[:, 0:nb], wOU[:, kc, fs], mT[:, kc, 0:nb], kc == 0, kc == 7, [wOU, mT], [p_])
                    k.stt(xT[:, fc, 0:nb], p_[:, 0:nb], self.mods[:, m, 2, fc:fc + 1], xT[:, fc, 0:nb], ALU.mult, ALU.add, [p_, self.mods, xT], [xT])
                k.dma("pool", XTv[:, :, n0:n0 + nb], xT[:, :, 0:nb], R=[xT])

    def phase_ffn(self, l, last):
        k, nc = self.k, self.nc
        XTv = self.XT.rearrange("k p n -> p k n")
        NB = 256
        with ExitStack() as st:
            w1 = k.sb(st, [128, 8, FFN], BF16, "w1")
            w3 = k.sb(st, [128, 8, FFN], BF16, "w3")
            w2 = k.sb(st, [128, NFC, D], BF16, "w2")
            for kc in range(8):
                k.dma("sp", w1[:, kc, :], self.W1[l][kc * 128:(kc + 1) * 128, :], W=[w1], acc=(kc > 0))
                k.dma("sp", w3[:, kc, :], self.W3[l][kc * 128:(kc + 1) * 128, :], W=[w3], acc=(kc > 0))
            for hc in range(NFC):
                k.dma("sp", w2[:, hc, :], self.W2[l][hc * 128:(hc + 1) * 128, :], W=[w2], acc=(hc > 0))
            xTs = [k.sb(st, [128, 8, NB], F32, "fxT") for _ in range(2)]
            work = k.sb(st, [128, 8, NB], F32, "fwork")
            hTs = [k.sb(st, [128, 8, NB], BF16, "fhT") for _ in range(2)]
            rstd = k.sb(st, [128, NB], F32, "frstd")
            hid = k.sb(st, [128, NFC, NB], BF16, "fhid")
            sl = [k.sb(st, [128, NB], F32, "fsl") for _ in range(2)]
            pss = k.ps(st, [128, 512], F32, "fpss")
            pa = [k.ps(st, [128, 512], F32, "fpa") for _ in range(2)]
            pb_ = [k.ps(st, [128, 512], F32, "fpb") for _ in range(2)]
            po = [k.ps(st, [128, 512], F32, "fpo") for _ in range(2)]
            if last:
                ptr = k.ps(st, [128, 512], F32, "fptr")
                xo = [k.sb(st, [128, D], F32, "fxo") for _ in range(1)]
            nblk = self.NTOK // NB
            blist = [bi for bi in range(nblk) if not (last and bi * NB < LCTX)]

            def do_norm(ii):
                bi_ = blist[ii]
                n0_ = bi_ * NB
                xT_ = xTs[ii % 2]
                k.dma("sp", xT_[:], XTv[:, :, n0_:n0_ + NB], W=[xT_])
                self.norm_mod((xT_, work, hTs[ii % 2], rstd, pss), (n0_, NB, n0_ < LCTX, 0), l, 1)

            do_norm(0)
            for ii, bi in enumerate(blist):
                n0 = bi * NB
                is_ctx = n0 < LCTX
                m = 1 if is_ctx else 0
                xT = xTs[ii % 2]
                hT = hTs[ii % 2]
                if ii + 1 < len(blist):
                    do_norm(ii + 1)
                for hc in range(NFC):
                    hs = slice(hc * 128, (hc + 1) * 128)
                    a_, b_ = pa[hc % 2], pb_[hc % 2]
                    for kc in range(8):
                        k.mm(a_[:, 0:NB], w1[:, kc, hs], hT[:, kc, :], kc == 0, kc == 7, [w1, hT], [a_])
                    for kc in range(8):
                        k.mm(b_[:, 0:NB], w3[:, kc, hs], hT[:, kc, :], kc == 0, kc == 7, [w3, hT], [b_])
                    s_ = sl[hc % 2]
                    k.act(s_[:], a_[:, 0:NB], AF.Silu, [a_], [s_])
                    k.tt("dve", hid[:, hc, :], s_[:], b_[:, 0:NB], ALU.mult, [s_, b_], [hid], acc=(hc > 0))
                for fc in range(8):
                    fs = slice(fc * 128, (fc + 1) * 128)
                    p_ = po[fc % 2]
                    for hc in range(NFC):
                        k.mm(p_[:, 0:NB], w2[:, hc, fs], hid[:, hc, :], hc == 0, hc == NFC - 1, [w2, hid], [p_])
                    k.stt(xT[:, fc, :], p_[:, 0:NB], self.mods[:, m, 5, fc:fc + 1], xT[:, fc, :], ALU.mult, ALU.add, [p_, self.mods, xT], [xT])
                if not last:
                    k.dma("pool", XTv[:, :, n0:n0 + NB], xT[:], R=[xT])
                else:
                    for tt_ in range(NB // 128):
                        xo_ = xo[0]
                        for half in range(2):
                            for j in range(4):
                                kc = half * 4 + j
                                k.tr(ptr[:, j * 128:(j + 1) * 128], xT[:, kc, tt_ * 128:(tt_ + 1) * 128], self.ident_f[:], [xT, self.ident_f], [ptr], inc=(j == 3))
                            k.cp("act" if half == 0 else "dve", xo_[:, half * 512:(half + 1) * 512], ptr[:], [ptr], [xo_], acc=(half == 1))
                        t0 = n0 - LCTX + tt_ * 128
                        k.dma("pool", self.out[t0:t0 + 128, :], xo_[:], R=[xo_])


_SMALL_NAMES = ("b_mod", "norm_mix", "norm_ffn", "q_gain", "k_gain", "attn_sink", "rwkv_conv", "rwkv_w0", "rwkv_a0",
                "rwkv_k_k", "rwkv_k_a", "rwkv_r_k", "rwkv_ln_w", "rwkv_ln_b", "rwkv_vres_b", "sgu_ln_w", "sgu_ln_b", "sgu_b")


def make_in_maps(inputs, T, ncores):
    inp = {k_: np.asarray(v) for k_, v in inputs.items()}
    hc = host_consts(T)
    smallp = np.stack([pack_small(inp, l) for l in range(DEPTH)]).astype(np.float32)
    shared = {"smallp": smallp}
    for name, arr in hc.items():
        shared["c_" + name] = np.ascontiguousarray(arr, dtype=np.float32)
    for name, shape in PARAMS:
        if name in _SMALL_NAMES:
            continue
        shared[name] = np.ascontiguousarray(inp[name], dtype=np.float32)
    maps = []
    for b in range(ncores):
        m = dict(shared)
        m["x"] = np.ascontiguousarray(inp["x"][b, :T], dtype=np.float32)
        m["ctx"] = np.ascontiguousarray(inp["ctx"][b], dtype=np.float32)
        cf = np.concatenate([inp["c"][b].reshape(8, 128).T, inp["c_ctx"].reshape(8, 128).T], axis=1)
        m["cfm"] = np.ascontiguousarray(cf, dtype=np.float32)
        maps.append(m)
    return maps


def kernel(**inputs):
    T = 4096
    B = 8
    prog = Prog(T)
    nc = prog.build()
    maps = make_in_maps(inputs, T, B)
    res = run_bass_kernel_spmd(nc, maps, core_ids=list(range(B)))
    out = np.stack([np.asarray(res.results[b]["out"], dtype=np.float32) for b in range(B)])
    return out
```
